# Optimizing a Trainium2 kernel written in Bass

```python
import math
import jax, jax.numpy as jnp
from jax import lax
import numpy as np


D_MODEL = 1024
BATCH = 32
SEQ = 2048
DEPTH = 2

CHUNK = 64
Q_BLOCK = 128
HEAD_DIM = 64
HEADS_FOX = 6
HEADS_SB = 5
HEADS_DSA = 5
IDX_HEADS = 8
IDX_DIM = 64
DSA_TOPK_MAX = 256
D_FF = 4 * D_MODEL
PLE_DIM = 256
ROPE_THETA = 10000.0
LN_EPS = 1e-5
N_BRANCH = 3
NEG = -1e30
DEEPNORM_ALPHA = (2 * DEPTH) ** 0.25
DEEPNORM_BETA = (8 * DEPTH) ** -0.25

W_FOX = HEADS_FOX * HEAD_DIM
W_SB = HEADS_SB * HEAD_DIM
W_DSA = HEADS_DSA * HEAD_DIM
IN_SPLITS = (W_FOX, W_FOX, W_FOX, HEADS_FOX,
             W_SB, W_SB, W_SB,
             W_DSA, HEAD_DIM, HEAD_DIM,
             IDX_HEADS * IDX_DIM, IDX_DIM, IDX_HEADS,
             N_BRANCH * D_MODEL)
C_IN = sum(IN_SPLITS)

kernel_name = 'hybrid_fox_stickbreak_dsa_deepnorm_block'


def split_cols(h):
    out = []
    o = 0
    for w in IN_SPLITS:
        out.append(h[..., o:o + w])
        o += w
    return out


def layer_norm(x, g, b):
    xf = x.astype(jnp.float32)
    mu = jnp.mean(xf, axis=-1, keepdims=True)
    var = jnp.mean(jnp.square(xf - mu), axis=-1, keepdims=True)
    y = (xf - mu) * lax.rsqrt(var + LN_EPS) * g.astype(jnp.float32) + b.astype(jnp.float32)
    return y.astype(x.dtype)


def rope(x, pos):
    half = x.shape[-1] // 2
    inv = ROPE_THETA ** (-jnp.arange(half, dtype=jnp.float32) / half)
    ang = pos.astype(jnp.float32)[:, None] * inv[None, :]
    cos = jnp.cos(ang)[None, :, None, :].astype(x.dtype)
    sin = jnp.sin(ang)[None, :, None, :].astype(x.dtype)
    x1, x2 = x[..., :half], x[..., half:]
    return jnp.concatenate([x1 * cos - x2 * sin, x2 * cos + x1 * sin], axis=-1)


def to_blocks(a):
    b, s = a.shape[:2]
    return jnp.moveaxis(a.reshape(b, s // Q_BLOCK, Q_BLOCK, *a.shape[2:]), 1, 0)


def from_blocks(a):
    nb, b, qb = a.shape[:3]
    return jnp.moveaxis(a, 0, 1).reshape(b, nb * qb, *a.shape[3:])


def fox_attention(q, k, v, log_f):
    b, s, h, dh = q.shape
    c = jnp.cumsum(log_f, axis=1)
    c_k = jnp.transpose(c, (0, 2, 1))[:, :, None, :]
    kpos = jnp.arange(s)
    scale = dh ** -0.5

    def block(args):
        qb, cb, bi = args
        qpos = bi * Q_BLOCK + jnp.arange(Q_BLOCK)
        logits = jnp.einsum('bqhd,bkhd->bhqk', qb, k).astype(jnp.float32) * scale
        logits = logits + jnp.transpose(cb, (0, 2, 1))[..., None] - c_k
        causal = kpos[None, :] <= qpos[:, None]
        logits = jnp.where(causal, logits, NEG)
        w = jax.nn.softmax(logits, axis=-1)
        return jnp.einsum('bhqk,bkhd->bqhd', w.astype(v.dtype), v)

    out = lax.map(block, (to_blocks(q), to_blocks(c), jnp.arange(s // Q_BLOCK)))
    return from_blocks(out).reshape(b, s, h * dh)


def stick_breaking_attention(q, k, v):
    b, s, h, dh = q.shape
    kpos = jnp.arange(s)
    scale = dh ** -0.5

    def block(args):
        qb, bi = args
        qpos = bi * Q_BLOCK + jnp.arange(Q_BLOCK)
        z = jnp.einsum('bqhd,bkhd->bhqk', qb, k).astype(jnp.float32) * scale
        strict = kpos[None, :] < qpos[:, None]
        log_not = jnp.where(strict, jax.nn.log_sigmoid(-z), 0.0)
        after = lax.cumsum(log_not, axis=3, reverse=True) - log_not
        a = jnp.where(strict, jnp.exp(jax.nn.log_sigmoid(z) + after), 0.0)
        return jnp.einsum('bhqk,bkhd->bqhd', a.astype(v.dtype), v)

    out = lax.map(block, (to_blocks(q), jnp.arange(s // Q_BLOCK)))
    return from_blocks(out).reshape(b, s, h * dh)


def dsa_attention(q, k, v, iq, ik, iw):
    b, s, h, dh = q.shape
    n_sel = min(DSA_TOPK_MAX, s // 4)
    kchunk = jnp.arange(s) // CHUNK
    bidx = jnp.arange(b)[:, None, None]
    scale = dh ** -0.5

    def block(args):
        qb, iqb, iwb, bi = args
        qchunk = (bi * Q_BLOCK + jnp.arange(Q_BLOCK)) // CHUNK
        admiss = kchunk[None, :] <= qchunk[:, None]
        idx_logits = jnp.einsum('bqhd,bkd->bqhk', iqb, ik).astype(jnp.float32)
        score = jnp.einsum('bqh,bqhk->bqk', iwb.astype(jnp.float32), jax.nn.relu(idx_logits))
        score = jnp.where(admiss[None], score, NEG)
        _, sel = lax.top_k(score, n_sel)
        sel_ok = (sel // CHUNK) <= qchunk[None, :, None]
        ks = k[bidx, sel]
        vs = v[bidx, sel]
        logits = jnp.einsum('bqhd,bqnd->bhqn', qb, ks).astype(jnp.float32) * scale
        logits = jnp.where(sel_ok[:, None], logits, NEG)
        w = jax.nn.softmax(logits, axis=-1)
        return jnp.einsum('bhqn,bqnd->bqhd', w.astype(vs.dtype), vs)

    out = lax.map(block, (to_blocks(q), to_blocks(iq), to_blocks(iw), jnp.arange(s // Q_BLOCK)))
    return from_blocks(out).reshape(b, s, h * dh)


def setup_inputs(seed: int = 0) -> dict:
    key = jax.random.key(seed)
    ks = jax.random.split(key, 20)
    n = jax.random.normal
    f32 = jnp.float32
    L = DEPTH
    bt = DEEPNORM_BETA
    return {
        'x': n(ks[0], (BATCH, SEQ, D_MODEL), f32),
        'p': n(ks[1], (DEPTH, BATCH, SEQ, PLE_DIM), f32),
        'w_in': n(ks[2], (L, D_MODEL, C_IN), f32) * D_MODEL ** -0.5,
        'b_forget': 2.0 + 0.5 * n(ks[3], (L, HEADS_FOX), f32),
        'w_up_fox': n(ks[4], (L, W_FOX, D_MODEL), f32) * W_FOX ** -0.5 * bt,
        'w_up_sb': n(ks[5], (L, W_SB, D_MODEL), f32) * W_SB ** -0.5 * bt,
        'w_up_dsa': n(ks[6], (L, W_DSA, D_MODEL), f32) * W_DSA ** -0.5 * bt,
        'w_out': n(ks[7], (L, D_MODEL, D_MODEL), f32) * D_MODEL ** -0.5 * bt,
        'ln1_g': 1.0 + 0.05 * n(ks[8], (L, D_MODEL), f32),
        'ln1_b': 0.02 * n(ks[9], (L, D_MODEL), f32),
        'w_ff_in': n(ks[10], (L, D_MODEL, D_FF), f32) * D_MODEL ** -0.5,
        'w_ff_out': n(ks[11], (L, D_FF, D_MODEL), f32) * D_FF ** -0.5 * bt,
        'w_ple': n(ks[12], (L, PLE_DIM, D_MODEL), f32) * PLE_DIM ** -0.5 * bt,
        'w_ple_gate': n(ks[13], (L, D_MODEL, D_MODEL), f32) * D_MODEL ** -0.5,
        'ln2_g': 1.0 + 0.05 * n(ks[14], (L, D_MODEL), f32),
        'ln2_b': 0.02 * n(ks[15], (L, D_MODEL), f32),
    }


def reference(x, p, w_in, b_forget, w_up_fox, w_up_sb, w_up_dsa, w_out, ln1_g, ln1_b,
              w_ff_in, w_ff_out, w_ple, w_ple_gate, ln2_g, ln2_b):
    b, s, _ = x.shape
    pos = jnp.arange(s)
    for i in range(DEPTH):
        h = x @ w_in[i]
        (fq, fk, fv, ff_logit, sq, sk, sv, dq, dk, dv, iq, ik, iw, g) = split_cols(h)
        fq = fq.reshape(b, s, HEADS_FOX, HEAD_DIM)
        fk = fk.reshape(b, s, HEADS_FOX, HEAD_DIM)
        fv = fv.reshape(b, s, HEADS_FOX, HEAD_DIM)
        log_f = jax.nn.log_sigmoid(ff_logit.astype(jnp.float32) + b_forget[i].astype(jnp.float32))
        o_fox = fox_attention(fq, fk, fv, log_f)

        o_sb = stick_breaking_attention(sq.reshape(b, s, HEADS_SB, HEAD_DIM),
                                        sk.reshape(b, s, HEADS_SB, HEAD_DIM),
                                        sv.reshape(b, s, HEADS_SB, HEAD_DIM))

        dq = rope(dq.reshape(b, s, HEADS_DSA, HEAD_DIM), pos)
        dk = rope(dk[:, :, None, :], pos)[:, :, 0, :]
        iq = rope(iq.reshape(b, s, IDX_HEADS, IDX_DIM), pos)
        ik = rope(ik[:, :, None, :], pos)[:, :, 0, :]
        o_dsa = dsa_attention(dq, dk, dv, iq, ik, iw)

        gates = jax.nn.sigmoid(g.reshape(b, s, N_BRANCH, D_MODEL))
        merged = (gates[:, :, 0] * (o_fox @ w_up_fox[i])
                  + gates[:, :, 1] * (o_sb @ w_up_sb[i])
                  + gates[:, :, 2] * (o_dsa @ w_up_dsa[i]))
        x = layer_norm(DEEPNORM_ALPHA * x + merged @ w_out[i], ln1_g[i], ln1_b[i])

        ffn = jnp.square(jax.nn.relu(x @ w_ff_in[i])) @ w_ff_out[i]
        ple = jax.nn.sigmoid(x @ w_ple_gate[i]) * (p[i] @ w_ple[i])
        x = layer_norm(DEEPNORM_ALPHA * x + ffn + ple, ln2_g[i], ln2_b[i])
    return x
```

```python
from contextlib import ExitStack
import numpy as np
import concourse.bass as bass
import concourse.mybir as mybir
from concourse.bass_utils import run_bass_kernel_spmd

F32 = mybir.dt.float32
BF16 = mybir.dt.bfloat16
U8 = mybir.dt.uint8
AF = mybir.ActivationFunctionType
ALU = mybir.AluOpType

D = 1024
HD = 64
NFOX, NSB, NDSA, NIDX = 6, 5, 5, 8
WF, WS, WD = 384, 320, 320
PLE = 256
DFF = 4096
C_IN = 6222
NEG = -1.0e30
ALPHA = 4 ** 0.25
LN_EPS = 1e-5
O_FQ, O_FK, O_FV, O_FF = 0, 384, 768, 1152
O_SQ, O_SK, O_SV = 1158, 1478, 1798
O_DQ, O_DK, O_DV = 2118, 2438, 2502
O_IQ, O_IK, O_IW = 2566, 3078, 3142
O_G = 3150
R_DQ, R_DK, R_IQ, R_IK = 0, 320, 384, 896
NROT = 960

SB_BASE = 16640
SB_LIMIT = 229344


class Buf:
    __slots__ = ("w", "rs", "chan")

    def __init__(self):
        self.w = None
        self.rs = {}
        self.chan = None


class Chan:
    def __init__(self, sem):
        self.sem = sem
        self.count = 0


class Sched:
    ENG = ("pe", "act", "dve", "pool", "sp")

    def __init__(self, nc, stack):
        self.nc = nc
        self.stack = stack
        self.ops = {e: [] for e in self.ENG}
        self.cnt = {e: 0 for e in self.ENG}
        self.sem = {e: stack.enter_context(nc.semaphore("s_" + e)) for e in ("pe", "act", "dve", "pool")}
        self.seen = {e: {} for e in self.ENG}
        self.chans = []
        self.chmap = {}

    def chan(self, key):
        if key not in self.chmap:
            c = Chan(self.stack.enter_context(self.nc.semaphore("c%d" % len(self.chans))))
            c.last = None
            self.chans.append(c)
            self.chmap[key] = c
        return self.chmap[key]

    def _wait(self, eng, tok):
        sem, val = tok
        k = id(sem)
        if self.seen[eng].get(k, 0) >= val:
            return
        self.seen[eng][k] = val
        self.ops[eng].append(("w", sem, val))

    def op(self, eng, fn, reads=(), writes=(), dma=None):
        own = self.sem.get(eng)
        deps = []
        if dma is not None:
            ch = self.chan(dma)
            if ch.last is not None and ch.last is not writes[0] and ch.count > 0:
                deps.append((ch.sem, ch.count))
            ch.last = writes[0]
        for b in reads:
            if b.w is not None:
                deps.append(b.w)
        for b in writes:
            if b.w is not None:
                deps.append(b.w)
            for t in b.rs.values():
                if dma is not None or t[0] is not own:
                    deps.append(t)
        for t in deps:
            if dma is None and eng == "pe" and t[0] is own:
                continue
            self._wait(eng, t)
        if dma is None:
            self.cnt[eng] += 1
            tok = (own, self.cnt[eng])
            self.ops[eng].append(("i", fn, own, 1))
        else:
            ch.count += 16
            tok = (ch.sem, ch.count)
            self.ops[eng].append(("i", fn, ch.sem, 16))
        for b in writes:
            b.w = tok
            b.rs = {}
        for b in reads:
            k = id(tok[0])
            if k not in b.rs or b.rs[k][1] < tok[1]:
                b.rs[k] = tok
        return tok

    def barrier(self):
        toks = [(self.sem[e], self.cnt[e]) for e in self.sem if self.cnt[e] > 0]
        toks += [(c.sem, c.count) for c in self.chans if c.count > 0]
        for e in self.ENG:
            for t in toks:
                self._wait(e, t)

    def emit(self):
        with self.nc.Block() as block:
            def mk(eng):
                ops = self.ops[eng]

                def f(e):
                    for o in ops:
                        if o[0] == "w":
                            e.wait_ge(o[1], o[2])
                        else:
                            o[1](e).then_inc(o[2], o[3])
                return f
            block.tensor(mk("pe"))
            block.scalar(mk("act"))
            block.vector(mk("dve"))
            block.gpsimd(mk("pool"))
            block.sync(mk("sp"))


class Tile:
    __slots__ = ("ap", "b")

    def __init__(self, ap, b=None):
        self.ap = ap
        self.b = b if b is not None else Buf()


class Arena:
    cnt = [0]

    def __init__(self, nc, base=SB_BASE, limit=SB_LIMIT):
        self.nc = nc
        self.top = base
        self.base = base
        self.limit = limit
        self.peak = 0

    def alloc(self, shape, dtype, at=None):
        esz = {F32: 4, BF16: 2, U8: 1}[dtype]
        size = esz
        for s in shape[1:]:
            size *= s
        size = (size + 63) // 64 * 64
        if at is None:
            off = self.top
            self.top += size
            self.peak = max(self.peak, self.top)
            assert self.top <= self.limit, ("SBUF overflow", self.top, self.limit)
        else:
            off = at
        Arena.cnt[0] += 1
        h = self.nc.alloc_sbuf_tensor_at("t%d" % Arena.cnt[0], list(shape), dtype, offset=off)
        return h.ap()

    def tile(self, shape, dtype):
        return Tile(self.alloc(shape, dtype))

    def mark(self):
        return self.top

    def release(self, m):
        self.top = m


def MM(out, lhsT, rhs, start=True, stop=True):
    return lambda e: e.matmul(out, lhsT, rhs, start=start, stop=stop)


def TRN(out, in_, ident):
    return lambda e: e.transpose(out, in_, ident)


def ACTF(out, in_, func, bias=0.0, scale=1.0):
    return lambda e: e.activation(out, in_, func, bias=bias, scale=scale)


def TS(out, in0, s1, s2, op0, op1=None):
    if op1 is None:
        return lambda e: e.tensor_scalar(out, in0, s1, None, op0)
    return lambda e: e.tensor_scalar(out, in0, s1, s2, op0, op1)


def TT(out, in0, in1, op):
    return lambda e: e.tensor_tensor(out, in0, in1, op)


def STT(out, in0, scalar, in1, op0, op1):
    return lambda e: e.scalar_tensor_tensor(out, in0, scalar, in1, op0, op1)


def CPY(out, in_):
    return lambda e: e.tensor_copy(out, in_)


def ACPY(out, in_):
    return lambda e: e.activation(out, in_, AF.Copy)


def MSET(ap, v):
    return lambda e: e.memset(ap, v)


def DMA(out, in_):
    return lambda e: e.dma_start(out=out, in_=in_)


def SCAN(out, d0, d1, init):
    return lambda e: e.tensor_tensor_scan(out, d0, d1, init, ALU.mult, ALU.add)


def RECIP(out, in_):
    return lambda e: e.reciprocal(out, in_)


def build_program(NSEQ, S, DEPTH, NSEL):
    NT = S // 128
    NG = S // 512
    nc = bass.Bass("TRN2", target_bir_lowering=False)
    stack = ExitStack()

    def din(name, shape):
        return nc.dram_tensor(name, list(shape), F32, kind="ExternalInput").ap()

    x_d = din("x", [NSEQ, S, D])
    p_d = din("p", [DEPTH, NSEQ, S, PLE])
    w_in_d = din("w_in", [DEPTH, D, C_IN])
    w_rot_d = din("w_rot", [DEPTH, D, NROT])
    bf_d = din("b_forget", [DEPTH, NFOX])
    wup_d = [din("w_up_fox", [DEPTH, WF, D]), din("w_up_sb", [DEPTH, WS, D]), din("w_up_dsa", [DEPTH, WD, D])]
    wout_d = din("w_out", [DEPTH, D, D])
    ln1g_d = din("ln1_g", [DEPTH, D])
    ln1b_d = din("ln1_b", [DEPTH, D])
    wfi_d = din("w_ff_in", [DEPTH, D, DFF])
    wfo_d = din("w_ff_out", [DEPTH, DFF, D])
    wple_d = din("w_ple", [DEPTH, PLE, D])
    wpg_d = din("w_ple_gate", [DEPTH, D, D])
    ln2g_d = din("ln2_g", [DEPTH, D])
    ln2b_d = din("ln2_b", [DEPTH, D])
    cst_d = din("consts", [128, 6, 128])
    cos_d = din("ropecos", [128, S])
    sin_d = din("ropesin", [128, S])
    out_d = nc.dram_tensor("out", [NSEQ, S, D], F32, kind="ExternalOutput").ap()
    dbg_d = nc.dram_tensor("dbg", [9, 128, S], F32, kind="ExternalOutput").ap() if DEBUG else None

    sc = Sched(nc, stack)
    ar = Arena(nc)
    psa = nc.alloc_psum_tensor("ps", [128, 8, 512], F32).ap()
    bank = [Tile(psa[:, k, :]) for k in range(8)]

    class Rot:
        def __init__(self, ids):
            self.ids = ids
            self.i = 0

        def next(self):
            t = bank[self.ids[self.i % len(self.ids)]]
            self.i += 1
            return t

    op = sc.op

    x_tok = ar.alloc([128, NT, D], F32)
    xtb = [Buf() for _ in range(NT)]
    oT_base = ar.mark()
    oT = [[ar.tile([128, S], BF16) for _ in range(3)] for _ in range(3)]
    oT_sz = (ar.mark() - oT_base) // 3

    def scratch(first_free_mixer):
        if S < 2048:
            return ar
        return Arena(nc, oT_base + first_free_mixer * oT_sz, oT_base + 3 * oT_sz)

    cst_f = ar.tile([128, 6, 128], F32)
    cst_b = ar.tile([128, 6, 128], BF16)
    identS_b, fmaskS_b = cst_b.ap[:, 4, :], cst_b.ap[:, 5, :]
    ident_b, ones_b, fmask_b, tri_b = (cst_b.ap[:, i, :] for i in range(4))
    ident_f, ones_f, tri_f = cst_f.ap[:, 0, :], cst_f.ap[:, 1, :], cst_f.ap[:, 3, :]
    wslot = [ar.tile([128, 8, 512], BF16) for _ in range(2)]
    wsl_i = [0]
    small = ar.tile([128, 16], F32)
    epsT = ar.tile([128, 1], F32)
    outb = [Buf() for _ in range(4)]

    op("sp", DMA(cst_f.ap, cst_d), writes=[cst_f.b], dma="cst")
    op("dve", CPY(cst_b.ap, cst_f.ap), reads=[cst_f.b], writes=[cst_b.b])
    op("pool", MSET(epsT.ap, LN_EPS), writes=[epsT.b])
    if DEBUG:
        for mi in range(3):
            for pr in range(3):
                op("pool", MSET(oT[mi][pr].ap, 0.0), writes=[oT[mi][pr].b])

    def ones_bc(npart, n):
        return cst_f.ap[0:npart, 1, 0:1].to_broadcast([npart, n])

    def wload(dst_ap, dst_b, src_ap, key):
        op("pool", DMA(dst_ap, src_ap), writes=[dst_b], dma=key)

    def next_slot():
        k = wsl_i[0] % 2
        wsl_i[0] += 1
        return wslot[k], "w%d" % k

    def wtile(src2d, ncols, rows=D):
        t, key = next_slot()
        kc = rows // 128
        wload(t.ap[:, 0:kc, 0:ncols], t.b, src2d.rearrange("(kc p) n -> p kc n", p=128), key)
        return t

    def make_xT(xT, xTb):
        m = ar.mark()
        xb = [ar.tile([128, D], BF16) for _ in range(2)]
        rot = Rot([0, 1, 2, 3])
        for i in range(NT):
            t = xb[i % 2]
            op("pool", CPY(t.ap, x_tok[:, i, :]), reads=[xtb[i]], writes=[t.b])
            pb = rot.next()
            pbv = pb.ap.bitcast(BF16)
            for c in range(8):
                op("pe", TRN(pbv[:, c * 128:(c + 1) * 128], t.ap[:, c * 128:(c + 1) * 128], ident_b),
                   reads=[t.b, cst_b.b], writes=[pb.b])
            op("act", ACPY(xT[:, :, i * 128:(i + 1) * 128], pbv.rearrange("p (c t) -> p c t", c=8)),
               reads=[pb.b], writes=[xTb[i // 4]])
        sc.barrier()
        ar.release(m)

    def new_xT():
        xT = ar.alloc([128, 8, S], BF16)
        xTb = [Buf() for _ in range(NG)]
        return xT, xTb

    def proj_fm(xT, xTb, wt, col0, ncols, dst_ap, dst_b, rot, evac="act"):
        for g in range(NG):
            pb = rot.next()
            for kc in range(8):
                op("pe", MM(pb.ap[0:ncols, :], wt.ap[:, kc, col0:col0 + ncols], xT[:, kc, g * 512:(g + 1) * 512],
                            start=(kc == 0), stop=(kc == 7)), reads=[wt.b, xTb[g]], writes=[pb.b])
            if evac == "act":
                op("act", ACPY(dst_ap[0:ncols, g * 512:(g + 1) * 512], pb.ap[0:ncols, :]), reads=[pb.b], writes=[dst_b])
            else:
                op("dve", CPY(dst_ap[0:ncols, g * 512:(g + 1) * 512], pb.ap[0:ncols, :]), reads=[pb.b], writes=[dst_b])

    def proj_rope(xT, xTb, wa, wb, ca, cb, ncols, dst_ap, dst_b, rot, ct):
        for g in range(NG):
            op("sp", DMA(ct.ap[:, 0, :], cos_d[:, g * 512:(g + 1) * 512]), writes=[ct.b], dma="cs")
            op("sp", DMA(ct.ap[:, 1, :], sin_d[:, g * 512:(g + 1) * 512]), writes=[ct.b], dma="cs")
            pa = rot.next()
            pb = rot.next()
            for kc in range(8):
                op("pe", MM(pa.ap[0:ncols, :], wa.ap[:, kc, ca:ca + ncols], xT[:, kc, g * 512:(g + 1) * 512],
                            start=(kc == 0), stop=(kc == 7)), reads=[wa.b, xTb[g]], writes=[pa.b])
            for kc in range(8):
                op("pe", MM(pb.ap[0:ncols, :], wb.ap[:, kc, cb:cb + ncols], xT[:, kc, g * 512:(g + 1) * 512],
                            start=(kc == 0), stop=(kc == 7)), reads=[wb.b, xTb[g]], writes=[pb.b])
            t1 = ct.ap[0:ncols, 2, :]
            t2 = ct.ap[0:ncols, 3, :]
            op("dve", TT(t1, pa.ap[0:ncols, :], ct.ap[0:ncols, 0, :], ALU.mult), reads=[pa.b, ct.b], writes=[ct.b])
            op("dve", TT(t2, pb.ap[0:ncols, :], ct.ap[0:ncols, 1, :], ALU.mult), reads=[pb.b, ct.b], writes=[ct.b])
            op("pool", TT(dst_ap[0:ncols, g * 512:(g + 1) * 512], t1, t2, ALU.add), reads=[ct.b], writes=[dst_b])

    def softmax_norm(acc, den_row, bp, dst, g, rd, tmp, bcb):
        r = den_row
        op("dve", RECIP(rd.ap[r:r + 1, :], acc.ap[r:r + 1, :]), reads=[acc.b], writes=[rd.b])
        op("pe", MM(bcb.ap, ones_f[r:r + 1, :], rd.ap[r:r + 1, :]), reads=[rd.b, cst_f.b], writes=[bcb.b])
        op("act", ACPY(tmp.ap[bp:bp + 64, :], bcb.ap[bp:bp + 64, :]), reads=[bcb.b], writes=[tmp.b])
        op("dve", TT(dst.ap[bp:bp + 64, g * 512:(g + 1) * 512], acc.ap[bp:bp + 64, :], tmp.ap[bp:bp + 64, :], ALU.mult),
           reads=[acc.b, tmp.b], writes=[dst.b])

    def fox_phase(l):
        m = ar.mark()
        sa = scratch(1)
        qT = [ar.tile([128, S], BF16) for _ in range(3)]
        kT = [ar.tile([128, S], BF16) for _ in range(3)]
        V = ar.tile([128, NT, 3 * 192], BF16)
        CQ = [Tile(sa.alloc([128, S], BF16)) for _ in range(3)]
        negc = ar.tile([128, NT * NFOX], F32)
        bneg = ar.tile([128, 1], F32)
        ft = [Tile(sa.alloc([8, 512], F32)) for _ in range(3)]
        c3 = Tile(sa.alloc([8, 3, 512], BF16))
        m2 = ar.mark()
        xT, xTb = new_xT()
        make_xT(xT, xTb)
        rot = Rot([0, 1, 2, 3])
        w_l = w_in_d[l]
        if FOX_STOP == 1:
            sc.barrier(); ar.release(m); return
        for cq_ in CQ:
            op("pool", MSET(cq_.ap, 0.0), writes=[cq_.b])
        op("pool", MSET(V.ap, 0.0), writes=[V.b])
        for j in range(3):
            op("pool", MSET(V.ap[:, :, j * 192 + 64:j * 192 + 65], 1.0), writes=[V.b])
        op("sp", DMA(bneg.ap[0:NFOX, :], bf_d[l].rearrange("(h o) -> h o", o=1)), writes=[bneg.b], dma="bneg")
        op("dve", TS(bneg.ap[0:NFOX, :], bneg.ap[0:NFOX, :], -1.0, None, ALU.mult), reads=[bneg.b], writes=[bneg.b])
        wq = wtile(w_l[:, O_FQ:O_FQ + 384], 384)
        for j in range(3):
            proj_fm(xT, xTb, wq, j * 128, 128, qT[j].ap, qT[j].b, rot)
        wk = wtile(w_l[:, O_FK:O_FK + 384], 384)
        for j in range(3):
            proj_fm(xT, xTb, wk, j * 128, 128, kT[j].ap, kT[j].b, rot, evac="dve")
        wv = wtile(w_l[:, O_FV:O_FV + 390], 390)
        for i in range(NT):
            pb = rot.next()
            for kc in range(8):
                op("pe", MM(pb.ap[:, 0:384], xT[:, kc, i * 128:(i + 1) * 128], wv.ap[:, kc, 0:384],
                            start=(kc == 0), stop=(kc == 7)), reads=[wv.b, xTb[i // 4]], writes=[pb.b])
            src = pb.ap[:, 0:384].rearrange("p (j e d) -> p j e d", j=3, e=2)
            dstv = V.ap[:, i, :].rearrange("p (j c) -> p j c", c=192)
            op("act", ACPY(dstv[:, :, 0:64], src[:, :, 0, :]), reads=[pb.b], writes=[V.b])
            op("dve", CPY(dstv[:, :, 128:192], src[:, :, 1, :]), reads=[pb.b], writes=[V.b])
        if FOX_STOP == 2:
            sc.barrier(); ar.release(m); return
        e_t, n_t, r_t = ft
        for g in range(NG):
            pb = rot.next()
            for kc in range(8):
                op("pe", MM(pb.ap[0:NFOX, :], wv.ap[:, kc, 384:390], xT[:, kc, g * 512:(g + 1) * 512],
                            start=(kc == 0), stop=(kc == 7)), reads=[wv.b, xTb[g]], writes=[pb.b])
            op("act", ACTF(e_t.ap[0:NFOX, :], pb.ap[0:NFOX, :], AF.Exp, bias=bneg.ap[0:NFOX, :], scale=-1.0),
               reads=[pb.b, bneg.b], writes=[e_t.b])
            op("act", ACTF(e_t.ap[0:NFOX, :], e_t.ap[0:NFOX, :], AF.Ln, bias=1.0, scale=1.0), reads=[e_t.b], writes=[e_t.b])
            init = 0.0 if g == 0 else small.ap[0:NFOX, 0:1]
            op("dve", SCAN(n_t.ap[0:NFOX, :], ones_bc(NFOX, 512), e_t.ap[0:NFOX, :], init),
               reads=[e_t.b, small.b, cst_f.b], writes=[n_t.b])
            op("dve", CPY(small.ap[0:NFOX, 0:1], n_t.ap[0:NFOX, 511:512]), reads=[n_t.b], writes=[small.b])
            for s in range(4):
                i = g * 4 + s
                tb = rot.next()
                op("pe", MM(tb.ap[:, 0:NFOX], n_t.ap[0:NFOX, s * 128:(s + 1) * 128], ident_f[0:NFOX, 0:NFOX]),
                   reads=[n_t.b, cst_f.b], writes=[tb.b])
                op("act", ACPY(negc.ap[:, i * NFOX:(i + 1) * NFOX], tb.ap[:, 0:NFOX]), reads=[tb.b], writes=[negc.b])
            op("dve", TS(r_t.ap[0:NFOX, :], n_t.ap[0:NFOX, :], -8.0, None, ALU.mult), reads=[n_t.b], writes=[r_t.b])
            for k3 in range(3):
                op("dve", CPY(c3.ap[0:NFOX, k3, :], r_t.ap[0:NFOX, :]), reads=[r_t.b], writes=[c3.b])
                if k3 < 2:
                    op("dve", TT(r_t.ap[0:NFOX, :], r_t.ap[0:NFOX, :], c3.ap[0:NFOX, k3, :], ALU.subtract),
                       reads=[r_t.b, c3.b], writes=[r_t.b])
            for h in range(NFOX):
                cq = CQ[h // 2]
                rb = 64 * (h % 2)
                for k3 in range(3):
                    op("sp", DMA(cq.ap[rb + k3:rb + k3 + 1, g * 512:(g + 1) * 512], c3.ap[h:h + 1, k3, :]),
                       reads=[c3.b], writes=[cq.b], dma="cq%d" % (h // 2))
        sc.barrier()
        ar.release(m2)
        if FOX_STOP == 3:
            ar.release(m); return
        PT = [ar.tile([128, 512], BF16) for _ in range(3)]
        rd = ar.tile([128, 512], F32)
        tmp = ar.tile([128, 512], F32)
        srot = Rot([0, 1, 2, 3])
        arot = Rot([4, 5, 6])
        bcb = bank[7]
        pti = 0
        for h in range(NFOX):
            j, bp = h // 2, 64 * (h % 2)
            cq = CQ[j]
            lv = (j * 192, j * 192 + 65) if bp == 0 else (j * 192 + 64, j * 192 + 192)
            den_row = 64 if bp == 0 else 0
            for g in range(NG):
                acc = arot.next()
                nkb = 4 * g + 4
                for kb in range(nkb):
                    c0 = 0 if kb < 4 * g else 128 * (kb - 4 * g)
                    diag = kb >= 4 * g
                    sp_ = srot.next()
                    q0 = g * 512 + c0
                    op("pe", MM(sp_.ap[:, c0:512], kT[j].ap[bp:bp + 64, kb * 128:(kb + 1) * 128],
                                qT[j].ap[bp:bp + 64, q0:(g + 1) * 512], start=True, stop=False),
                       reads=[kT[j].b, qT[j].b], writes=[sp_.b])
                    op("pe", MM(sp_.ap[:, c0:512], ones_b[bp:bp + 64, :], cq.ap[bp:bp + 64, q0:(g + 1) * 512],
                                start=False, stop=not diag), reads=[cq.b, cst_b.b], writes=[sp_.b])
                    if diag:
                        op("pe", MM(sp_.ap[:, c0:c0 + 128], ident_b[bp:bp + 64, :], fmask_b[bp:bp + 64, :],
                                    start=False, stop=False), reads=[cst_b.b], writes=[sp_.b])
                        op("pe", MM(sp_.ap[:, c0:c0 + 128], identS_b[bp:bp + 64, :], fmaskS_b[bp:bp + 64, :],
                                    start=False, stop=True), reads=[cst_b.b], writes=[sp_.b])
                    pt = PT[pti % 3]
                    pti += 1
                    op("act", ACTF(pt.ap[:, c0:512], sp_.ap[:, c0:512], AF.Exp,
                                   bias=negc.ap[:, kb * NFOX + h:kb * NFOX + h + 1], scale=0.125),
                       reads=[sp_.b, negc.b], writes=[pt.b])
                    op("pe", MM(acc.ap[:, c0:512] if bp else acc.ap[0:65, c0:512], V.ap[:, kb, lv[0]:lv[1]], pt.ap[:, c0:512],
                                start=(kb == 0), stop=(kb == nkb - 1)), reads=[V.b, pt.b], writes=[acc.b])
                softmax_norm(acc, den_row, bp, oT[0][j], g, rd, tmp, bcb)
        sc.barrier()
        ar.release(m)

    def sb_phase(l):
        m = ar.mark()
        qT = [ar.tile([128, S], BF16) for _ in range(3)]
        kT = [ar.tile([128, S], BF16) for _ in range(3)]
        V = ar.tile([128, NT, 384], BF16)
        m2 = ar.mark()
        xT, xTb = new_xT()
        make_xT(xT, xTb)
        rot = Rot([0, 1, 2, 3])
        w_l = w_in_d[l]
        op("pool", MSET(V.ap, 0.0), writes=[V.b])
        wq = wtile(w_l[:, O_SQ:O_SQ + 320], 320)
        for j in range(3):
            proj_fm(xT, xTb, wq, j * 128, 128 if j < 2 else 64, qT[j].ap, qT[j].b, rot)
        wk = wtile(w_l[:, O_SK:O_SK + 320], 320)
        for j in range(3):
            proj_fm(xT, xTb, wk, j * 128, 128 if j < 2 else 64, kT[j].ap, kT[j].b, rot, evac="dve")
        wv = wtile(w_l[:, O_SV:O_SV + 320], 320)
        for i in range(NT):
            pb = rot.next()
            for kc in range(8):
                op("pe", MM(pb.ap[:, 0:320], xT[:, kc, i * 128:(i + 1) * 128], wv.ap[:, kc, 0:320],
                            start=(kc == 0), stop=(kc == 7)), reads=[wv.b, xTb[i // 4]], writes=[pb.b])
            op("act", ACPY(V.ap[:, i, 0:320], pb.ap[:, 0:320]), reads=[pb.b], writes=[V.b])
        sc.barrier()
        ar.release(m2)
        NB = 2
        SPt = [ar.tile([128, S], F32) for _ in range(NB)]
        PXt = [ar.tile([128, S], F32) for _ in range(NB)]
        At = [ar.tile([128, S], BF16) for _ in range(NB)]
        ATt = [ar.tile([128, NT, 128], BF16) for _ in range(NB)]
        ntot = [ar.tile([128, 1], F32) for _ in range(NB)]
        trot = Rot([4, 5])
        arot = Rot([6, 7])
        it = 0
        for h in range(NSB):
            j, bp = h // 2, 64 * (h % 2)
            for i in range(NT):
                nk = 128 * (i + 1)
                sp_, px, a_, at_, nt_ = SPt[it % NB], PXt[it % NB], At[it % NB], ATt[it % NB], ntot[it % NB]
                it += 1
                nb = (nk + 511) // 512
                for c in range(nb):
                    n = min(512, nk - c * 512)
                    zb = bank[c]
                    op("pe", MM(zb.ap[:, 0:n], qT[j].ap[bp:bp + 64, i * 128:(i + 1) * 128],
                                kT[j].ap[bp:bp + 64, c * 512:c * 512 + n]), reads=[qT[j].b, kT[j].b], writes=[zb.b])
                zbs = [bank[c].b for c in range(nb)]
                zall = psa[:, 0:nb, :].rearrange("p c n -> p (c n)")[:, 0:nk]
                op("act", ACTF(sp_.ap[:, 0:nk], zall, AF.Exp, scale=0.125), reads=zbs, writes=[sp_.b])
                op("act", ACTF(sp_.ap[:, 0:nk], sp_.ap[:, 0:nk], AF.Ln, bias=1.0), reads=[sp_.b], writes=[sp_.b])
                op("pool", TT(sp_.ap[:, nk - 128:nk], sp_.ap[:, nk - 128:nk], tri_f, ALU.mult),
                   reads=[sp_.b, cst_f.b], writes=[sp_.b])
                op("dve", MSET(px.ap[:, 0:1], 0.0), writes=[px.b])
                op("dve", SCAN(px.ap[:, 1:nk], ones_bc(128, nk - 1), sp_.ap[:, 0:nk - 1], 0.0),
                   reads=[sp_.b, cst_f.b, px.b], writes=[px.b])
                op("dve", TS(nt_.ap, px.ap[:, nk - 1:nk], -1.0, None, ALU.mult), reads=[px.b], writes=[nt_.b])
                op("dve", STT(sp_.ap[:, 0:nk], zall, 0.125, px.ap[:, 0:nk], ALU.mult, ALU.add),
                   reads=zbs + [px.b], writes=[sp_.b])
                op("act", ACTF(a_.ap[:, 0:nk], sp_.ap[:, 0:nk], AF.Exp, bias=nt_.ap, scale=1.0),
                   reads=[sp_.b, nt_.b], writes=[a_.b])
                op("pool", TT(a_.ap[:, nk - 128:nk], a_.ap[:, nk - 128:nk], tri_b, ALU.mult),
                   reads=[a_.b, cst_b.b], writes=[a_.b])
                for c8 in range(0, i + 1, 8):
                    n8 = min(8, i + 1 - c8)
                    tb = trot.next()
                    tbv = tb.ap.bitcast(BF16)
                    for kb in range(c8, c8 + n8):
                        op("pe", TRN(tbv[:, (kb - c8) * 128:(kb - c8 + 1) * 128], a_.ap[:, kb * 128:(kb + 1) * 128], ident_b),
                           reads=[a_.b, cst_b.b], writes=[tb.b])
                    op("act", ACPY(at_.ap[:, c8:c8 + n8, :], tbv[:, 0:n8 * 128].rearrange("p (c t) -> p c t", t=128)),
                       reads=[tb.b], writes=[at_.b])
                acc = arot.next()
                lo, hi = (j * 128, j * 128 + 64) if bp == 0 else (j * 128, j * 128 + 128)
                for kb in range(i + 1):
                    op("pe", MM(acc.ap[0:hi - lo, 0:128], V.ap[:, kb, lo:hi], at_.ap[:, kb, :],
                                start=(kb == 0), stop=(kb == i)), reads=[V.b, at_.b], writes=[acc.b])
                op("dve", CPY(oT[1][j].ap[bp:bp + 64, i * 128:(i + 1) * 128], acc.ap[bp:bp + 64, 0:128]),
                   reads=[acc.b], writes=[oT[1][j].b])
        sc.barrier()
        ar.release(m)

    def dsa_phase(l):
        m = ar.mark()
        sa = scratch(2)
        dqT = [ar.tile([128, S], BF16) for _ in range(3)]
        kkd = ar.tile([128, S], BF16)
        kki = ar.tile([128, S], BF16)
        iqT = [ar.tile([128, S], BF16) for _ in range(4)]
        V = ar.tile([128, NT, 192], BF16)
        iw = ar.tile([128, NT, 8], F32)
        aw = ar.tile([128, NT, 8], F32)
        sg = ar.tile([128, NT, 8], F32)
        m2 = ar.mark()
        xT, xTb = new_xT()
        make_xT(xT, xTb)
        ct = Tile(sa.alloc([128, 4, 512], F32))
        rot = Rot([0, 1, 2, 3, 4, 5])
        w_l = w_in_d[l]
        r_l = w_rot_d[l]
        op("pool", MSET(V.ap, 0.0), writes=[V.b])
        op("pool", MSET(V.ap[:, :, 64:65], 1.0), writes=[V.b])
        wa = wtile(w_l[:, O_DQ:O_DQ + 320], 320)
        wb = wtile(r_l[:, R_DQ:R_DQ + 320], 320)
        for j in range(3):
            proj_rope(xT, xTb, wa, wb, j * 128, j * 128, 128 if j < 2 else 64, dqT[j].ap, dqT[j].b, rot, ct)
        wa = wtile(w_l[:, O_IQ:O_IQ + 512], 512)
        wb = wtile(r_l[:, R_IQ:R_IQ + 512], 512)
        for j in range(4):
            proj_rope(xT, xTb, wa, wb, j * 128, j * 128, 128, iqT[j].ap, iqT[j].b, rot, ct)
        wa, ka = next_slot()
        for q_, o_ in enumerate((O_DK, O_DK, O_IK, O_IK)):
            wload(wa.ap[:, :, q_ * 64:(q_ + 1) * 64], wa.b, w_l[:, o_:o_ + 64].rearrange("(kc p) n -> p kc n", p=128), ka)
        wb, kb_ = next_slot()
        for q_, o_ in enumerate((R_DK, R_DK, R_IK, R_IK)):
            wload(wb.ap[:, :, q_ * 64:(q_ + 1) * 64], wb.b, r_l[:, o_:o_ + 64].rearrange("(kc p) n -> p kc n", p=128), kb_)
        proj_rope(xT, xTb, wa, wb, 0, 0, 128, kkd.ap, kkd.b, rot, ct)
        proj_rope(xT, xTb, wa, wb, 128, 128, 128, kki.ap, kki.b, rot, ct)
        wv, kv = next_slot()
        wload(wv.ap[:, :, 0:64], wv.b, w_l[:, O_DV:O_DV + 64].rearrange("(kc p) n -> p kc n", p=128), kv)
        wload(wv.ap[:, :, 64:72], wv.b, w_l[:, O_IW:O_IW + 8].rearrange("(kc p) n -> p kc n", p=128), kv)
        for i in range(NT):
            pb = rot.next()
            for kc in range(8):
                op("pe", MM(pb.ap[:, 0:72], xT[:, kc, i * 128:(i + 1) * 128], wv.ap[:, kc, 0:72],
                            start=(kc == 0), stop=(kc == 7)), reads=[wv.b, xTb[i // 4]], writes=[pb.b])
            op("act", ACPY(V.ap[:, i, 0:64], pb.ap[:, 0:64]), reads=[pb.b], writes=[V.b])
            op("act", ACPY(V.ap[:, i, 128:192], pb.ap[:, 0:64]), reads=[pb.b], writes=[V.b])
            op("dve", CPY(iw.ap[:, i, :], pb.ap[:, 64:72]), reads=[pb.b], writes=[iw.b])
        op("act", ACTF(aw.ap, iw.ap, AF.Abs), reads=[iw.b], writes=[aw.b])
        op("act", ACTF(sg.ap, iw.ap, AF.Sign), reads=[iw.b], writes=[sg.b])
        sc.barrier()
        ar.release(m2)
        score = Tile(wslot[0].ap.rearrange("p a b -> p (a b)").bitcast(F32), wslot[0].b)
        work = Tile(wslot[1].ap.rearrange("p a b -> p (a b)").bitcast(F32), wslot[1].b)
        rl = [ar.tile([128, 512], F32) for _ in range(2)]
        msk = ar.tile([128, S], BF16)
        mT = ar.tile([128, NT, 512], BF16)
        m8 = ar.tile([128, 8], F32)
        thr = ar.tile([128, 1], F32)
        ET = [ar.tile([128, 512], BF16) for _ in range(2)]
        PT = [ar.tile([128, 512], BF16) for _ in range(2)]
        rd = ar.tile([128, 512], F32)
        tmp = ar.tile([128, 512], F32)
        trot = Rot([4])
        srot = Rot([5, 6])
        bcb = bank[4]
        ri = 0
        ei = 0
        for g in range(NG):
            for s in range(4):
                i = g * 4 + s
                nk = 128 * (i + 1)
                nb = (nk + 511) // 512
                for h in range(NIDX):
                    j, bp = h // 2, 64 * (h % 2)
                    for c in range(nb):
                        n = min(512, nk - c * 512)
                        lb = bank[c]
                        op("pe", MM(lb.ap[:, 0:n], iqT[j].ap[bp:bp + 64, i * 128:(i + 1) * 128],
                                    kki.ap[bp:bp + 64, c * 512:c * 512 + n]), reads=[iqT[j].b, kki.b], writes=[lb.b])
                        r_ = rl[ri % 2]
                        ri += 1
                        op("act", ACTF(r_.ap[:, 0:n], lb.ap[:, 0:n], AF.Relu, scale=aw.ap[:, i, h:h + 1]),
                           reads=[lb.b, aw.b], writes=[r_.b])
                        dst = score.ap[:, c * 512:c * 512 + n]
                        if h == 0:
                            op("dve", TS(dst, r_.ap[:, 0:n], sg.ap[:, i, h:h + 1], None, ALU.mult),
                               reads=[r_.b, sg.b], writes=[score.b])
                        else:
                            op("dve", STT(dst, r_.ap[:, 0:n], sg.ap[:, i, h:h + 1], dst, ALU.mult, ALU.add),
                               reads=[r_.b, sg.b, score.b], writes=[score.b])
                op("dve", MSET(score.ap[0:64, nk - 64:nk], NEG), writes=[score.b])
                if nk > NSEL:
                    rounds = NSEL // 8
                    for r in range(rounds):
                        src = score if r == 0 else work
                        op("dve", (lambda o, i_: (lambda e: e.max(out=o, in_=i_)))(m8.ap, src.ap[:, 0:nk]),
                           reads=[src.b], writes=[m8.b])
                        if r < rounds - 1:
                            op("dve", (lambda o, t_, v_: (lambda e: e.match_replace(
                                out=o, in_to_replace=t_, in_values=v_, imm_value=-3.0e38)))(
                                work.ap[:, 0:nk], m8.ap, src.ap[:, 0:nk]), reads=[src.b, m8.b], writes=[work.b])
                    op("dve", TS(thr.ap, m8.ap[:, 7:8], -1.0e29, None, ALU.max), reads=[m8.b], writes=[thr.b])
                else:
                    op("dve", MSET(thr.ap, -1.0e29), writes=[thr.b])
                op("dve", TS(msk.ap[:, 0:nk], score.ap[:, 0:nk], thr.ap, None, ALU.is_ge),
                   reads=[score.b, thr.b], writes=[msk.b])
                for c8 in range(0, i + 1, 8):
                    n8 = min(8, i + 1 - c8)
                    tb = trot.next()
                    tbv = tb.ap.bitcast(BF16)
                    for kb in range(c8, c8 + n8):
                        op("pe", TRN(tbv[:, (kb - c8) * 128:(kb - c8 + 1) * 128], msk.ap[:, kb * 128:(kb + 1) * 128], ident_b),
                           reads=[msk.b, cst_b.b], writes=[tb.b])
                    op("act", ACPY(mT.ap[:, c8:c8 + n8, s * 128:(s + 1) * 128],
                                   tbv[:, 0:n8 * 128].rearrange("p (c t) -> p c t", t=128)), reads=[tb.b], writes=[mT.b])
            for h in range(NDSA):
                j, bp = h // 2, 64 * (h % 2)
                lv = (0, 65) if bp == 0 else (64, 192)
                den_row = 64 if bp == 0 else 0
                acc = bank[7]
                nkb = 4 * g + 4
                for kb in range(nkb):
                    c0 = 0 if kb < 4 * g else 128 * (kb - 4 * g)
                    sp_ = srot.next()
                    q0 = g * 512 + c0
                    op("pe", MM(sp_.ap[:, c0:512], kkd.ap[bp:bp + 64, kb * 128:(kb + 1) * 128],
                                dqT[j].ap[bp:bp + 64, q0:(g + 1) * 512]), reads=[kkd.b, dqT[j].b], writes=[sp_.b])
                    et = ET[ei % 2]
                    pt = PT[ei % 2]
                    ei += 1
                    op("act", ACTF(et.ap[:, c0:512], sp_.ap[:, c0:512], AF.Exp, scale=0.125), reads=[sp_.b], writes=[et.b])
                    op("pool", TT(pt.ap[:, c0:512], et.ap[:, c0:512], mT.ap[:, kb, c0:512], ALU.mult),
                       reads=[et.b, mT.b], writes=[pt.b])
                    op("pe", MM(acc.ap[:, c0:512] if bp else acc.ap[0:65, c0:512], V.ap[:, kb, lv[0]:lv[1]], pt.ap[:, c0:512],
                                start=(kb == 0), stop=(kb == nkb - 1)), reads=[V.b, pt.b], writes=[acc.b])
                softmax_norm(acc, den_row, bp, oT[2][j], g, rd, tmp, bcb)
        sc.barrier()
        ar.release(m)

    def layer_norm_tiles(l, g_d, b_d):
        m = ar.mark()
        gB = ar.tile([128, D], F32)
        bB = ar.tile([128, D], F32)
        st = ar.tile([128, 2, 6], F32)
        mv = ar.tile([128, 4], F32)
        op("sp", DMA(gB.ap, g_d[l:l + 1, :].to_broadcast([128, D])), writes=[gB.b], dma="lng")
        op("sp", DMA(bB.ap, b_d[l:l + 1, :].to_broadcast([128, D])), writes=[bB.b], dma="lnb")
        for i in range(NT):
            xi = x_tok[:, i, :]
            for c in range(2):
                op("dve", (lambda o, i_: (lambda e: e.bn_stats(o, i_)))(st.ap[:, c, :], x_tok[:, i, c * 512:(c + 1) * 512]),
                   reads=[xtb[i]], writes=[st.b])
            op("dve", (lambda o, i_: (lambda e: e.bn_aggr(o, i_)))(mv.ap[:, 0:2], st.ap.rearrange("p a b -> p (a b)")),
               reads=[st.b], writes=[mv.b])
            op("act", ACTF(mv.ap[:, 2:3], mv.ap[:, 1:2], AF.Sqrt, bias=epsT.ap, scale=1.0), reads=[mv.b, epsT.b], writes=[mv.b])
            op("dve", RECIP(mv.ap[:, 2:3], mv.ap[:, 2:3]), reads=[mv.b], writes=[mv.b])
            op("dve", TS(mv.ap[:, 3:4], mv.ap[:, 0:1], mv.ap[:, 2:3], -1.0, ALU.mult, ALU.mult), reads=[mv.b], writes=[mv.b])
            op("act", ACTF(xi, xi, AF.Identity, bias=mv.ap[:, 3:4], scale=mv.ap[:, 2:3]), reads=[xtb[i], mv.b], writes=[xtb[i]])
            op("pool", TT(xi, xi, gB.ap, ALU.mult), reads=[xtb[i], gB.b], writes=[xtb[i]])
            op("pool", TT(xi, xi, bB.ap, ALU.add), reads=[xtb[i], bB.b], writes=[xtb[i]])
        sc.barrier()
        ar.release(m)

    def merge_phase(l):
        m = ar.mark()
        mg = ar.tile([128, 8, S], BF16)
        mgb = [Buf() for _ in range(NG)]
        xT, xTb = new_xT()
        make_xT(xT, xTb)
        wu = [ar.tile([128, 9, 128], BF16) for _ in range(2)]
        gs = [ar.tile([128, 512], F32) for _ in range(2)]
        tm = [ar.tile([128, 512], F32) for _ in range(2)]
        accs = [ar.tile([128, 512], F32) for _ in range(2)]
        urot = Rot([0, 1, 2])
        grot = Rot([3, 4, 5])
        nrows = (WF, WS, WD)
        k = 0
        for fc in range(8):
            wut = wu[fc % 2]
            for mi in range(3):
                for pr in range(3):
                    r0 = pr * 128
                    nr = min(128, nrows[mi] - r0)
                    wload(wut.ap[0:nr, mi * 3 + pr, :], wut.b, wup_d[mi][l, r0:r0 + nr, fc * 128:(fc + 1) * 128],
                          "wu%d" % (fc % 2))
            wg, kg = next_slot()
            for mi in range(3):
                c_ = O_G + mi * D + fc * 128
                wload(wg.ap[:, :, mi * 128:(mi + 1) * 128], wg.b,
                      w_in_d[l][:, c_:c_ + 128].rearrange("(kc p) n -> p kc n", p=128), kg)
            for g in range(NG):
                for mi in range(3):
                    ac = accs[(k // 3) % 2]
                    ub = urot.next()
                    for pr in range(3):
                        nr = min(128, nrows[mi] - pr * 128)
                        op("pe", MM(ub.ap, wut.ap[0:nr, mi * 3 + pr, :], oT[mi][pr].ap[0:nr, g * 512:(g + 1) * 512],
                                    start=(pr == 0), stop=(pr == 2)), reads=[wut.b, oT[mi][pr].b], writes=[ub.b])
                    gb = grot.next()
                    for kc in range(8):
                        op("pe", MM(gb.ap, wg.ap[:, kc, mi * 128:(mi + 1) * 128], xT[:, kc, g * 512:(g + 1) * 512],
                                    start=(kc == 0), stop=(kc == 7)), reads=[wg.b, xTb[g]], writes=[gb.b])
                    gt = gs[k % 2]
                    op("act", ACTF(gt.ap, gb.ap, AF.Sigmoid), reads=[gb.b], writes=[gt.b])
                    if mi == 0:
                        op("dve", TT(ac.ap, ub.ap, gt.ap, ALU.mult), reads=[ub.b, gt.b], writes=[ac.b])
                    else:
                        t_ = tm[k % 2]
                        op("dve", TT(t_.ap, ub.ap, gt.ap, ALU.mult), reads=[ub.b, gt.b], writes=[t_.b])
                        if mi == 1:
                            op("pool", TT(ac.ap, ac.ap, t_.ap, ALU.add), reads=[ac.b, t_.b], writes=[ac.b])
                        else:
                            op("pool", TT(mg.ap[:, fc, g * 512:(g + 1) * 512], ac.ap, t_.ap, ALU.add),
                               reads=[ac.b, t_.b], writes=[mgb[g]])
                    k += 1
        wo = [wtile(wout_d[l][:, c * 512:(c + 1) * 512], 512) for c in range(2)]
        orot = Rot([0, 1, 2, 3, 4, 5])
        for i in range(NT):
            for c in range(2):
                pb = orot.next()
                for kc in range(8):
                    op("pe", MM(pb.ap, mg.ap[:, kc, i * 128:(i + 1) * 128], wo[c].ap[:, kc, :],
                                start=(kc == 0), stop=(kc == 7)), reads=[mgb[i // 4], wo[c].b], writes=[pb.b])
                xs = x_tok[:, i, c * 512:(c + 1) * 512]
                op("dve", STT(xs, xs, ALPHA, pb.ap, ALU.mult, ALU.add), reads=[xtb[i], pb.b], writes=[xtb[i]])
        sc.barrier()
        ar.release(m)

    def ffn_phase(l, sq):
        m = ar.mark()
        sa = scratch(0)
        xT, xTb = new_xT()
        make_xT(xT, xTb)
        m1 = ar.mark()
        pT = ar.tile([128, 2, S], BF16)
        pst = [ar.tile([128, PLE], F32) for _ in range(2)]
        psb = [ar.tile([128, PLE], BF16) for _ in range(2)]
        wp = ar.tile([128, 2, D], BF16)
        sg_ = [ar.tile([128, 512], F32) for _ in range(2)]
        pl = [ar.tile([128, 512], F32) for _ in range(2)]
        rot = Rot([0, 1, 2, 3])
        trot = Rot([4, 5])
        for i in range(NT):
            a, b_ = pst[i % 2], psb[i % 2]
            op("sp", DMA(a.ap, p_d[l, sq, i * 128:(i + 1) * 128, :]), writes=[a.b], dma="p%d" % (i % 2))
            op("pool", CPY(b_.ap, a.ap), reads=[a.b], writes=[b_.b])
            tb = trot.next()
            tbv = tb.ap.bitcast(BF16)
            for c in range(2):
                op("pe", TRN(tbv[:, c * 128:(c + 1) * 128], b_.ap[:, c * 128:(c + 1) * 128], ident_b),
                   reads=[b_.b, cst_b.b], writes=[tb.b])
            op("act", ACPY(pT.ap[:, :, i * 128:(i + 1) * 128], tbv[:, 0:256].rearrange("p (c t) -> p c t", t=128)),
               reads=[tb.b], writes=[pT.b])
        wload(wp.ap, wp.b, wple_d[l].rearrange("(kc p) n -> p kc n", p=128), "wp")
        k = 0
        for cb in range(2):
            wg = wtile(wpg_d[l][:, cb * 512:(cb + 1) * 512], 512)
            for f4 in range(4):
                fc = cb * 4 + f4
                for g in range(NG):
                    gb = rot.next()
                    for kc in range(8):
                        op("pe", MM(gb.ap, wg.ap[:, kc, f4 * 128:(f4 + 1) * 128], xT[:, kc, g * 512:(g + 1) * 512],
                                    start=(kc == 0), stop=(kc == 7)), reads=[wg.b, xTb[g]], writes=[gb.b])
                    pb = rot.next()
                    for kc in range(2):
                        op("pe", MM(pb.ap, wp.ap[:, kc, fc * 128:(fc + 1) * 128], pT.ap[:, kc, g * 512:(g + 1) * 512],
                                    start=(kc == 0), stop=(kc == 1)), reads=[wp.b, pT.b], writes=[pb.b])
                    s_, p_ = sg_[k % 2], pl[k % 2]
                    k += 1
                    op("act", ACTF(s_.ap, gb.ap, AF.Sigmoid), reads=[gb.b], writes=[s_.b])
                    op("dve", TT(p_.ap, pb.ap, s_.ap, ALU.mult), reads=[pb.b, s_.b], writes=[p_.b])
                    tb = trot.next()
                    for s in range(4):
                        op("pe", TRN(tb.ap[:, s * 128:(s + 1) * 128], p_.ap[:, s * 128:(s + 1) * 128], ident_f),
                           reads=[p_.b, cst_f.b], writes=[tb.b])
                    for s in range(4):
                        i = g * 4 + s
                        xs = x_tok[:, i, fc * 128:(fc + 1) * 128]
                        op("dve", STT(xs, xs, ALPHA, tb.ap[:, s * 128:(s + 1) * 128], ALU.mult, ALU.add),
                           reads=[xtb[i], tb.b], writes=[xtb[i]])
        sc.barrier()
        ar.release(m1)
        TG = 512
        hT = ar.tile([128, 32, TG], BF16)
        wfo = [Tile(sa.alloc([128, 32, 128], BF16)) for _ in range(2)]
        rt = [Tile(sa.alloc([128, 512], F32)) for _ in range(2)]
        ot = [Tile(sa.alloc([128, 512], F32)) for _ in range(2)]
        hrot = Rot([0, 1, 2, 3])
        orot = Rot([4, 5])
        trot = Rot([6, 7])
        k = 0
        for tg in range(S // TG):
            t0 = tg * TG
            for w8 in range(8):
                wt = wtile(wfi_d[l][:, w8 * 512:(w8 + 1) * 512], 512)
                for f4 in range(4):
                    ffc = w8 * 4 + f4
                    hb = hrot.next()
                    for kc in range(8):
                        op("pe", MM(hb.ap[:, 0:TG], wt.ap[:, kc, f4 * 128:(f4 + 1) * 128], xT[:, kc, t0:t0 + TG],
                                    start=(kc == 0), stop=(kc == 7)), reads=[wt.b, xTb[t0 // 512]], writes=[hb.b])
                    r_ = rt[k % 2]
                    k += 1
                    op("act", ACTF(r_.ap[:, 0:TG], hb.ap[:, 0:TG], AF.Relu), reads=[hb.b], writes=[r_.b])
                    op("pool", TT(hT.ap[:, ffc, :], r_.ap[:, 0:TG], r_.ap[:, 0:TG], ALU.mult), reads=[r_.b], writes=[hT.b])
            for fc in range(8):
                wo = wfo[fc % 2]
                wload(wo.ap, wo.b, wfo_d[l][:, fc * 128:(fc + 1) * 128].rearrange("(kc p) n -> p kc n", p=128),
                      "wfo%d" % (fc % 2))
                ob = orot.next()
                for ffc in range(32):
                    op("pe", MM(ob.ap[:, 0:TG], wo.ap[:, ffc, :], hT.ap[:, ffc, :], start=(ffc == 0), stop=(ffc == 31)),
                       reads=[wo.b, hT.b], writes=[ob.b])
                o_ = ot[fc % 2]
                op("act", ACPY(o_.ap[:, 0:TG], ob.ap[:, 0:TG]), reads=[ob.b], writes=[o_.b])
                tb = trot.next()
                for s in range(TG // 128):
                    op("pe", TRN(tb.ap[:, s * 128:(s + 1) * 128], o_.ap[:, s * 128:(s + 1) * 128], ident_f),
                       reads=[o_.b, cst_f.b], writes=[tb.b])
                for s in range(TG // 128):
                    i = t0 // 128 + s
                    xs = x_tok[:, i, fc * 128:(fc + 1) * 128]
                    op("dve", TT(xs, xs, tb.ap[:, s * 128:(s + 1) * 128], ALU.add), reads=[xtb[i], tb.b], writes=[xtb[i]])
        sc.barrier()
        ar.release(m)

    stages = STAGES
    for sq in range(NSEQ):
        for i in range(NT):
            op("sp", DMA(x_tok[:, i, :], x_d[sq, i * 128:(i + 1) * 128, :]), writes=[xtb[i]], dma="x%d" % (i % 4))
        for l in range(DEPTH):
            if "fox" in stages:
                fox_phase(l)
            if "sb" in stages:
                sb_phase(l)
            if "dsa" in stages:
                dsa_phase(l)
            if DEBUG and sq == 0 and l == 0:
                for mi in range(3):
                    if ("fox", "sb", "dsa")[mi] not in stages:
                        continue
                    for pr in range(3):
                        op("pool", DMA(dbg_d[mi * 3 + pr], oT[mi][pr].ap), reads=[oT[mi][pr].b], writes=[outb[0]], dma="dbg")
            if "merge" in stages:
                merge_phase(l)
                layer_norm_tiles(l, ln1g_d, ln1b_d)
            if "ffn" in stages:
                ffn_phase(l, sq)
                layer_norm_tiles(l, ln2g_d, ln2b_d)
        for i in range(NT):
            op("sp", DMA(out_d[sq, i * 128:(i + 1) * 128, :], x_tok[:, i, :]), reads=[xtb[i]], writes=[outb[i % 4]],
               dma="o%d" % (i % 4))
    sc.barrier()
    sc.emit()
    stack.close()
    print("sbuf peak", ar.peak, "limit", SB_LIMIT, "ops", {e: len(v) for e, v in sc.ops.items()}, "chans", len(sc.chans))
    return nc


STAGES = ("fox", "sb", "dsa", "merge", "ffn")
DEBUG = False
FOX_STOP = 0


def host_consts(S):
    c = np.zeros((128, 6, 128), np.float32)
    c[:, 0, :] = np.eye(128, dtype=np.float32)
    c[:, 1, :] = 1.0
    kk = np.arange(128)[:, None]
    qq = np.arange(128)[None, :]
    c[:, 2, :] = np.where(kk > qq, -30000.0 * 8.0, 0.0)
    c[:, 3, :] = (qq < kk).astype(np.float32)
    c[:, 4, :] = np.concatenate([c[64:, 0, :], c[:64, 0, :]], axis=0)
    c[:, 5, :] = np.concatenate([c[64:, 2, :], c[:64, 2, :]], axis=0)
    half = HD // 2
    inv = (10000.0 ** (-(np.arange(half, dtype=np.float32) / half))).astype(np.float32)
    ang = np.arange(S, dtype=np.float32)[None, :] * inv[:, None]
    cos = np.cos(ang).astype(np.float32)
    sin = np.sin(ang).astype(np.float32)
    cos2 = np.concatenate([cos, cos, cos, cos], axis=0)
    sin2 = np.concatenate([-sin, sin, -sin, sin], axis=0)
    return c, np.ascontiguousarray(cos2), np.ascontiguousarray(sin2)


def rot_cols(w_in):
    def swap(o, nh):
        idx = []
        for h in range(nh):
            idx += list(range(o + h * 64 + 32, o + h * 64 + 64)) + list(range(o + h * 64, o + h * 64 + 32))
        return idx
    idx = swap(O_DQ, NDSA) + swap(O_DK, 1) + swap(O_IQ, NIDX) + swap(O_IK, 1)
    return np.ascontiguousarray(w_in[:, :, idx])


_CACHE = {}


def kernel(x, p, w_in, b_forget, w_up_fox, w_up_sb, w_up_dsa, w_out, ln1_g, ln1_b,
           w_ff_in, w_ff_out, w_ple, w_ple_gate, ln2_g, ln2_b):
    NC = 8
    B, S, _ = x.shape
    DEPTH = w_in.shape[0]
    NSEQ = B // NC
    NSEL = min(256, S // 4)
    key = (NSEQ, S, DEPTH, NSEL)
    nc = build_program(*key)
    cst, cos2, sin2 = host_consts(S)
    f = lambda a: np.ascontiguousarray(np.asarray(a, dtype=np.float32))
    shared = {
        "w_in": f(w_in), "w_rot": rot_cols(np.asarray(w_in, dtype=np.float32)), "b_forget": f(b_forget),
        "w_up_fox": f(w_up_fox), "w_up_sb": f(w_up_sb), "w_up_dsa": f(w_up_dsa), "w_out": f(w_out),
        "ln1_g": f(ln1_g), "ln1_b": f(ln1_b), "w_ff_in": f(w_ff_in), "w_ff_out": f(w_ff_out),
        "w_ple": f(w_ple), "w_ple_gate": f(w_ple_gate), "ln2_g": f(ln2_g), "ln2_b": f(ln2_b),
        "consts": cst, "ropecos": cos2, "ropesin": sin2,
    }
    x = np.asarray(x, dtype=np.float32)
    p = np.asarray(p, dtype=np.float32)
    in_maps = []
    for c in range(NC):
        d = dict(shared)
        d["x"] = np.ascontiguousarray(x[c * NSEQ:(c + 1) * NSEQ])
        d["p"] = np.ascontiguousarray(p[:, c * NSEQ:(c + 1) * NSEQ])
        in_maps.append(d)
    res = run_bass_kernel_spmd(nc, in_maps, core_ids=list(range(NC)))
    if DEBUG:
        _CACHE["dbg"] = [r["dbg"] for r in res.results]
    return np.concatenate([r["out"] for r in res.results], axis=0)
```

```python
from contextlib import ExitStack
import numpy as np
import concourse.bass as bass
import concourse.mybir as mybir
from concourse.bass_utils import run_bass_kernel_spmd

F32 = mybir.dt.float32
BF16 = mybir.dt.bfloat16
U8 = mybir.dt.uint8
AF = mybir.ActivationFunctionType
ALU = mybir.AluOpType
AXX = mybir.AxisListType.X
NBIS = 22

D = 1024
HD = 64
NFOX, NSB, NDSA, NIDX = 6, 5, 5, 8
WF, WS, WD = 384, 320, 320
PLE = 256
DFF = 4096
C_IN = 6222
NEG = -1.0e30
ALPHA = 4 ** 0.25
LN_EPS = 1e-5
O_FQ, O_FK, O_FV, O_FF = 0, 384, 768, 1152
O_SQ, O_SK, O_SV = 1158, 1478, 1798
O_DQ, O_DK, O_DV = 2118, 2438, 2502
O_IQ, O_IK, O_IW = 2566, 3078, 3142
O_G = 3150
R_DQ, R_DK, R_IQ, R_IK = 0, 320, 384, 896
NROT = 960

SB_BASE = 16640
SB_LIMIT = 229344


class Buf:
    __slots__ = ("w", "rs", "chan")

    def __init__(self):
        self.w = None
        self.rs = {}
        self.chan = None


class Chan:
    def __init__(self, sem):
        self.sem = sem
        self.count = 0


class Sched:
    ENG = ("pe", "act", "dve", "pool", "sp")

    def __init__(self, nc, stack):
        self.nc = nc
        self.stack = stack
        self.ops = {e: [] for e in self.ENG}
        self.cnt = {e: 0 for e in self.ENG}
        self.sem = {e: stack.enter_context(nc.semaphore("s_" + e)) for e in ("pe", "act", "dve", "pool")}
        self.seen = {e: {} for e in self.ENG}
        self.chans = []
        self.chmap = {}

    def chan(self, key):
        if key not in self.chmap:
            c = Chan(self.stack.enter_context(self.nc.semaphore("c%d" % len(self.chans))))
            c.last = None
            self.chans.append(c)
            self.chmap[key] = c
        return self.chmap[key]

    def _wait(self, eng, tok):
        sem, val = tok
        k = id(sem)
        if self.seen[eng].get(k, 0) >= val:
            return
        self.seen[eng][k] = val
        self.ops[eng].append(("w", sem, val))

    def op(self, eng, fn, reads=(), writes=(), dma=None):
        own = self.sem.get(eng)
        deps = []
        if dma is not None:
            ch = self.chan(dma)
            if ch.last is not None and ch.last is not writes[0] and ch.count > 0:
                deps.append((ch.sem, ch.count))
            ch.last = writes[0]
        for b in reads:
            if b.w is not None:
                deps.append(b.w)
        for b in writes:
            if b.w is not None:
                deps.append(b.w)
            for t in b.rs.values():
                if dma is not None or t[0] is not own:
                    deps.append(t)
        for t in deps:
            if dma is None and eng == "pe" and t[0] is own:
                continue
            self._wait(eng, t)
        if dma is None:
            self.cnt[eng] += 1
            tok = (own, self.cnt[eng])
            self.ops[eng].append(("i", fn, own, 1))
        else:
            ch.count += 16
            tok = (ch.sem, ch.count)
            self.ops[eng].append(("i", fn, ch.sem, 16))
        for b in writes:
            b.w = tok
            b.rs = {}
        for b in reads:
            k = id(tok[0])
            if k not in b.rs or b.rs[k][1] < tok[1]:
                b.rs[k] = tok
        return tok

    def barrier(self):
        toks = [(self.sem[e], self.cnt[e]) for e in self.sem if self.cnt[e] > 0]
        toks += [(c.sem, c.count) for c in self.chans if c.count > 0]
        for e in self.ENG:
            for t in toks:
                self._wait(e, t)

    def emit(self):
        with self.nc.Block() as block:
            def mk(eng):
                ops = self.ops[eng]

                def f(e):
                    for o in ops:
                        if o[0] == "w":
                            e.wait_ge(o[1], o[2])
                        else:
                            o[1](e).then_inc(o[2], o[3])
                return f
            block.tensor(mk("pe"))
            block.scalar(mk("act"))
            block.vector(mk("dve"))
            block.gpsimd(mk("pool"))
            block.sync(mk("sp"))


class Tile:
    __slots__ = ("ap", "b")

    def __init__(self, ap, b=None):
        self.ap = ap
        self.b = b if b is not None else Buf()


class Arena:
    cnt = [0]

    def __init__(self, nc, base=SB_BASE, limit=SB_LIMIT):
        self.nc = nc
        self.top = base
        self.base = base
        self.limit = limit
        self.peak = 0

    def alloc(self, shape, dtype, at=None):
        esz = {F32: 4, BF16: 2, U8: 1}[dtype]
        size = esz
        for s in shape[1:]:
            size *= s
        size = (size + 63) // 64 * 64
        if at is None:
            off = self.top
            self.top += size
            self.peak = max(self.peak, self.top)
            assert self.top <= self.limit, ("SBUF overflow", self.top, self.limit)
        else:
            off = at
        Arena.cnt[0] += 1
        h = self.nc.alloc_sbuf_tensor_at("t%d" % Arena.cnt[0], list(shape), dtype, offset=off)
        return h.ap()

    def tile(self, shape, dtype):
        return Tile(self.alloc(shape, dtype))

    def mark(self):
        return self.top

    def release(self, m):
        self.top = m


def MM(out, lhsT, rhs, start=True, stop=True):
    return lambda e: e.matmul(out, lhsT, rhs, start=start, stop=stop)


def TRN(out, in_, ident):
    return lambda e: e.transpose(out, in_, ident)


def ACTF(out, in_, func, bias=0.0, scale=1.0):
    return lambda e: e.activation(out, in_, func, bias=bias, scale=scale)


def TS(out, in0, s1, s2, op0, op1=None):
    if op1 is None:
        return lambda e: e.tensor_scalar(out, in0, s1, None, op0)
    return lambda e: e.tensor_scalar(out, in0, s1, s2, op0, op1)


def TT(out, in0, in1, op):
    return lambda e: e.tensor_tensor(out, in0, in1, op)


def STT(out, in0, scalar, in1, op0, op1):
    return lambda e: e.scalar_tensor_tensor(out, in0, scalar, in1, op0, op1)


def CPY(out, in_):
    return lambda e: e.tensor_copy(out, in_)


def ACPY(out, in_):
    return lambda e: e.activation(out, in_, AF.Copy)


def MSET(ap, v):
    return lambda e: e.memset(ap, v)


def DMA(out, in_):
    return lambda e: e.dma_start(out=out, in_=in_)


def SCAN(out, d0, d1, init):
    return lambda e: e.tensor_tensor_scan(out, d0, d1, init, ALU.mult, ALU.add)


def RECIP(out, in_):
    return lambda e: e.reciprocal(out, in_)


def build_program(NSEQ, S, DEPTH, NSEL):
    NT = S // 128
    NG = S // 512
    nc = bass.Bass("TRN2", target_bir_lowering=False)
    stack = ExitStack()

    def din(name, shape):
        return nc.dram_tensor(name, list(shape), F32, kind="ExternalInput").ap()

    x_d = din("x", [NSEQ, S, D])
    p_d = din("p", [DEPTH, NSEQ, S, PLE])
    w_in_d = din("w_in", [DEPTH, D, C_IN])
    w_rot_d = din("w_rot", [DEPTH, D, NROT])
    bf_d = din("b_forget", [DEPTH, NFOX])
    wup_d = [din("w_up_fox", [DEPTH, WF, D]), din("w_up_sb", [DEPTH, WS, D]), din("w_up_dsa", [DEPTH, WD, D])]
    wout_d = din("w_out", [DEPTH, D, D])
    ln1g_d = din("ln1_g", [DEPTH, D])
    ln1b_d = din("ln1_b", [DEPTH, D])
    wfi_d = din("w_ff_in", [DEPTH, D, DFF])
    wfo_d = din("w_ff_out", [DEPTH, DFF, D])
    wple_d = din("w_ple", [DEPTH, PLE, D])
    wpg_d = din("w_ple_gate", [DEPTH, D, D])
    ln2g_d = din("ln2_g", [DEPTH, D])
    ln2b_d = din("ln2_b", [DEPTH, D])
    cst_d = din("consts", [128, 6, 128])
    cos_d = din("ropecos", [128, S])
    sin_d = din("ropesin", [128, S])
    out_d = nc.dram_tensor("out", [NSEQ, S, D], F32, kind="ExternalOutput").ap()
    dbg_d = nc.dram_tensor("dbg", [9, 128, S], F32, kind="ExternalOutput").ap() if DEBUG else None

    sc = Sched(nc, stack)
    ar = Arena(nc)
    psa = nc.alloc_psum_tensor("ps", [128, 8, 512], F32).ap()
    bank = [Tile(psa[:, k, :]) for k in range(8)]

    class Rot:
        def __init__(self, ids):
            self.ids = ids
            self.i = 0

        def next(self):
            t = bank[self.ids[self.i % len(self.ids)]]
            self.i += 1
            return t

    op = sc.op

    x_tok = ar.alloc([128, NT, D], F32)
    xtb = [Buf() for _ in range(NT)]
    oT_base = ar.mark()
    oT = [[ar.tile([128, S], BF16) for _ in range(3)] for _ in range(3)]
    oT_sz = (ar.mark() - oT_base) // 3

    def scratch(first_free_mixer):
        if S < 2048:
            return ar
        return Arena(nc, oT_base + first_free_mixer * oT_sz, oT_base + 3 * oT_sz)

    cst_f = ar.tile([128, 6, 128], F32)
    cst_b = ar.tile([128, 6, 128], BF16)
    identS_b, fmaskS_b = cst_b.ap[:, 4, :], cst_b.ap[:, 5, :]
    ident_b, ones_b, fmask_b, tri_b = (cst_b.ap[:, i, :] for i in range(4))
    ident_f, ones_f, tri_f = cst_f.ap[:, 0, :], cst_f.ap[:, 1, :], cst_f.ap[:, 3, :]
    wslot = [ar.tile([128, 8, 512], BF16) for _ in range(2)]
    wsl_i = [0]
    small = ar.tile([128, 16], F32)
    epsT = ar.tile([128, 1], F32)
    outb = [Buf() for _ in range(4)]

    op("sp", DMA(cst_f.ap, cst_d), writes=[cst_f.b], dma="cst")
    op("dve", CPY(cst_b.ap, cst_f.ap), reads=[cst_f.b], writes=[cst_b.b])
    op("pool", MSET(epsT.ap, LN_EPS), writes=[epsT.b])
    pw2 = ar.tile([128, 32], F32)
    for t in range(32):
        op("pool", MSET(pw2.ap[:, t:t + 1], 2.0 ** (-(t + 1))), writes=[pw2.b])
    if DEBUG:
        for mi in range(3):
            for pr in range(3):
                op("pool", MSET(oT[mi][pr].ap, 0.0), writes=[oT[mi][pr].b])

    def ones_bc(npart, n):
        return cst_f.ap[0:npart, 1, 0:1].to_broadcast([npart, n])

    def wload(dst_ap, dst_b, src_ap, key):
        op("pool", DMA(dst_ap, src_ap), writes=[dst_b], dma=key)

    def next_slot():
        k = wsl_i[0] % 2
        wsl_i[0] += 1
        return wslot[k], "w%d" % k

    def wtile(src2d, ncols, rows=D):
        t, key = next_slot()
        kc = rows // 128
        wload(t.ap[:, 0:kc, 0:ncols], t.b, src2d.rearrange("(kc p) n -> p kc n", p=128), key)
        return t

    def make_xT(xT, xTb):
        m = ar.mark()
        xb = [ar.tile([128, D], BF16) for _ in range(2)]
        rot = Rot([0, 1, 2, 3])
        for i in range(NT):
            t = xb[i % 2]
            op("pool", CPY(t.ap, x_tok[:, i, :]), reads=[xtb[i]], writes=[t.b])
            pb = rot.next()
            pbv = pb.ap.bitcast(BF16)
            for c in range(8):
                op("pe", TRN(pbv[:, c * 128:(c + 1) * 128], t.ap[:, c * 128:(c + 1) * 128], ident_b),
                   reads=[t.b, cst_b.b], writes=[pb.b])
            op("act", ACPY(xT[:, :, i * 128:(i + 1) * 128], pbv.rearrange("p (c t) -> p c t", c=8)),
               reads=[pb.b], writes=[xTb[i // 4]])
        sc.barrier()
        ar.release(m)

    def new_xT():
        xT = ar.alloc([128, 8, S], BF16)
        xTb = [Buf() for _ in range(NG)]
        return xT, xTb

    def proj_fm(xT, xTb, wt, col0, ncols, dst_ap, dst_b, rot, evac="act"):
        for g in range(NG):
            pb = rot.next()
            for kc in range(8):
                op("pe", MM(pb.ap[0:ncols, :], wt.ap[:, kc, col0:col0 + ncols], xT[:, kc, g * 512:(g + 1) * 512],
                            start=(kc == 0), stop=(kc == 7)), reads=[wt.b, xTb[g]], writes=[pb.b])
            if evac == "act":
                op("act", ACPY(dst_ap[0:ncols, g * 512:(g + 1) * 512], pb.ap[0:ncols, :]), reads=[pb.b], writes=[dst_b])
            else:
                op("dve", CPY(dst_ap[0:ncols, g * 512:(g + 1) * 512], pb.ap[0:ncols, :]), reads=[pb.b], writes=[dst_b])

    def proj_rope(xT, xTb, wa, wb, ca, cb, ncols, dst_ap, dst_b, rot, ct):
        for g in range(NG):
            op("sp", DMA(ct.ap[:, 0, :], cos_d[:, g * 512:(g + 1) * 512]), writes=[ct.b], dma="cs")
            op("sp", DMA(ct.ap[:, 1, :], sin_d[:, g * 512:(g + 1) * 512]), writes=[ct.b], dma="cs")
            pa = rot.next()
            pb = rot.next()
            for kc in range(8):
                op("pe", MM(pa.ap[0:ncols, :], wa.ap[:, kc, ca:ca + ncols], xT[:, kc, g * 512:(g + 1) * 512],
                            start=(kc == 0), stop=(kc == 7)), reads=[wa.b, xTb[g]], writes=[pa.b])
            for kc in range(8):
                op("pe", MM(pb.ap[0:ncols, :], wb.ap[:, kc, cb:cb + ncols], xT[:, kc, g * 512:(g + 1) * 512],
                            start=(kc == 0), stop=(kc == 7)), reads=[wb.b, xTb[g]], writes=[pb.b])
            t1 = ct.ap[0:ncols, 2, :]
            t2 = ct.ap[0:ncols, 3, :]
            op("dve", TT(t1, pa.ap[0:ncols, :], ct.ap[0:ncols, 0, :], ALU.mult), reads=[pa.b, ct.b], writes=[ct.b])
            op("dve", TT(t2, pb.ap[0:ncols, :], ct.ap[0:ncols, 1, :], ALU.mult), reads=[pb.b, ct.b], writes=[ct.b])
            op("pool", TT(dst_ap[0:ncols, g * 512:(g + 1) * 512], t1, t2, ALU.add), reads=[ct.b], writes=[dst_b])

    def softmax_norm(acc, den_row, bp, dst, g, rd, tmp, bcb):
        r = den_row
        op("dve", RECIP(rd.ap[r:r + 1, :], acc.ap[r:r + 1, :]), reads=[acc.b], writes=[rd.b])
        op("pe", MM(bcb.ap, ones_f[r:r + 1, :], rd.ap[r:r + 1, :]), reads=[rd.b, cst_f.b], writes=[bcb.b])
        op("act", ACPY(tmp.ap[bp:bp + 64, :], bcb.ap[bp:bp + 64, :]), reads=[bcb.b], writes=[tmp.b])
        op("dve", TT(dst.ap[bp:bp + 64, g * 512:(g + 1) * 512], acc.ap[bp:bp + 64, :], tmp.ap[bp:bp + 64, :], ALU.mult),
           reads=[acc.b, tmp.b], writes=[dst.b])

    def fox_phase(l):
        m = ar.mark()
        sa = scratch(1)
        qT = [ar.tile([128, S], BF16) for _ in range(3)]
        kT = [ar.tile([128, S], BF16) for _ in range(3)]
        V = ar.tile([128, NT, 3 * 192], BF16)
        CQ = [Tile(sa.alloc([128, S], BF16)) for _ in range(3)]
        negc = ar.tile([128, NT * NFOX], F32)
        bneg = ar.tile([128, 1], F32)
        ft = [Tile(sa.alloc([8, 512], F32)) for _ in range(3)]
        c3 = Tile(sa.alloc([8, 3, 512], BF16))
        m2 = ar.mark()
        xT, xTb = new_xT()
        make_xT(xT, xTb)
        rot = Rot([0, 1, 2, 3])
        w_l = w_in_d[l]
        if FOX_STOP == 1:
            sc.barrier(); ar.release(m); return
        for cq_ in CQ:
            op("pool", MSET(cq_.ap, 0.0), writes=[cq_.b])
        op("pool", MSET(V.ap, 0.0), writes=[V.b])
        for j in range(3):
            op("pool", MSET(V.ap[:, :, j * 192 + 64:j * 192 + 65], 1.0), writes=[V.b])
        op("sp", DMA(bneg.ap[0:NFOX, :], bf_d[l].rearrange("(h o) -> h o", o=1)), writes=[bneg.b], dma="bneg")
        op("dve", TS(bneg.ap[0:NFOX, :], bneg.ap[0:NFOX, :], -1.0, None, ALU.mult), reads=[bneg.b], writes=[bneg.b])
        wq = wtile(w_l[:, O_FQ:O_FQ + 384], 384)
        for j in range(3):
            proj_fm(xT, xTb, wq, j * 128, 128, qT[j].ap, qT[j].b, rot)
        wk = wtile(w_l[:, O_FK:O_FK + 384], 384)
        for j in range(3):
            proj_fm(xT, xTb, wk, j * 128, 128, kT[j].ap, kT[j].b, rot, evac="dve")
        wv = wtile(w_l[:, O_FV:O_FV + 390], 390)
        for i in range(NT):
            pb = rot.next()
            for kc in range(8):
                op("pe", MM(pb.ap[:, 0:384], xT[:, kc, i * 128:(i + 1) * 128], wv.ap[:, kc, 0:384],
                            start=(kc == 0), stop=(kc == 7)), reads=[wv.b, xTb[i // 4]], writes=[pb.b])
            src = pb.ap[:, 0:384].rearrange("p (j e d) -> p j e d", j=3, e=2)
            dstv = V.ap[:, i, :].rearrange("p (j c) -> p j c", c=192)
            op("act", ACPY(dstv[:, :, 0:64], src[:, :, 0, :]), reads=[pb.b], writes=[V.b])
            op("dve", CPY(dstv[:, :, 128:192], src[:, :, 1, :]), reads=[pb.b], writes=[V.b])
        if FOX_STOP == 2:
            sc.barrier(); ar.release(m); return
        e_t, n_t, r_t = ft
        for g in range(NG):
            pb = rot.next()
            for kc in range(8):
                op("pe", MM(pb.ap[0:NFOX, :], wv.ap[:, kc, 384:390], xT[:, kc, g * 512:(g + 1) * 512],
                            start=(kc == 0), stop=(kc == 7)), reads=[wv.b, xTb[g]], writes=[pb.b])
            op("act", ACTF(e_t.ap[0:NFOX, :], pb.ap[0:NFOX, :], AF.Exp, bias=bneg.ap[0:NFOX, :], scale=-1.0),
               reads=[pb.b, bneg.b], writes=[e_t.b])
            op("act", ACTF(e_t.ap[0:NFOX, :], e_t.ap[0:NFOX, :], AF.Ln, bias=1.0, scale=1.0), reads=[e_t.b], writes=[e_t.b])
            init = 0.0 if g == 0 else small.ap[0:NFOX, 0:1]
            op("dve", SCAN(n_t.ap[0:NFOX, :], ones_bc(NFOX, 512), e_t.ap[0:NFOX, :], init),
               reads=[e_t.b, small.b, cst_f.b], writes=[n_t.b])
            op("dve", CPY(small.ap[0:NFOX, 0:1], n_t.ap[0:NFOX, 511:512]), reads=[n_t.b], writes=[small.b])
            for s in range(4):
                i = g * 4 + s
                tb = rot.next()
                op("pe", MM(tb.ap[:, 0:NFOX], n_t.ap[0:NFOX, s * 128:(s + 1) * 128], ident_f[0:NFOX, 0:NFOX]),
                   reads=[n_t.b, cst_f.b], writes=[tb.b])
                op("act", ACPY(negc.ap[:, i * NFOX:(i + 1) * NFOX], tb.ap[:, 0:NFOX]), reads=[tb.b], writes=[negc.b])
            op("dve", TS(r_t.ap[0:NFOX, :], n_t.ap[0:NFOX, :], -8.0, None, ALU.mult), reads=[n_t.b], writes=[r_t.b])
            for k3 in range(3):
                op("dve", CPY(c3.ap[0:NFOX, k3, :], r_t.ap[0:NFOX, :]), reads=[r_t.b], writes=[c3.b])
                if k3 < 2:
                    op("dve", TT(r_t.ap[0:NFOX, :], r_t.ap[0:NFOX, :], c3.ap[0:NFOX, k3, :], ALU.subtract),
                       reads=[r_t.b, c3.b], writes=[r_t.b])
            for h in range(NFOX):
                cq = CQ[h // 2]
                rb = 64 * (h % 2)
                for k3 in range(3):
                    op("sp", DMA(cq.ap[rb + k3:rb + k3 + 1, g * 512:(g + 1) * 512], c3.ap[h:h + 1, k3, :]),
                       reads=[c3.b], writes=[cq.b], dma="cq%d" % (h // 2))
        sc.barrier()
        ar.release(m2)
        if FOX_STOP == 3:
            ar.release(m); return
        PT = [ar.tile([128, 512], BF16) for _ in range(3)]
        rd = ar.tile([128, 512], F32)
        tmp = ar.tile([128, 512], F32)
        srot = Rot([0, 1, 2, 3])
        arot = Rot([4, 5, 6])
        bcb = bank[7]
        pti = 0
        for h in range(NFOX):
            j, bp = h // 2, 64 * (h % 2)
            cq = CQ[j]
            lv = (j * 192, j * 192 + 65) if bp == 0 else (j * 192 + 64, j * 192 + 192)
            den_row = 64 if bp == 0 else 0
            for g in range(NG):
                acc = arot.next()
                nkb = 4 * g + 4
                for kb in range(nkb):
                    c0 = 0 if kb < 4 * g else 128 * (kb - 4 * g)
                    diag = kb >= 4 * g
                    sp_ = srot.next()
                    q0 = g * 512 + c0
                    op("pe", MM(sp_.ap[:, c0:512], kT[j].ap[bp:bp + 64, kb * 128:(kb + 1) * 128],
                                qT[j].ap[bp:bp + 64, q0:(g + 1) * 512], start=True, stop=False),
                       reads=[kT[j].b, qT[j].b], writes=[sp_.b])
                    op("pe", MM(sp_.ap[:, c0:512], ones_b[bp:bp + 64, :], cq.ap[bp:bp + 64, q0:(g + 1) * 512],
                                start=False, stop=not diag), reads=[cq.b, cst_b.b], writes=[sp_.b])
                    if diag:
                        op("pe", MM(sp_.ap[:, c0:c0 + 128], ident_b[bp:bp + 64, :], fmask_b[bp:bp + 64, :],
                                    start=False, stop=False), reads=[cst_b.b], writes=[sp_.b])
                        op("pe", MM(sp_.ap[:, c0:c0 + 128], identS_b[bp:bp + 64, :], fmaskS_b[bp:bp + 64, :],
                                    start=False, stop=True), reads=[cst_b.b], writes=[sp_.b])
                    pt = PT[pti % 3]
                    pti += 1
                    op("act", ACTF(pt.ap[:, c0:512], sp_.ap[:, c0:512], AF.Exp,
                                   bias=negc.ap[:, kb * NFOX + h:kb * NFOX + h + 1], scale=0.125),
                       reads=[sp_.b, negc.b], writes=[pt.b])
                    op("pe", MM(acc.ap[:, c0:512] if bp else acc.ap[0:65, c0:512], V.ap[:, kb, lv[0]:lv[1]], pt.ap[:, c0:512],
                                start=(kb == 0), stop=(kb == nkb - 1)), reads=[V.b, pt.b], writes=[acc.b])
                softmax_norm(acc, den_row, bp, oT[0][j], g, rd, tmp, bcb)
        sc.barrier()
        ar.release(m)

    def sb_phase(l):
        m = ar.mark()
        qT = [ar.tile([128, S], BF16) for _ in range(3)]
        kT = [ar.tile([128, S], BF16) for _ in range(3)]
        V = ar.tile([128, NT, 384], BF16)
        m2 = ar.mark()
        xT, xTb = new_xT()
        make_xT(xT, xTb)
        rot = Rot([0, 1, 2, 3])
        w_l = w_in_d[l]
        op("pool", MSET(V.ap, 0.0), writes=[V.b])
        wq = wtile(w_l[:, O_SQ:O_SQ + 320], 320)
        for j in range(3):
            proj_fm(xT, xTb, wq, j * 128, 128 if j < 2 else 64, qT[j].ap, qT[j].b, rot)
        wk = wtile(w_l[:, O_SK:O_SK + 320], 320)
        for j in range(3):
            proj_fm(xT, xTb, wk, j * 128, 128 if j < 2 else 64, kT[j].ap, kT[j].b, rot, evac="dve")
        wv = wtile(w_l[:, O_SV:O_SV + 320], 320)
        for i in range(NT):
            pb = rot.next()
            for kc in range(8):
                op("pe", MM(pb.ap[:, 0:320], xT[:, kc, i * 128:(i + 1) * 128], wv.ap[:, kc, 0:320],
                            start=(kc == 0), stop=(kc == 7)), reads=[wv.b, xTb[i // 4]], writes=[pb.b])
            op("act", ACPY(V.ap[:, i, 0:320], pb.ap[:, 0:320]), reads=[pb.b], writes=[V.b])
        sc.barrier()
        ar.release(m2)
        NB = 2
        SPt = [ar.tile([128, S], F32) for _ in range(NB)]
        PXt = [ar.tile([128, S], F32) for _ in range(NB)]
        At = [ar.tile([128, S], BF16) for _ in range(NB)]
        ATt = [ar.tile([128, NT, 128], BF16) for _ in range(NB)]
        ntot = [ar.tile([128, 1], F32) for _ in range(NB)]
        trot = Rot([4, 5])
        arot = Rot([6, 7])
        it = 0
        for h in range(NSB):
            j, bp = h // 2, 64 * (h % 2)
            for i in range(NT):
                nk = 128 * (i + 1)
                sp_, px, a_, at_, nt_ = SPt[it % NB], PXt[it % NB], At[it % NB], ATt[it % NB], ntot[it % NB]
                it += 1
                nb = (nk + 511) // 512
                b0 = 2 * (it % 2) if nb <= 2 else 0
                for c in range(nb):
                    n = min(512, nk - c * 512)
                    zb = bank[b0 + c]
                    op("pe", MM(zb.ap[:, 0:n], qT[j].ap[bp:bp + 64, i * 128:(i + 1) * 128],
                                kT[j].ap[bp:bp + 64, c * 512:c * 512 + n]), reads=[qT[j].b, kT[j].b], writes=[zb.b])
                zbs = [bank[b0 + c].b for c in range(nb)]
                zall = psa[:, b0:b0 + nb, :].rearrange("p c n -> p (c n)")[:, 0:nk]
                op("act", ACTF(sp_.ap[:, 0:nk], zall, AF.Exp, scale=0.125), reads=zbs, writes=[sp_.b])
                op("act", ACTF(sp_.ap[:, 0:nk], sp_.ap[:, 0:nk], AF.Ln, bias=1.0), reads=[sp_.b], writes=[sp_.b])
                op("pool", TT(sp_.ap[:, nk - 128:nk], sp_.ap[:, nk - 128:nk], tri_f, ALU.mult),
                   reads=[sp_.b, cst_f.b], writes=[sp_.b])
                op("dve", MSET(px.ap[:, 0:1], 0.0), writes=[px.b])
                op("dve", SCAN(px.ap[:, 1:nk], ones_bc(128, nk - 1), sp_.ap[:, 0:nk - 1], 0.0),
                   reads=[sp_.b, cst_f.b, px.b], writes=[px.b])
                op("dve", TS(nt_.ap, px.ap[:, nk - 1:nk], -1.0, None, ALU.mult), reads=[px.b], writes=[nt_.b])
                op("dve", STT(sp_.ap[:, 0:nk], zall, 0.125, px.ap[:, 0:nk], ALU.mult, ALU.add),
                   reads=zbs + [px.b], writes=[sp_.b])
                op("act", ACTF(a_.ap[:, 0:nk], sp_.ap[:, 0:nk], AF.Exp, bias=nt_.ap, scale=1.0),
                   reads=[sp_.b, nt_.b], writes=[a_.b])
                op("pool", TT(a_.ap[:, nk - 128:nk], a_.ap[:, nk - 128:nk], tri_b, ALU.mult),
                   reads=[a_.b, cst_b.b], writes=[a_.b])
                for c8 in range(0, i + 1, 8):
                    n8 = min(8, i + 1 - c8)
                    tb = trot.next()
                    tbv = tb.ap.bitcast(BF16)
                    for kb in range(c8, c8 + n8):
                        op("pe", TRN(tbv[:, (kb - c8) * 128:(kb - c8 + 1) * 128], a_.ap[:, kb * 128:(kb + 1) * 128], ident_b),
                           reads=[a_.b, cst_b.b], writes=[tb.b])
                    op("act", ACPY(at_.ap[:, c8:c8 + n8, :], tbv[:, 0:n8 * 128].rearrange("p (c t) -> p c t", t=128)),
                       reads=[tb.b], writes=[at_.b])
                acc = arot.next()
                lo, hi = (j * 128, j * 128 + 64) if bp == 0 else (j * 128, j * 128 + 128)
                for kb in range(i + 1):
                    op("pe", MM(acc.ap[0:hi - lo, 0:128], V.ap[:, kb, lo:hi], at_.ap[:, kb, :],
                                start=(kb == 0), stop=(kb == i)), reads=[V.b, at_.b], writes=[acc.b])
                op("dve", CPY(oT[1][j].ap[bp:bp + 64, i * 128:(i + 1) * 128], acc.ap[bp:bp + 64, 0:128]),
                   reads=[acc.b], writes=[oT[1][j].b])
        sc.barrier()
        ar.release(m)

    def dsa_phase(l):
        m = ar.mark()
        sa = scratch(2)
        dqT = [ar.tile([128, S], BF16) for _ in range(3)]
        kkd = ar.tile([128, S], BF16)
        kki = ar.tile([128, S], BF16)
        iqT = [ar.tile([128, S], BF16) for _ in range(4)]
        V = ar.tile([128, NT, 192], BF16)
        iw = ar.tile([128, NT, 8], F32)
        aw = ar.tile([128, NT, 8], F32)
        sg = ar.tile([128, NT, 8], F32)
        m2 = ar.mark()
        xT, xTb = new_xT()
        make_xT(xT, xTb)
        ct = Tile(sa.alloc([128, 4, 512], F32))
        rot = Rot([0, 1, 2, 3, 4, 5])
        w_l = w_in_d[l]
        r_l = w_rot_d[l]
        op("pool", MSET(V.ap, 0.0), writes=[V.b])
        op("pool", MSET(V.ap[:, :, 64:65], 1.0), writes=[V.b])
        wa = wtile(w_l[:, O_DQ:O_DQ + 320], 320)
        wb = wtile(r_l[:, R_DQ:R_DQ + 320], 320)
        for j in range(3):
            proj_rope(xT, xTb, wa, wb, j * 128, j * 128, 128 if j < 2 else 64, dqT[j].ap, dqT[j].b, rot, ct)
        wa = wtile(w_l[:, O_IQ:O_IQ + 512], 512)
        wb = wtile(r_l[:, R_IQ:R_IQ + 512], 512)
        for j in range(4):
            proj_rope(xT, xTb, wa, wb, j * 128, j * 128, 128, iqT[j].ap, iqT[j].b, rot, ct)
        wa, ka = next_slot()
        for q_, o_ in enumerate((O_DK, O_DK, O_IK, O_IK)):
            wload(wa.ap[:, :, q_ * 64:(q_ + 1) * 64], wa.b, w_l[:, o_:o_ + 64].rearrange("(kc p) n -> p kc n", p=128), ka)
        wb, kb_ = next_slot()
        for q_, o_ in enumerate((R_DK, R_DK, R_IK, R_IK)):
            wload(wb.ap[:, :, q_ * 64:(q_ + 1) * 64], wb.b, r_l[:, o_:o_ + 64].rearrange("(kc p) n -> p kc n", p=128), kb_)
        proj_rope(xT, xTb, wa, wb, 0, 0, 128, kkd.ap, kkd.b, rot, ct)
        proj_rope(xT, xTb, wa, wb, 128, 128, 128, kki.ap, kki.b, rot, ct)
        wv, kv = next_slot()
        wload(wv.ap[:, :, 0:64], wv.b, w_l[:, O_DV:O_DV + 64].rearrange("(kc p) n -> p kc n", p=128), kv)
        wload(wv.ap[:, :, 64:72], wv.b, w_l[:, O_IW:O_IW + 8].rearrange("(kc p) n -> p kc n", p=128), kv)
        for i in range(NT):
            pb = rot.next()
            for kc in range(8):
                op("pe", MM(pb.ap[:, 0:72], xT[:, kc, i * 128:(i + 1) * 128], wv.ap[:, kc, 0:72],
                            start=(kc == 0), stop=(kc == 7)), reads=[wv.b, xTb[i // 4]], writes=[pb.b])
            op("act", ACPY(V.ap[:, i, 0:64], pb.ap[:, 0:64]), reads=[pb.b], writes=[V.b])
            op("act", ACPY(V.ap[:, i, 128:192], pb.ap[:, 0:64]), reads=[pb.b], writes=[V.b])
            op("dve", CPY(iw.ap[:, i, :], pb.ap[:, 64:72]), reads=[pb.b], writes=[iw.b])
        op("act", ACTF(aw.ap, iw.ap, AF.Abs), reads=[iw.b], writes=[aw.b])
        op("act", ACTF(sg.ap, iw.ap, AF.Sign), reads=[iw.b], writes=[sg.b])
        sc.barrier()
        ar.release(m2)
        score = Tile(wslot[0].ap.rearrange("p a b -> p (a b)").bitcast(F32), wslot[0].b)
        work = Tile(wslot[1].ap.rearrange("p a b -> p (a b)").bitcast(F32), wslot[1].b)
        rl = [ar.tile([128, 512], F32) for _ in range(2)]
        msk = ar.tile([128, S], BF16)
        mT = ar.tile([128, NT, 512], BF16)
        bs = ar.tile([128, 8], F32)
        wt = ar.tile([128, 32], F32)
        thr = ar.tile([128, 1], F32)
        ET = [ar.tile([128, 512], BF16) for _ in range(2)]
        PT = [ar.tile([128, 512], BF16) for _ in range(2)]
        rd = ar.tile([128, 512], F32)
        tmp = ar.tile([128, 512], F32)
        trot = Rot([4])
        srot = Rot([5, 6])
        bcb = bank[4]
        ri = 0
        ei = 0
        for g in range(NG):
            for s in range(4):
                i = g * 4 + s
                nk = 128 * (i + 1)
                nb = (nk + 511) // 512
                for h in range(NIDX):
                    j, bp = h // 2, 64 * (h % 2)
                    lb0 = 2 * (h % 2) if nb <= 2 else 0
                    for c in range(nb):
                        n = min(512, nk - c * 512)
                        lb = bank[lb0 + c]
                        op("pe", MM(lb.ap[:, 0:n], iqT[j].ap[bp:bp + 64, i * 128:(i + 1) * 128],
                                    kki.ap[bp:bp + 64, c * 512:c * 512 + n]), reads=[iqT[j].b, kki.b], writes=[lb.b])
                        r_ = rl[ri % 2]
                        ri += 1
                        op("act", ACTF(r_.ap[:, 0:n], lb.ap[:, 0:n], AF.Relu, scale=aw.ap[:, i, h:h + 1]),
                           reads=[lb.b, aw.b], writes=[r_.b])
                        dst = score.ap[:, c * 512:c * 512 + n]
                        if h == 0:
                            op("dve", TS(dst, r_.ap[:, 0:n], sg.ap[:, i, h:h + 1], None, ALU.mult),
                               reads=[r_.b, sg.b], writes=[score.b])
                        else:
                            op("dve", STT(dst, r_.ap[:, 0:n], sg.ap[:, i, h:h + 1], dst, ALU.mult, ALU.add),
                               reads=[r_.b, sg.b, score.b], writes=[score.b])
                if nk > NSEL:
                    sv = score.ap[:, 0:nk]
                    op("dve", (lambda o, i_: (lambda e: e.reduce_max(out=o, in_=i_, axis=AXX)))(bs.ap[:, 0:1], sv),
                       reads=[score.b], writes=[bs.b])
                    op("dve", (lambda o, i_: (lambda e: e.tensor_reduce(out=o, in_=i_, axis=AXX, op=ALU.min)))(bs.ap[:, 1:2], sv),
                       reads=[score.b], writes=[bs.b])
                    op("dve", TS(bs.ap[:, 2:3], bs.ap[:, 0:1], bs.ap[:, 1:2], 1.001, ALU.subtract, ALU.mult),
                       reads=[bs.b], writes=[bs.b])
                    op("dve", TS(bs.ap[:, 3:4], bs.ap[:, 0:1], bs.ap[:, 1:2], 0.5, ALU.add, ALU.mult),
                       reads=[bs.b], writes=[bs.b])
                    op("dve", TS(wt.ap, pw2.ap, bs.ap[:, 2:3], None, ALU.mult), reads=[bs.b, pw2.b], writes=[wt.b])
                    op("dve", MSET(score.ap[0:64, nk - 64:nk], NEG), writes=[score.b])
                    for t in range(NBIS):
                        op("dve", (lambda o, i_, m_, c_: (lambda e: e.tensor_scalar(o, i_, m_, None, ALU.is_ge, ALU.add, accum_out=c_)))(
                            msk.ap[:, 0:nk], sv, bs.ap[:, 3:4], bs.ap[:, 4:5]), reads=[score.b, bs.b], writes=[msk.b, bs.b])
                        op("dve", TS(bs.ap[:, 5:6], bs.ap[:, 4:5], NSEL - 0.5, 0.5, ALU.is_ge, ALU.subtract),
                           reads=[bs.b], writes=[bs.b])
                        op("dve", STT(bs.ap[:, 3:4], bs.ap[:, 5:6], wt.ap[:, t:t + 1], bs.ap[:, 3:4], ALU.mult, ALU.add),
                           reads=[bs.b, wt.b], writes=[bs.b])
                    op("dve", TT(thr.ap, bs.ap[:, 3:4], wt.ap[:, NBIS:NBIS + 1], ALU.subtract), reads=[bs.b, wt.b], writes=[thr.b])
                else:
                    op("dve", MSET(score.ap[0:64, nk - 64:nk], NEG), writes=[score.b])
                    op("dve", MSET(thr.ap, -1.0e29), writes=[thr.b])
                op("dve", TS(msk.ap[:, 0:nk], score.ap[:, 0:nk], thr.ap, None, ALU.is_ge),
                   reads=[score.b, thr.b], writes=[msk.b])
                for c8 in range(0, i + 1, 8):
                    n8 = min(8, i + 1 - c8)
                    tb = trot.next()
                    tbv = tb.ap.bitcast(BF16)
                    for kb in range(c8, c8 + n8):
                        op("pe", TRN(tbv[:, (kb - c8) * 128:(kb - c8 + 1) * 128], msk.ap[:, kb * 128:(kb + 1) * 128], ident_b),
                           reads=[msk.b, cst_b.b], writes=[tb.b])
                    op("act", ACPY(mT.ap[:, c8:c8 + n8, s * 128:(s + 1) * 128],
                                   tbv[:, 0:n8 * 128].rearrange("p (c t) -> p c t", t=128)), reads=[tb.b], writes=[mT.b])
            for h in range(NDSA):
                j, bp = h // 2, 64 * (h % 2)
                lv = (0, 65) if bp == 0 else (64, 192)
                den_row = 64 if bp == 0 else 0
                acc = bank[7]
                nkb = 4 * g + 4
                for kb in range(nkb):
                    c0 = 0 if kb < 4 * g else 128 * (kb - 4 * g)
                    sp_ = srot.next()
                    q0 = g * 512 + c0
                    op("pe", MM(sp_.ap[:, c0:512], kkd.ap[bp:bp + 64, kb * 128:(kb + 1) * 128],
                                dqT[j].ap[bp:bp + 64, q0:(g + 1) * 512]), reads=[kkd.b, dqT[j].b], writes=[sp_.b])
                    et = ET[ei % 2]
                    pt = PT[ei % 2]
                    ei += 1
                    op("act", ACTF(et.ap[:, c0:512], sp_.ap[:, c0:512], AF.Exp, scale=0.125), reads=[sp_.b], writes=[et.b])
                    op("pool", TT(pt.ap[:, c0:512], et.ap[:, c0:512], mT.ap[:, kb, c0:512], ALU.mult),
                       reads=[et.b, mT.b], writes=[pt.b])
                    op("pe", MM(acc.ap[:, c0:512] if bp else acc.ap[0:65, c0:512], V.ap[:, kb, lv[0]:lv[1]], pt.ap[:, c0:512],
                                start=(kb == 0), stop=(kb == nkb - 1)), reads=[V.b, pt.b], writes=[acc.b])
                softmax_norm(acc, den_row, bp, oT[2][j], g, rd, tmp, bcb)
        sc.barrier()
        ar.release(m)

    def layer_norm_tiles(l, g_d, b_d):
        m = ar.mark()
        gB = ar.tile([128, D], F32)
        bB = ar.tile([128, D], F32)
        st = ar.tile([128, 2, 6], F32)
        mv = ar.tile([128, 4], F32)
        op("sp", DMA(gB.ap, g_d[l:l + 1, :].to_broadcast([128, D])), writes=[gB.b], dma="lng")
        op("sp", DMA(bB.ap, b_d[l:l + 1, :].to_broadcast([128, D])), writes=[bB.b], dma="lnb")
        for i in range(NT):
            xi = x_tok[:, i, :]
            for c in range(2):
                op("dve", (lambda o, i_: (lambda e: e.bn_stats(o, i_)))(st.ap[:, c, :], x_tok[:, i, c * 512:(c + 1) * 512]),
                   reads=[xtb[i]], writes=[st.b])
            op("dve", (lambda o, i_: (lambda e: e.bn_aggr(o, i_)))(mv.ap[:, 0:2], st.ap.rearrange("p a b -> p (a b)")),
               reads=[st.b], writes=[mv.b])
            op("act", ACTF(mv.ap[:, 2:3], mv.ap[:, 1:2], AF.Sqrt, bias=epsT.ap, scale=1.0), reads=[mv.b, epsT.b], writes=[mv.b])
            op("dve", RECIP(mv.ap[:, 2:3], mv.ap[:, 2:3]), reads=[mv.b], writes=[mv.b])
            op("dve", TS(mv.ap[:, 3:4], mv.ap[:, 0:1], mv.ap[:, 2:3], -1.0, ALU.mult, ALU.mult), reads=[mv.b], writes=[mv.b])
            op("act", ACTF(xi, xi, AF.Identity, bias=mv.ap[:, 3:4], scale=mv.ap[:, 2:3]), reads=[xtb[i], mv.b], writes=[xtb[i]])
            op("pool", TT(xi, xi, gB.ap, ALU.mult), reads=[xtb[i], gB.b], writes=[xtb[i]])
            op("pool", TT(xi, xi, bB.ap, ALU.add), reads=[xtb[i], bB.b], writes=[xtb[i]])
        sc.barrier()
        ar.release(m)

    def merge_phase(l):
        m = ar.mark()
        mg = ar.tile([128, 8, S], BF16)
        mgb = [Buf() for _ in range(NG)]
        xT, xTb = new_xT()
        make_xT(xT, xTb)
        wu = [ar.tile([128, 9, 128], BF16) for _ in range(2)]
        gs = [ar.tile([128, 512], F32) for _ in range(2)]
        tm = [ar.tile([128, 512], F32) for _ in range(2)]
        accs = [ar.tile([128, 512], F32) for _ in range(2)]
        urot = Rot([0, 1, 2])
        grot = Rot([3, 4, 5])
        nrows = (WF, WS, WD)
        k = 0
        for fc in range(8):
            wut = wu[fc % 2]
            for mi in range(3):
                for pr in range(3):
                    r0 = pr * 128
                    nr = min(128, nrows[mi] - r0)
                    wload(wut.ap[0:nr, mi * 3 + pr, :], wut.b, wup_d[mi][l, r0:r0 + nr, fc * 128:(fc + 1) * 128],
                          "wu%d" % (fc % 2))
            wg, kg = next_slot()
            for mi in range(3):
                c_ = O_G + mi * D + fc * 128
                wload(wg.ap[:, :, mi * 128:(mi + 1) * 128], wg.b,
                      w_in_d[l][:, c_:c_ + 128].rearrange("(kc p) n -> p kc n", p=128), kg)
            for g in range(NG):
                for mi in range(3):
                    ac = accs[(k // 3) % 2]
                    ub = urot.next()
                    for pr in range(3):
                        nr = min(128, nrows[mi] - pr * 128)
                        op("pe", MM(ub.ap, wut.ap[0:nr, mi * 3 + pr, :], oT[mi][pr].ap[0:nr, g * 512:(g + 1) * 512],
                                    start=(pr == 0), stop=(pr == 2)), reads=[wut.b, oT[mi][pr].b], writes=[ub.b])
                    gb = grot.next()
                    for kc in range(8):
                        op("pe", MM(gb.ap, wg.ap[:, kc, mi * 128:(mi + 1) * 128], xT[:, kc, g * 512:(g + 1) * 512],
                                    start=(kc == 0), stop=(kc == 7)), reads=[wg.b, xTb[g]], writes=[gb.b])
                    gt = gs[k % 2]
                    op("act", ACTF(gt.ap, gb.ap, AF.Sigmoid), reads=[gb.b], writes=[gt.b])
                    if mi == 0:
                        op("dve", TT(ac.ap, ub.ap, gt.ap, ALU.mult), reads=[ub.b, gt.b], writes=[ac.b])
                    else:
                        t_ = tm[k % 2]
                        op("dve", TT(t_.ap, ub.ap, gt.ap, ALU.mult), reads=[ub.b, gt.b], writes=[t_.b])
                        if mi == 1:
                            op("pool", TT(ac.ap, ac.ap, t_.ap, ALU.add), reads=[ac.b, t_.b], writes=[ac.b])
                        else:
                            op("pool", TT(mg.ap[:, fc, g * 512:(g + 1) * 512], ac.ap, t_.ap, ALU.add),
                               reads=[ac.b, t_.b], writes=[mgb[g]])
                    k += 1
        wo = [wtile(wout_d[l][:, c * 512:(c + 1) * 512], 512) for c in range(2)]
        orot = Rot([0, 1, 2, 3, 4, 5])
        for i in range(NT):
            for c in range(2):
                pb = orot.next()
                for kc in range(8):
                    op("pe", MM(pb.ap, mg.ap[:, kc, i * 128:(i + 1) * 128], wo[c].ap[:, kc, :],
                                start=(kc == 0), stop=(kc == 7)), reads=[mgb[i // 4], wo[c].b], writes=[pb.b])
                xs = x_tok[:, i, c * 512:(c + 1) * 512]
                op("dve", STT(xs, xs, ALPHA, pb.ap, ALU.mult, ALU.add), reads=[xtb[i], pb.b], writes=[xtb[i]])
        sc.barrier()
        ar.release(m)

    def ffn_phase(l, sq):
        m = ar.mark()
        sa = scratch(0)
        xT, xTb = new_xT()
        make_xT(xT, xTb)
        m1 = ar.mark()
        pT = ar.tile([128, 2, S], BF16)
        pst = [ar.tile([128, PLE], F32) for _ in range(2)]
        psb = [ar.tile([128, PLE], BF16) for _ in range(2)]
        wp = ar.tile([128, 2, D], BF16)
        sg_ = [ar.tile([128, 512], F32) for _ in range(2)]
        pl = [ar.tile([128, 512], F32) for _ in range(2)]
        rot = Rot([0, 1, 2, 3])
        trot = Rot([4, 5])
        for i in range(NT):
            a, b_ = pst[i % 2], psb[i % 2]
            op("sp", DMA(a.ap, p_d[l, sq, i * 128:(i + 1) * 128, :]), writes=[a.b], dma="p%d" % (i % 2))
            op("pool", CPY(b_.ap, a.ap), reads=[a.b], writes=[b_.b])
            tb = trot.next()
            tbv = tb.ap.bitcast(BF16)
            for c in range(2):
                op("pe", TRN(tbv[:, c * 128:(c + 1) * 128], b_.ap[:, c * 128:(c + 1) * 128], ident_b),
                   reads=[b_.b, cst_b.b], writes=[tb.b])
            op("act", ACPY(pT.ap[:, :, i * 128:(i + 1) * 128], tbv[:, 0:256].rearrange("p (c t) -> p c t", t=128)),
               reads=[tb.b], writes=[pT.b])
        wload(wp.ap, wp.b, wple_d[l].rearrange("(kc p) n -> p kc n", p=128), "wp")
        k = 0
        for cb in range(2):
            wg = wtile(wpg_d[l][:, cb * 512:(cb + 1) * 512], 512)
            for f4 in range(4):
                fc = cb * 4 + f4
                for g in range(NG):
                    gb = rot.next()
                    for kc in range(8):
                        op("pe", MM(gb.ap, wg.ap[:, kc, f4 * 128:(f4 + 1) * 128], xT[:, kc, g * 512:(g + 1) * 512],
                                    start=(kc == 0), stop=(kc == 7)), reads=[wg.b, xTb[g]], writes=[gb.b])
                    pb = rot.next()
                    for kc in range(2):
                        op("pe", MM(pb.ap, wp.ap[:, kc, fc * 128:(fc + 1) * 128], pT.ap[:, kc, g * 512:(g + 1) * 512],
                                    start=(kc == 0), stop=(kc == 1)), reads=[wp.b, pT.b], writes=[pb.b])
                    s_, p_ = sg_[k % 2], pl[k % 2]
                    k += 1
                    op("act", ACTF(s_.ap, gb.ap, AF.Sigmoid), reads=[gb.b], writes=[s_.b])
                    op("dve", TT(p_.ap, pb.ap, s_.ap, ALU.mult), reads=[pb.b, s_.b], writes=[p_.b])
                    tb = trot.next()
                    for s in range(4):
                        op("pe", TRN(tb.ap[:, s * 128:(s + 1) * 128], p_.ap[:, s * 128:(s + 1) * 128], ident_f),
                           reads=[p_.b, cst_f.b], writes=[tb.b])
                    for s in range(4):
                        i = g * 4 + s
                        xs = x_tok[:, i, fc * 128:(fc + 1) * 128]
                        op("dve", STT(xs, xs, ALPHA, tb.ap[:, s * 128:(s + 1) * 128], ALU.mult, ALU.add),
                           reads=[xtb[i], tb.b], writes=[xtb[i]])
        sc.barrier()
        ar.release(m1)
        TG = 512
        hT = ar.tile([128, 32, TG], BF16)
        wfo = [Tile(sa.alloc([128, 32, 128], BF16)) for _ in range(2)]
        rt = [Tile(sa.alloc([128, 512], F32)) for _ in range(2)]
        ot = [Tile(sa.alloc([128, 512], F32)) for _ in range(2)]
        hrot = Rot([0, 1, 2, 3])
        orot = Rot([4, 5])
        trot = Rot([6, 7])
        k = 0
        for tg in range(S // TG):
            t0 = tg * TG
            for w8 in range(8):
                wt = wtile(wfi_d[l][:, w8 * 512:(w8 + 1) * 512], 512)
                for f4 in range(4):
                    ffc = w8 * 4 + f4
                    hb = hrot.next()
                    for kc in range(8):
                        op("pe", MM(hb.ap[:, 0:TG], wt.ap[:, kc, f4 * 128:(f4 + 1) * 128], xT[:, kc, t0:t0 + TG],
                                    start=(kc == 0), stop=(kc == 7)), reads=[wt.b, xTb[t0 // 512]], writes=[hb.b])
                    r_ = rt[k % 2]
                    k += 1
                    op("act", ACTF(r_.ap[:, 0:TG], hb.ap[:, 0:TG], AF.Relu), reads=[hb.b], writes=[r_.b])
                    op("pool", TT(hT.ap[:, ffc, :], r_.ap[:, 0:TG], r_.ap[:, 0:TG], ALU.mult), reads=[r_.b], writes=[hT.b])
            for fc in range(8):
                wo = wfo[fc % 2]
                wload(wo.ap, wo.b, wfo_d[l][:, fc * 128:(fc + 1) * 128].rearrange("(kc p) n -> p kc n", p=128),
                      "wfo%d" % (fc % 2))
                ob = orot.next()
                for ffc in range(32):
                    op("pe", MM(ob.ap[:, 0:TG], wo.ap[:, ffc, :], hT.ap[:, ffc, :], start=(ffc == 0), stop=(ffc == 31)),
                       reads=[wo.b, hT.b], writes=[ob.b])
                o_ = ot[fc % 2]
                op("act", ACPY(o_.ap[:, 0:TG], ob.ap[:, 0:TG]), reads=[ob.b], writes=[o_.b])
                tb = trot.next()
                for s in range(TG // 128):
                    op("pe", TRN(tb.ap[:, s * 128:(s + 1) * 128], o_.ap[:, s * 128:(s + 1) * 128], ident_f),
                       reads=[o_.b, cst_f.b], writes=[tb.b])
                for s in range(TG // 128):
                    i = t0 // 128 + s
                    xs = x_tok[:, i, fc * 128:(fc + 1) * 128]
                    op("dve", TT(xs, xs, tb.ap[:, s * 128:(s + 1) * 128], ALU.add), reads=[xtb[i], tb.b], writes=[xtb[i]])
        sc.barrier()
        ar.release(m)

    stages = STAGES
    for sq in range(NSEQ):
        for i in range(NT):
            op("sp", DMA(x_tok[:, i, :], x_d[sq, i * 128:(i + 1) * 128, :]), writes=[xtb[i]], dma="x%d" % (i % 4))
        for l in range(DEPTH):
            if "fox" in stages:
                fox_phase(l)
            if "sb" in stages:
                sb_phase(l)
            if "dsa" in stages:
                dsa_phase(l)
            if DEBUG and sq == 0 and l == 0:
                for mi in range(3):
                    if ("fox", "sb", "dsa")[mi] not in stages:
                        continue
                    for pr in range(3):
                        op("pool", DMA(dbg_d[mi * 3 + pr], oT[mi][pr].ap), reads=[oT[mi][pr].b], writes=[outb[0]], dma="dbg")
            if "merge" in stages:
                merge_phase(l)
                layer_norm_tiles(l, ln1g_d, ln1b_d)
            if "ffn" in stages:
                ffn_phase(l, sq)
                layer_norm_tiles(l, ln2g_d, ln2b_d)
        for i in range(NT):
            op("sp", DMA(out_d[sq, i * 128:(i + 1) * 128, :], x_tok[:, i, :]), reads=[xtb[i]], writes=[outb[i % 4]],
               dma="o%d" % (i % 4))
    sc.barrier()
    sc.emit()
    stack.close()
    print("sbuf peak", ar.peak, "limit", SB_LIMIT, "ops", {e: len(v) for e, v in sc.ops.items()}, "chans", len(sc.chans))
    return nc


STAGES = ("fox", "sb", "dsa", "merge", "ffn")
DEBUG = False
FOX_STOP = 0


def host_consts(S):
    c = np.zeros((128, 6, 128), np.float32)
    c[:, 0, :] = np.eye(128, dtype=np.float32)
    c[:, 1, :] = 1.0
    kk = np.arange(128)[:, None]
    qq = np.arange(128)[None, :]
    c[:, 2, :] = np.where(kk > qq, -30000.0 * 8.0, 0.0)
    c[:, 3, :] = (qq < kk).astype(np.float32)
    c[:, 4, :] = np.concatenate([c[64:, 0, :], c[:64, 0, :]], axis=0)
    c[:, 5, :] = np.concatenate([c[64:, 2, :], c[:64, 2, :]], axis=0)
    half = HD // 2
    inv = (10000.0 ** (-(np.arange(half, dtype=np.float32) / half))).astype(np.float32)
    ang = np.arange(S, dtype=np.float32)[None, :] * inv[:, None]
    cos = np.cos(ang).astype(np.float32)
    sin = np.sin(ang).astype(np.float32)
    cos2 = np.concatenate([cos, cos, cos, cos], axis=0)
    sin2 = np.concatenate([-sin, sin, -sin, sin], axis=0)
    return c, np.ascontiguousarray(cos2), np.ascontiguousarray(sin2)


def rot_cols(w_in):
    def swap(o, nh):
        idx = []
        for h in range(nh):
            idx += list(range(o + h * 64 + 32, o + h * 64 + 64)) + list(range(o + h * 64, o + h * 64 + 32))
        return idx
    idx = swap(O_DQ, NDSA) + swap(O_DK, 1) + swap(O_IQ, NIDX) + swap(O_IK, 1)
    return np.ascontiguousarray(w_in[:, :, idx])


_CACHE = {}


def kernel(x, p, w_in, b_forget, w_up_fox, w_up_sb, w_up_dsa, w_out, ln1_g, ln1_b,
           w_ff_in, w_ff_out, w_ple, w_ple_gate, ln2_g, ln2_b):
    NC = 8
    B, S, _ = x.shape
    DEPTH = w_in.shape[0]
    NSEQ = B // NC
    NSEL = min(256, S // 4)
    key = (NSEQ, S, DEPTH, NSEL)
    nc = build_program(*key)
    cst, cos2, sin2 = host_consts(S)
    f = lambda a: np.ascontiguousarray(np.asarray(a, dtype=np.float32))
    shared = {
        "w_in": f(w_in), "w_rot": rot_cols(np.asarray(w_in, dtype=np.float32)), "b_forget": f(b_forget),
        "w_up_fox": f(w_up_fox), "w_up_sb": f(w_up_sb), "w_up_dsa": f(w_up_dsa), "w_out": f(w_out),
        "ln1_g": f(ln1_g), "ln1_b": f(ln1_b), "w_ff_in": f(w_ff_in), "w_ff_out": f(w_ff_out),
        "w_ple": f(w_ple), "w_ple_gate": f(w_ple_gate), "ln2_g": f(ln2_g), "ln2_b": f(ln2_b),
        "consts": cst, "ropecos": cos2, "ropesin": sin2,
    }
    x = np.asarray(x, dtype=np.float32)
    p = np.asarray(p, dtype=np.float32)
    in_maps = []
    for c in range(NC):
        d = dict(shared)
        d["x"] = np.ascontiguousarray(x[c * NSEQ:(c + 1) * NSEQ])
        d["p"] = np.ascontiguousarray(p[:, c * NSEQ:(c + 1) * NSEQ])
        in_maps.append(d)
    res = run_bass_kernel_spmd(nc, in_maps, core_ids=list(range(NC)))
    if DEBUG:
        _CACHE["dbg"] = [r["dbg"] for r in res.results]
    return np.concatenate([r["out"] for r in res.results], axis=0)
```

```python
from contextlib import ExitStack
import numpy as np
import concourse.bass as bass
import concourse.mybir as mybir
from concourse.bass_utils import run_bass_kernel_spmd

F32 = mybir.dt.float32
BF16 = mybir.dt.bfloat16
U8 = mybir.dt.uint8
AF = mybir.ActivationFunctionType
ALU = mybir.AluOpType
AXX = mybir.AxisListType.X
NBIS = 22

D = 1024
HD = 64
NFOX, NSB, NDSA, NIDX = 6, 5, 5, 8
WF, WS, WD = 384, 320, 320
PLE = 256
DFF = 4096
C_IN = 6222
NEG = -1.0e30
ALPHA = 4 ** 0.25
LN_EPS = 1e-5
O_FQ, O_FK, O_FV, O_FF = 0, 384, 768, 1152
O_SQ, O_SK, O_SV = 1158, 1478, 1798
O_DQ, O_DK, O_DV = 2118, 2438, 2502
O_IQ, O_IK, O_IW = 2566, 3078, 3142
O_G = 3150
R_DQ, R_DK, R_IQ, R_IK = 0, 320, 384, 896
NROT = 960

SB_BASE = 16640
SB_LIMIT = 229344


class Buf:
    __slots__ = ("w", "rs", "chan")

    def __init__(self):
        self.w = None
        self.rs = {}
        self.chan = None


class Chan:
    def __init__(self, sem):
        self.sem = sem
        self.count = 0


class Sched:
    ENG = ("pe", "act", "dve", "pool", "sp")

    def __init__(self, nc, stack):
        self.nc = nc
        self.stack = stack
        self.ops = {e: [] for e in self.ENG}
        self.cnt = {e: 0 for e in self.ENG}
        self.sem = {e: stack.enter_context(nc.semaphore("s_" + e)) for e in ("pe", "act", "dve", "pool")}
        self.seen = {e: {} for e in self.ENG}
        self.chans = []
        self.chmap = {}

    def chan(self, key):
        if key not in self.chmap:
            c = Chan(self.stack.enter_context(self.nc.semaphore("c%d" % len(self.chans))))
            c.last = None
            self.chans.append(c)
            self.chmap[key] = c
        return self.chmap[key]

    def _wait(self, eng, tok):
        sem, val = tok
        k = id(sem)
        if self.seen[eng].get(k, 0) >= val:
            return
        self.seen[eng][k] = val
        self.ops[eng].append(("w", sem, val))

    def op(self, eng, fn, reads=(), writes=(), dma=None):
        own = self.sem.get(eng)
        deps = []
        if dma is not None:
            ch = self.chan(dma)
            if ch.last is not None and ch.last is not writes[0] and ch.count > 0:
                deps.append((ch.sem, ch.count))
            ch.last = writes[0]
        for b in reads:
            if b.w is not None:
                deps.append(b.w)
        for b in writes:
            if b.w is not None:
                deps.append(b.w)
            for t in b.rs.values():
                if dma is not None or t[0] is not own:
                    deps.append(t)
        for t in deps:
            if dma is None and eng == "pe" and t[0] is own:
                continue
            self._wait(eng, t)
        if dma is None:
            self.cnt[eng] += 1
            tok = (own, self.cnt[eng])
            self.ops[eng].append(("i", fn, own, 1))
        else:
            ch.count += 16
            tok = (ch.sem, ch.count)
            self.ops[eng].append(("i", fn, ch.sem, 16))
        for b in writes:
            b.w = tok
            b.rs = {}
        for b in reads:
            k = id(tok[0])
            if k not in b.rs or b.rs[k][1] < tok[1]:
                b.rs[k] = tok
        return tok

    def barrier(self):
        toks = [(self.sem[e], self.cnt[e]) for e in self.sem if self.cnt[e] > 0]
        toks += [(c.sem, c.count) for c in self.chans if c.count > 0]
        for e in self.ENG:
            for t in toks:
                self._wait(e, t)

    def emit(self):
        with self.nc.Block() as block:
            def mk(eng):
                ops = self.ops[eng]

                def f(e):
                    for o in ops:
                        if o[0] == "w":
                            e.wait_ge(o[1], o[2])
                        else:
                            o[1](e).then_inc(o[2], o[3])
                return f
            block.tensor(mk("pe"))
            block.scalar(mk("act"))
            block.vector(mk("dve"))
            block.gpsimd(mk("pool"))
            block.sync(mk("sp"))


class Tile:
    __slots__ = ("ap", "b")

    def __init__(self, ap, b=None):
        self.ap = ap
        self.b = b if b is not None else Buf()


class Arena:
    cnt = [0]

    def __init__(self, nc, base=SB_BASE, limit=SB_LIMIT):
        self.nc = nc
        self.top = base
        self.base = base
        self.limit = limit
        self.peak = 0

    def alloc(self, shape, dtype, at=None):
        esz = {F32: 4, BF16: 2, U8: 1}[dtype]
        size = esz
        for s in shape[1:]:
            size *= s
        size = (size + 63) // 64 * 64
        if at is None:
            off = self.top
            self.top += size
            self.peak = max(self.peak, self.top)
            assert self.top <= self.limit, ("SBUF overflow", self.top, self.limit)
        else:
            off = at
        Arena.cnt[0] += 1
        h = self.nc.alloc_sbuf_tensor_at("t%d" % Arena.cnt[0], list(shape), dtype, offset=off)
        return h.ap()

    def tile(self, shape, dtype):
        return Tile(self.alloc(shape, dtype))

    def mark(self):
        return self.top

    def release(self, m):
        self.top = m


def MM(out, lhsT, rhs, start=True, stop=True):
    return lambda e: e.matmul(out, lhsT, rhs, start=start, stop=stop)


def TRN(out, in_, ident):
    return lambda e: e.transpose(out, in_, ident)


def ACTF(out, in_, func, bias=0.0, scale=1.0):
    return lambda e: e.activation(out, in_, func, bias=bias, scale=scale)


def TS(out, in0, s1, s2, op0, op1=None):
    if op1 is None:
        return lambda e: e.tensor_scalar(out, in0, s1, None, op0)
    return lambda e: e.tensor_scalar(out, in0, s1, s2, op0, op1)


def TT(out, in0, in1, op):
    return lambda e: e.tensor_tensor(out, in0, in1, op)


def STT(out, in0, scalar, in1, op0, op1):
    return lambda e: e.scalar_tensor_tensor(out, in0, scalar, in1, op0, op1)


def CPY(out, in_):
    return lambda e: e.tensor_copy(out, in_)


def ACPY(out, in_):
    return lambda e: e.activation(out, in_, AF.Copy)


def MSET(ap, v):
    return lambda e: e.memset(ap, v)


def DMA(out, in_):
    return lambda e: e.dma_start(out=out, in_=in_)


def SCAN(out, d0, d1, init):
    return lambda e: e.tensor_tensor_scan(out, d0, d1, init, ALU.mult, ALU.add)


def RECIP(out, in_):
    return lambda e: e.reciprocal(out, in_)


def build_program(NSEQ, S, DEPTH, NSEL):
    NT = S // 128
    NG = S // 512
    nc = bass.Bass("TRN2", target_bir_lowering=False)
    stack = ExitStack()

    def din(name, shape):
        return nc.dram_tensor(name, list(shape), F32, kind="ExternalInput").ap()

    x_d = din("x", [NSEQ, S, D])
    p_d = din("p", [DEPTH, NSEQ, S, PLE])
    w_in_d = din("w_in", [DEPTH, D, C_IN])
    w_rot_d = din("w_rot", [DEPTH, D, NROT])
    bf_d = din("b_forget", [DEPTH, NFOX])
    wup_d = [din("w_up_fox", [DEPTH, WF, D]), din("w_up_sb", [DEPTH, WS, D]), din("w_up_dsa", [DEPTH, WD, D])]
    wout_d = din("w_out", [DEPTH, D, D])
    ln1g_d = din("ln1_g", [DEPTH, D])
    ln1b_d = din("ln1_b", [DEPTH, D])
    wfi_d = din("w_ff_in", [DEPTH, D, DFF])
    wfo_d = din("w_ff_out", [DEPTH, DFF, D])
    wple_d = din("w_ple", [DEPTH, PLE, D])
    wpg_d = din("w_ple_gate", [DEPTH, D, D])
    ln2g_d = din("ln2_g", [DEPTH, D])
    ln2b_d = din("ln2_b", [DEPTH, D])
    cst_d = din("consts", [128, 6, 128])
    cos_d = din("ropecos", [128, S])
    sin_d = din("ropesin", [128, S])
    out_d = nc.dram_tensor("out", [NSEQ, S, D], F32, kind="ExternalOutput").ap()
    dbg_d = nc.dram_tensor("dbg", [9, 128, S], F32, kind="ExternalOutput").ap() if DEBUG else None

    sc = Sched(nc, stack)
    ar = Arena(nc)
    psa = nc.alloc_psum_tensor("ps", [128, 8, 512], F32).ap()
    bank = [Tile(psa[:, k, :]) for k in range(8)]

    class Rot:
        def __init__(self, ids):
            self.ids = ids
            self.i = 0

        def next(self):
            t = bank[self.ids[self.i % len(self.ids)]]
            self.i += 1
            return t

    op = sc.op

    x_tok = ar.alloc([128, NT, D], F32)
    xtb = [Buf() for _ in range(NT)]
    oT_base = ar.mark()
    oT = [[ar.tile([128, S], BF16) for _ in range(3)] for _ in range(3)]
    oT_sz = (ar.mark() - oT_base) // 3

    def scratch(first_free_mixer):
        if S < 2048:
            return ar
        return Arena(nc, oT_base + first_free_mixer * oT_sz, oT_base + 3 * oT_sz)

    cst_f = ar.tile([128, 6, 128], F32)
    cst_b = ar.tile([128, 6, 128], BF16)
    identS_b, fmaskS_b = cst_b.ap[:, 4, :], cst_b.ap[:, 5, :]
    ident_b, ones_b, fmask_b, tri_b = (cst_b.ap[:, i, :] for i in range(4))
    ident_f, ones_f, tri_f = cst_f.ap[:, 0, :], cst_f.ap[:, 1, :], cst_f.ap[:, 3, :]
    wslot = [ar.tile([128, 8, 512], BF16) for _ in range(2)]
    wsl_i = [0]
    small = ar.tile([128, 16], F32)
    epsT = ar.tile([128, 1], F32)
    outb = [Buf() for _ in range(4)]

    op("sp", DMA(cst_f.ap, cst_d), writes=[cst_f.b], dma="cst")
    op("dve", CPY(cst_b.ap, cst_f.ap), reads=[cst_f.b], writes=[cst_b.b])
    op("pool", MSET(epsT.ap, LN_EPS), writes=[epsT.b])
    pw2 = ar.tile([128, 32], F32)
    for t in range(32):
        op("pool", MSET(pw2.ap[:, t:t + 1], 2.0 ** (-(t + 1))), writes=[pw2.b])

    def ones_bc(npart, n):
        return cst_f.ap[0:npart, 1, 0:1].to_broadcast([npart, n])

    def wload(dst_ap, dst_b, src_ap, key):
        op("pool", DMA(dst_ap, src_ap), writes=[dst_b], dma=key)

    def next_slot():
        k = wsl_i[0] % 2
        wsl_i[0] += 1
        return wslot[k], "w%d" % k

    def wtile(src2d, ncols, rows=D):
        t, key = next_slot()
        kc = rows // 128
        wload(t.ap[:, 0:kc, 0:ncols], t.b, src2d.rearrange("(kc p) n -> p kc n", p=128), key)
        return t

    def make_xT(xT, xTb):
        m = ar.mark()
        xb = [ar.tile([128, D], BF16) for _ in range(2)]
        rot = Rot([0, 1, 2, 3])
        for i in range(NT):
            t = xb[i % 2]
            op("pool", CPY(t.ap, x_tok[:, i, :]), reads=[xtb[i]], writes=[t.b])
            pb = rot.next()
            pbv = pb.ap.bitcast(BF16)
            for c in range(8):
                op("pe", TRN(pbv[:, c * 128:(c + 1) * 128], t.ap[:, c * 128:(c + 1) * 128], ident_b),
                   reads=[t.b, cst_b.b], writes=[pb.b])
            op("act", ACPY(xT[:, :, i * 128:(i + 1) * 128], pbv.rearrange("p (c t) -> p c t", c=8)),
               reads=[pb.b], writes=[xTb[i // 4]])
        sc.barrier()
        ar.release(m)

    def new_xT():
        xT = ar.alloc([128, 8, S], BF16)
        xTb = [Buf() for _ in range(NG)]
        return xT, xTb

    def proj_fm(xT, xTb, wt, col0, ncols, dst_ap, dst_b, rot, evac="act"):
        for g in range(NG):
            pb = rot.next()
            for kc in range(8):
                op("pe", MM(pb.ap[0:ncols, :], wt.ap[:, kc, col0:col0 + ncols], xT[:, kc, g * 512:(g + 1) * 512],
                            start=(kc == 0), stop=(kc == 7)), reads=[wt.b, xTb[g]], writes=[pb.b])
            if evac == "act":
                op("act", ACPY(dst_ap[0:ncols, g * 512:(g + 1) * 512], pb.ap[0:ncols, :]), reads=[pb.b], writes=[dst_b])
            else:
                op("dve", CPY(dst_ap[0:ncols, g * 512:(g + 1) * 512], pb.ap[0:ncols, :]), reads=[pb.b], writes=[dst_b])

    def proj_rope(xT, xTb, wa, wb, ca, cb, ncols, dst_ap, dst_b, rot, ct):
        for g in range(NG):
            op("sp", DMA(ct.ap[:, 0, :], cos_d[:, g * 512:(g + 1) * 512]), writes=[ct.b], dma="cs")
            op("sp", DMA(ct.ap[:, 1, :], sin_d[:, g * 512:(g + 1) * 512]), writes=[ct.b], dma="cs")
            pa = rot.next()
            pb = rot.next()
            for kc in range(8):
                op("pe", MM(pa.ap[0:ncols, :], wa.ap[:, kc, ca:ca + ncols], xT[:, kc, g * 512:(g + 1) * 512],
                            start=(kc == 0), stop=(kc == 7)), reads=[wa.b, xTb[g]], writes=[pa.b])
            for kc in range(8):
                op("pe", MM(pb.ap[0:ncols, :], wb.ap[:, kc, cb:cb + ncols], xT[:, kc, g * 512:(g + 1) * 512],
                            start=(kc == 0), stop=(kc == 7)), reads=[wb.b, xTb[g]], writes=[pb.b])
            t1 = ct.ap[0:ncols, 2, :]
            t2 = ct.ap[0:ncols, 3, :]
            op("dve", TT(t1, pa.ap[0:ncols, :], ct.ap[0:ncols, 0, :], ALU.mult), reads=[pa.b, ct.b], writes=[ct.b])
            op("dve", TT(t2, pb.ap[0:ncols, :], ct.ap[0:ncols, 1, :], ALU.mult), reads=[pb.b, ct.b], writes=[ct.b])
            op("pool", TT(dst_ap[0:ncols, g * 512:(g + 1) * 512], t1, t2, ALU.add), reads=[ct.b], writes=[dst_b])

    def softmax_norm(acc, den_row, bp, dst, g, rd, tmp, bcb):
        r = den_row
        op("dve", RECIP(rd.ap[r:r + 1, :], acc.ap[r:r + 1, :]), reads=[acc.b], writes=[rd.b])
        op("pe", MM(bcb.ap, ones_f[r:r + 1, :], rd.ap[r:r + 1, :]), reads=[rd.b, cst_f.b], writes=[bcb.b])
        op("act", ACPY(tmp.ap[bp:bp + 64, :], bcb.ap[bp:bp + 64, :]), reads=[bcb.b], writes=[tmp.b])
        op("dve", TT(dst.ap[bp:bp + 64, g * 512:(g + 1) * 512], acc.ap[bp:bp + 64, :], tmp.ap[bp:bp + 64, :], ALU.mult),
           reads=[acc.b, tmp.b], writes=[dst.b])

    def pipeline(n_units, stages):
        ns = len(stages)
        for t in range(n_units + ns - 1):
            for st in range(ns):
                u = t - st
                if 0 <= u < n_units and stages[st] is not None:
                    stages[st](u)

    def fox_phase(l):
        m = ar.mark()
        sa = scratch(1)
        qT = [ar.tile([128, S], BF16) for _ in range(3)]
        kT = [ar.tile([128, S], BF16) for _ in range(3)]
        V = ar.tile([128, NT, 3 * 192], BF16)
        CQ = [Tile(sa.alloc([128, S], BF16)) for _ in range(3)]
        negc = ar.tile([128, NT * NFOX], F32)
        bneg = ar.tile([128, 1], F32)
        ft = [Tile(sa.alloc([8, 512], F32)) for _ in range(3)]
        c3 = Tile(sa.alloc([8, 3, 512], BF16))
        m2 = ar.mark()
        xT, xTb = new_xT()
        make_xT(xT, xTb)
        rot = Rot([0, 1, 2, 3])
        w_l = w_in_d[l]
        if FOX_STOP == 1:
            sc.barrier(); ar.release(m); return
        for cq_ in CQ:
            op("pool", MSET(cq_.ap, 0.0), writes=[cq_.b])
        op("pool", MSET(V.ap, 0.0), writes=[V.b])
        for j in range(3):
            op("pool", MSET(V.ap[:, :, j * 192 + 64:j * 192 + 65], 1.0), writes=[V.b])
        op("sp", DMA(bneg.ap[0:NFOX, :], bf_d[l].rearrange("(h o) -> h o", o=1)), writes=[bneg.b], dma="bneg")
        op("dve", TS(bneg.ap[0:NFOX, :], bneg.ap[0:NFOX, :], -1.0, None, ALU.mult), reads=[bneg.b], writes=[bneg.b])
        wq = wtile(w_l[:, O_FQ:O_FQ + 384], 384)
        for j in range(3):
            proj_fm(xT, xTb, wq, j * 128, 128, qT[j].ap, qT[j].b, rot)
        wk = wtile(w_l[:, O_FK:O_FK + 384], 384)
        for j in range(3):
            proj_fm(xT, xTb, wk, j * 128, 128, kT[j].ap, kT[j].b, rot, evac="dve")
        wv = wtile(w_l[:, O_FV:O_FV + 390], 390)
        for i in range(NT):
            pb = rot.next()
            for kc in range(8):
                op("pe", MM(pb.ap[:, 0:384], xT[:, kc, i * 128:(i + 1) * 128], wv.ap[:, kc, 0:384],
                            start=(kc == 0), stop=(kc == 7)), reads=[wv.b, xTb[i // 4]], writes=[pb.b])
            src = pb.ap[:, 0:384].rearrange("p (j e d) -> p j e d", j=3, e=2)
            dstv = V.ap[:, i, :].rearrange("p (j c) -> p j c", c=192)
            op("act", ACPY(dstv[:, :, 0:64], src[:, :, 0, :]), reads=[pb.b], writes=[V.b])
            op("dve", CPY(dstv[:, :, 128:192], src[:, :, 1, :]), reads=[pb.b], writes=[V.b])
        if FOX_STOP == 2:
            sc.barrier(); ar.release(m); return
        e_t, n_t, r_t = ft
        for g in range(NG):
            pb = rot.next()
            for kc in range(8):
                op("pe", MM(pb.ap[0:NFOX, :], wv.ap[:, kc, 384:390], xT[:, kc, g * 512:(g + 1) * 512],
                            start=(kc == 0), stop=(kc == 7)), reads=[wv.b, xTb[g]], writes=[pb.b])
            op("act", ACTF(e_t.ap[0:NFOX, :], pb.ap[0:NFOX, :], AF.Exp, bias=bneg.ap[0:NFOX, :], scale=-1.0),
               reads=[pb.b, bneg.b], writes=[e_t.b])
            op("act", ACTF(e_t.ap[0:NFOX, :], e_t.ap[0:NFOX, :], AF.Ln, bias=1.0, scale=1.0), reads=[e_t.b], writes=[e_t.b])
            init = 0.0 if g == 0 else small.ap[0:NFOX, 0:1]
            op("dve", SCAN(n_t.ap[0:NFOX, :], ones_bc(NFOX, 512), e_t.ap[0:NFOX, :], init),
               reads=[e_t.b, small.b, cst_f.b], writes=[n_t.b])
            op("dve", CPY(small.ap[0:NFOX, 0:1], n_t.ap[0:NFOX, 511:512]), reads=[n_t.b], writes=[small.b])
            for s in range(4):
                i = g * 4 + s
                tb = rot.next()
                op("pe", MM(tb.ap[:, 0:NFOX], n_t.ap[0:NFOX, s * 128:(s + 1) * 128], ident_f[0:NFOX, 0:NFOX]),
                   reads=[n_t.b, cst_f.b], writes=[tb.b])
                op("act", ACPY(negc.ap[:, i * NFOX:(i + 1) * NFOX], tb.ap[:, 0:NFOX]), reads=[tb.b], writes=[negc.b])
            op("dve", TS(r_t.ap[0:NFOX, :], n_t.ap[0:NFOX, :], -8.0, None, ALU.mult), reads=[n_t.b], writes=[r_t.b])
            for k3 in range(3):
                op("dve", CPY(c3.ap[0:NFOX, k3, :], r_t.ap[0:NFOX, :]), reads=[r_t.b], writes=[c3.b])
                if k3 < 2:
                    op("dve", TT(r_t.ap[0:NFOX, :], r_t.ap[0:NFOX, :], c3.ap[0:NFOX, k3, :], ALU.subtract),
                       reads=[r_t.b, c3.b], writes=[r_t.b])
            for h in range(NFOX):
                cq = CQ[h // 2]
                rb = 64 * (h % 2)
                for k3 in range(3):
                    op("sp", DMA(cq.ap[rb + k3:rb + k3 + 1, g * 512:(g + 1) * 512], c3.ap[h:h + 1, k3, :]),
                       reads=[c3.b], writes=[cq.b], dma="cq%d" % (h // 2))
        sc.barrier()
        ar.release(m2)
        if FOX_STOP == 3:
            ar.release(m); return
        PT = [ar.tile([128, 512], BF16) for _ in range(3)]
        rd = ar.tile([128, 512], F32)
        tmp = ar.tile([128, 512], F32)
        bcb = bank[7]
        units = []
        gi = 0
        for h in range(NFOX):
            for g in range(NG):
                nkb = 4 * g + 4
                for kb in range(nkb):
                    units.append((h, g, kb, nkb, gi))
                gi += 1

        def fx_a(u):
            h, g, kb, nkb, gi_ = units[u]
            j, bp = h // 2, 64 * (h % 2)
            cq = CQ[j]
            c0 = 0 if kb < 4 * g else 128 * (kb - 4 * g)
            diag = kb >= 4 * g
            sp_ = bank[u % 4]
            q0 = g * 512 + c0
            op("pe", MM(sp_.ap[:, c0:512], kT[j].ap[bp:bp + 64, kb * 128:(kb + 1) * 128],
                        qT[j].ap[bp:bp + 64, q0:(g + 1) * 512], start=True, stop=False),
               reads=[kT[j].b, qT[j].b], writes=[sp_.b])
            op("pe", MM(sp_.ap[:, c0:512], ones_b[bp:bp + 64, :], cq.ap[bp:bp + 64, q0:(g + 1) * 512],
                        start=False, stop=not diag), reads=[cq.b, cst_b.b], writes=[sp_.b])
            if diag:
                op("pe", MM(sp_.ap[:, c0:c0 + 128], ident_b[bp:bp + 64, :], fmask_b[bp:bp + 64, :],
                            start=False, stop=False), reads=[cst_b.b], writes=[sp_.b])
                op("pe", MM(sp_.ap[:, c0:c0 + 128], identS_b[bp:bp + 64, :], fmaskS_b[bp:bp + 64, :],
                            start=False, stop=True), reads=[cst_b.b], writes=[sp_.b])
            pt = PT[u % 3]
            op("act", ACTF(pt.ap[:, c0:512], sp_.ap[:, c0:512], AF.Exp,
                           bias=negc.ap[:, kb * NFOX + h:kb * NFOX + h + 1], scale=0.125),
               reads=[sp_.b, negc.b], writes=[pt.b])

        def fx_b(u):
            h, g, kb, nkb, gi_ = units[u]
            j, bp = h // 2, 64 * (h % 2)
            c0 = 0 if kb < 4 * g else 128 * (kb - 4 * g)
            lv = (j * 192, j * 192 + 65) if bp == 0 else (j * 192 + 64, j * 192 + 192)
            acc = bank[4 + gi_ % 3]
            pt = PT[u % 3]
            op("pe", MM(acc.ap[:, c0:512] if bp else acc.ap[0:65, c0:512], V.ap[:, kb, lv[0]:lv[1]], pt.ap[:, c0:512],
                        start=(kb == 0), stop=(kb == nkb - 1)), reads=[V.b, pt.b], writes=[acc.b])
            if kb == nkb - 1:
                r = 64 if bp == 0 else 0
                op("dve", RECIP(rd.ap[r:r + 1, :], acc.ap[r:r + 1, :]), reads=[acc.b], writes=[rd.b])

        def fx_c(u):
            h, g, kb, nkb, gi_ = units[u]
            if kb != nkb - 1:
                return
            j, bp = h // 2, 64 * (h % 2)
            r = 64 if bp == 0 else 0
            acc = bank[4 + gi_ % 3]
            dst = oT[0][j]
            op("pe", MM(bcb.ap, ones_f[r:r + 1, :], rd.ap[r:r + 1, :]), reads=[rd.b, cst_f.b], writes=[bcb.b])
            op("act", ACPY(tmp.ap[bp:bp + 64, :], bcb.ap[bp:bp + 64, :]), reads=[bcb.b], writes=[tmp.b])
            op("dve", TT(dst.ap[bp:bp + 64, g * 512:(g + 1) * 512], acc.ap[bp:bp + 64, :], tmp.ap[bp:bp + 64, :], ALU.mult),
               reads=[acc.b, tmp.b], writes=[dst.b])

        pipeline(len(units), [fx_a, fx_b, None, fx_c])
        sc.barrier()
        ar.release(m)

    def sb_phase(l):
        m = ar.mark()
        qT = [ar.tile([128, S], BF16) for _ in range(3)]
        kT = [ar.tile([128, S], BF16) for _ in range(3)]
        V = ar.tile([128, NT, 384], BF16)
        m2 = ar.mark()
        xT, xTb = new_xT()
        make_xT(xT, xTb)
        rot = Rot([0, 1, 2, 3])
        w_l = w_in_d[l]
        op("pool", MSET(V.ap, 0.0), writes=[V.b])
        wq = wtile(w_l[:, O_SQ:O_SQ + 320], 320)
        for j in range(3):
            proj_fm(xT, xTb, wq, j * 128, 128 if j < 2 else 64, qT[j].ap, qT[j].b, rot)
        wk = wtile(w_l[:, O_SK:O_SK + 320], 320)
        for j in range(3):
            proj_fm(xT, xTb, wk, j * 128, 128 if j < 2 else 64, kT[j].ap, kT[j].b, rot, evac="dve")
        wv = wtile(w_l[:, O_SV:O_SV + 320], 320)
        for i in range(NT):
            pb = rot.next()
            for kc in range(8):
                op("pe", MM(pb.ap[:, 0:320], xT[:, kc, i * 128:(i + 1) * 128], wv.ap[:, kc, 0:320],
                            start=(kc == 0), stop=(kc == 7)), reads=[wv.b, xTb[i // 4]], writes=[pb.b])
            op("act", ACPY(V.ap[:, i, 0:320], pb.ap[:, 0:320]), reads=[pb.b], writes=[V.b])
        sc.barrier()
        ar.release(m2)
        CH = 1024
        NB = 2
        SPt = [ar.tile([128, CH], F32) for _ in range(NB)]
        PXt = [ar.tile([128, CH + 1], F32) for _ in range(NB)]
        At = [ar.tile([128, CH], BF16) for _ in range(NB)]
        ATt = [ar.tile([128, CH // 128, 128], BF16) for _ in range(NB)]
        NTs = [ar.tile([128, 1], F32) for _ in range(4)]
        units = []
        gi = 0
        for h in range(NSB):
            for i in range(NT):
                nk = 128 * (i + 1)
                starts = list(range(0, nk, CH))
                for ci, k0 in enumerate(reversed(starts)):
                    units.append((h, i, k0, min(CH, nk - k0), ci == 0, ci == len(starts) - 1, gi))
                gi += 1

        def zview(u, n):
            b0 = 2 * (u % 2)
            nb = (n + 511) // 512
            return b0, nb, psa[:, b0:b0 + nb, :].rearrange("p c n -> p (c n)")[:, 0:n]

        def sb_a(u):
            h, i, k0, n, first, last, gi_ = units[u]
            j, bp = h // 2, 64 * (h % 2)
            sp_, px, nt_ = SPt[u % NB], PXt[u % NB], NTs[u % 4]
            b0, nb, zall = zview(u, n)
            for c in range(nb):
                nn = min(512, n - c * 512)
                zb = bank[b0 + c]
                op("pe", MM(zb.ap[:, 0:nn], qT[j].ap[bp:bp + 64, i * 128:(i + 1) * 128],
                            kT[j].ap[bp:bp + 64, k0 + c * 512:k0 + c * 512 + nn]), reads=[qT[j].b, kT[j].b], writes=[zb.b])
            zbs = [bank[b0 + c].b for c in range(nb)]
            op("act", ACTF(sp_.ap[:, 0:n], zall, AF.Exp, scale=0.125), reads=zbs, writes=[sp_.b])
            op("act", ACTF(sp_.ap[:, 0:n], sp_.ap[:, 0:n], AF.Ln, bias=1.0), reads=[sp_.b], writes=[sp_.b])
            if first:
                op("pool", TT(sp_.ap[:, n - 128:n], sp_.ap[:, n - 128:n], tri_f, ALU.mult),
                   reads=[sp_.b, cst_f.b], writes=[sp_.b])
            op("dve", MSET(px.ap[:, 0:1], 0.0), writes=[px.b])
            op("dve", SCAN(px.ap[:, 1:n + 1], ones_bc(128, n), sp_.ap[:, 0:n], 0.0),
               reads=[sp_.b, cst_f.b, px.b], writes=[px.b])
            if first:
                op("dve", TS(nt_.ap, px.ap[:, n:n + 1], -1.0, None, ALU.mult), reads=[px.b], writes=[nt_.b])
            else:
                ntp = NTs[(u - 1) % 4]
                op("dve", STT(nt_.ap, px.ap[:, n:n + 1], -1.0, ntp.ap, ALU.mult, ALU.add),
                   reads=[px.b, ntp.b], writes=[nt_.b])
            op("dve", STT(sp_.ap[:, 0:n], zall, 0.125, px.ap[:, 0:n], ALU.mult, ALU.add),
               reads=zbs + [px.b], writes=[sp_.b])

        def sb_b1(u):
            h, i, k0, n, first, last, gi_ = units[u]
            sp_, a_, nt_ = SPt[u % NB], At[u % NB], NTs[u % 4]
            op("act", ACTF(a_.ap[:, 0:n], sp_.ap[:, 0:n], AF.Exp, bias=nt_.ap, scale=1.0),
               reads=[sp_.b, nt_.b], writes=[a_.b])
            if first:
                op("pool", TT(a_.ap[:, n - 128:n], a_.ap[:, n - 128:n], tri_b, ALU.mult),
                   reads=[a_.b, cst_b.b], writes=[a_.b])
            tb = bank[4 + u % 2]
            tbv = tb.ap.bitcast(BF16)
            for kb in range(n // 128):
                op("pe", TRN(tbv[:, kb * 128:(kb + 1) * 128], a_.ap[:, kb * 128:(kb + 1) * 128], ident_b),
                   reads=[a_.b, cst_b.b], writes=[tb.b])

        def sb_b2(u):
            h, i, k0, n, first, last, gi_ = units[u]
            tb = bank[4 + u % 2]
            tbv = tb.ap.bitcast(BF16)
            at_ = ATt[u % NB]
            n8 = n // 128
            op("act", ACPY(at_.ap[:, 0:n8, :], tbv[:, 0:n8 * 128].rearrange("p (c t) -> p c t", t=128)),
               reads=[tb.b], writes=[at_.b])

        def sb_b3(u):
            h, i, k0, n, first, last, gi_ = units[u]
            j, bp = h // 2, 64 * (h % 2)
            at_ = ATt[u % NB]
            acc = bank[6 + gi_ % 2]
            lo, hi = (j * 128, j * 128 + 64) if bp == 0 else (j * 128, j * 128 + 128)
            n8 = n // 128
            for kb in range(n8):
                op("pe", MM(acc.ap[0:hi - lo, 0:128], V.ap[:, k0 // 128 + kb, lo:hi], at_.ap[:, kb, :],
                            start=(first and kb == 0), stop=(last and kb == n8 - 1)), reads=[V.b, at_.b], writes=[acc.b])
            if last:
                op("dve", CPY(oT[1][j].ap[bp:bp + 64, i * 128:(i + 1) * 128], acc.ap[bp:bp + 64, 0:128]),
                   reads=[acc.b], writes=[oT[1][j].b])

        pipeline(len(units), [sb_a, sb_b1, sb_b2, sb_b3])
        sc.barrier()
        ar.release(m)

    def dsa_phase(l):
        m = ar.mark()
        sa = scratch(2)
        dqT = [ar.tile([128, S], BF16) for _ in range(3)]
        kkd = ar.tile([128, S], BF16)
        kki = ar.tile([128, S], BF16)
        iqT = [ar.tile([128, S], BF16) for _ in range(4)]
        V = ar.tile([128, NT, 192], BF16)
        iw = ar.tile([128, NT, 8], F32)
        aw = ar.tile([128, NT, 8], F32)
        sg = ar.tile([128, NT, 8], F32)
        m2 = ar.mark()
        xT, xTb = new_xT()
        make_xT(xT, xTb)
        ct = Tile(sa.alloc([128, 4, 512], F32))
        rot = Rot([0, 1, 2, 3, 4, 5])
        w_l = w_in_d[l]
        r_l = w_rot_d[l]
        op("pool", MSET(V.ap, 0.0), writes=[V.b])
        op("pool", MSET(V.ap[:, :, 64:65], 1.0), writes=[V.b])
        wa = wtile(w_l[:, O_DQ:O_DQ + 320], 320)
        wb = wtile(r_l[:, R_DQ:R_DQ + 320], 320)
        for j in range(3):
            proj_rope(xT, xTb, wa, wb, j * 128, j * 128, 128 if j < 2 else 64, dqT[j].ap, dqT[j].b, rot, ct)
        wa = wtile(w_l[:, O_IQ:O_IQ + 512], 512)
        wb = wtile(r_l[:, R_IQ:R_IQ + 512], 512)
        for j in range(4):
            proj_rope(xT, xTb, wa, wb, j * 128, j * 128, 128, iqT[j].ap, iqT[j].b, rot, ct)
        wa, ka = next_slot()
        for q_, o_ in enumerate((O_DK, O_DK, O_IK, O_IK)):
            wload(wa.ap[:, :, q_ * 64:(q_ + 1) * 64], wa.b, w_l[:, o_:o_ + 64].rearrange("(kc p) n -> p kc n", p=128), ka)
        wb, kb_ = next_slot()
        for q_, o_ in enumerate((R_DK, R_DK, R_IK, R_IK)):
            wload(wb.ap[:, :, q_ * 64:(q_ + 1) * 64], wb.b, r_l[:, o_:o_ + 64].rearrange("(kc p) n -> p kc n", p=128), kb_)
        proj_rope(xT, xTb, wa, wb, 0, 0, 128, kkd.ap, kkd.b, rot, ct)
        proj_rope(xT, xTb, wa, wb, 128, 128, 128, kki.ap, kki.b, rot, ct)
        wv, kv = next_slot()
        wload(wv.ap[:, :, 0:64], wv.b, w_l[:, O_DV:O_DV + 64].rearrange("(kc p) n -> p kc n", p=128), kv)
        wload(wv.ap[:, :, 64:72], wv.b, w_l[:, O_IW:O_IW + 8].rearrange("(kc p) n -> p kc n", p=128), kv)
        for i in range(NT):
            pb = rot.next()
            for kc in range(8):
                op("pe", MM(pb.ap[:, 0:72], xT[:, kc, i * 128:(i + 1) * 128], wv.ap[:, kc, 0:72],
                            start=(kc == 0), stop=(kc == 7)), reads=[wv.b, xTb[i // 4]], writes=[pb.b])
            op("act", ACPY(V.ap[:, i, 0:64], pb.ap[:, 0:64]), reads=[pb.b], writes=[V.b])
            op("act", ACPY(V.ap[:, i, 128:192], pb.ap[:, 0:64]), reads=[pb.b], writes=[V.b])
            op("dve", CPY(iw.ap[:, i, :], pb.ap[:, 64:72]), reads=[pb.b], writes=[iw.b])
        op("act", ACTF(aw.ap, iw.ap, AF.Abs), reads=[iw.b], writes=[aw.b])
        op("act", ACTF(sg.ap, iw.ap, AF.Sign), reads=[iw.b], writes=[sg.b])
        sc.barrier()
        ar.release(m2)
        score = Tile(wslot[0].ap.rearrange("p a b -> p (a b)").bitcast(F32), wslot[0].b)
        work = Tile(wslot[1].ap.rearrange("p a b -> p (a b)").bitcast(F32), wslot[1].b)
        rl = [ar.tile([128, 512], F32) for _ in range(2)]
        msk = ar.tile([128, S], BF16)
        mT = ar.tile([128, NT, 512], BF16)
        bs = ar.tile([128, 8], F32)
        wt = ar.tile([128, 32], F32)
        thr = ar.tile([128, 1], F32)
        ET = [ar.tile([128, 512], BF16) for _ in range(3)]
        PT = [ar.tile([128, 512], BF16) for _ in range(4)]
        rd, tmp = rl[0], rl[1]
        ucount = [0]
        trot = Rot([4])
        srot = Rot([5, 6])
        bcb = bank[4]
        ri = 0
        ei = 0
        for g in range(NG):
            for s in range(4):
                i = g * 4 + s
                nk = 128 * (i + 1)
                nb = (nk + 511) // 512
                for h in range(NIDX):
                    j, bp = h // 2, 64 * (h % 2)
                    lb0 = 2 * (h % 2) if nb <= 2 else 0
                    for c in range(nb):
                        n = min(512, nk - c * 512)
                        lb = bank[lb0 + c]
                        op("pe", MM(lb.ap[:, 0:n], iqT[j].ap[bp:bp + 64, i * 128:(i + 1) * 128],
                                    kki.ap[bp:bp + 64, c * 512:c * 512 + n]), reads=[iqT[j].b, kki.b], writes=[lb.b])
                        r_ = rl[ri % 2]
                        ri += 1
                        op("act", ACTF(r_.ap[:, 0:n], lb.ap[:, 0:n], AF.Relu, scale=aw.ap[:, i, h:h + 1]),
                           reads=[lb.b, aw.b], writes=[r_.b])
                        dst = score.ap[:, c * 512:c * 512 + n]
                        if h == 0:
                            op("dve", TS(dst, r_.ap[:, 0:n], sg.ap[:, i, h:h + 1], None, ALU.mult),
                               reads=[r_.b, sg.b], writes=[score.b])
                        else:
                            op("dve", STT(dst, r_.ap[:, 0:n], sg.ap[:, i, h:h + 1], dst, ALU.mult, ALU.add),
                               reads=[r_.b, sg.b, score.b], writes=[score.b])
                if nk > NSEL:
                    sv = score.ap[:, 0:nk]
                    op("dve", (lambda o, i_: (lambda e: e.reduce_max(out=o, in_=i_, axis=AXX)))(bs.ap[:, 0:1], sv),
                       reads=[score.b], writes=[bs.b])
                    op("dve", (lambda o, i_: (lambda e: e.tensor_reduce(out=o, in_=i_, axis=AXX, op=ALU.min)))(bs.ap[:, 1:2], sv),
                       reads=[score.b], writes=[bs.b])
                    op("dve", TS(bs.ap[:, 2:3], bs.ap[:, 0:1], bs.ap[:, 1:2], 1.001, ALU.subtract, ALU.mult),
                       reads=[bs.b], writes=[bs.b])
                    op("dve", TS(bs.ap[:, 3:4], bs.ap[:, 0:1], bs.ap[:, 1:2], 0.5, ALU.add, ALU.mult),
                       reads=[bs.b], writes=[bs.b])
                    op("dve", TS(wt.ap, pw2.ap, bs.ap[:, 2:3], None, ALU.mult), reads=[bs.b, pw2.b], writes=[wt.b])
                    op("dve", MSET(score.ap[0:64, nk - 64:nk], NEG), writes=[score.b])
                    for t in range(NBIS):
                        op("dve", (lambda o, i_, m_, c_: (lambda e: e.tensor_scalar(o, i_, m_, None, ALU.is_ge, ALU.add, accum_out=c_)))(
                            msk.ap[:, 0:nk], sv, bs.ap[:, 3:4], bs.ap[:, 4:5]), reads=[score.b, bs.b], writes=[msk.b, bs.b])
                        op("dve", TS(bs.ap[:, 5:6], bs.ap[:, 4:5], NSEL - 0.5, 0.5, ALU.is_ge, ALU.subtract),
                           reads=[bs.b], writes=[bs.b])
                        op("dve", STT(bs.ap[:, 3:4], bs.ap[:, 5:6], wt.ap[:, t:t + 1], bs.ap[:, 3:4], ALU.mult, ALU.add),
                           reads=[bs.b, wt.b], writes=[bs.b])
                    op("dve", TT(thr.ap, bs.ap[:, 3:4], wt.ap[:, NBIS:NBIS + 1], ALU.subtract), reads=[bs.b, wt.b], writes=[thr.b])
                else:
                    op("dve", MSET(score.ap[0:64, nk - 64:nk], NEG), writes=[score.b])
                    op("dve", MSET(thr.ap, -1.0e29), writes=[thr.b])
                op("dve", TS(msk.ap[:, 0:nk], score.ap[:, 0:nk], thr.ap, None, ALU.is_ge),
                   reads=[score.b, thr.b], writes=[msk.b])
                for c8 in range(0, i + 1, 8):
                    n8 = min(8, i + 1 - c8)
                    tb = trot.next()
                    tbv = tb.ap.bitcast(BF16)
                    for kb in range(c8, c8 + n8):
                        op("pe", TRN(tbv[:, (kb - c8) * 128:(kb - c8 + 1) * 128], msk.ap[:, kb * 128:(kb + 1) * 128], ident_b),
                           reads=[msk.b, cst_b.b], writes=[tb.b])
                    op("act", ACPY(mT.ap[:, c8:c8 + n8, s * 128:(s + 1) * 128],
                                   tbv[:, 0:n8 * 128].rearrange("p (c t) -> p c t", t=128)), reads=[tb.b], writes=[mT.b])
            units = []
            for h in range(NDSA):
                nkb = 4 * g + 4
                for kb in range(nkb):
                    units.append((h, kb, nkb))
            ubase = ucount[0]
            ucount[0] += len(units)

            def ds_a(u, units=units, g=g, ubase=ubase):
                h, kb, nkb = units[u]
                j, bp = h // 2, 64 * (h % 2)
                c0 = 0 if kb < 4 * g else 128 * (kb - 4 * g)
                sp_ = bank[5 + (ubase + u) % 2]
                q0 = g * 512 + c0
                op("pe", MM(sp_.ap[:, c0:512], kkd.ap[bp:bp + 64, kb * 128:(kb + 1) * 128],
                            dqT[j].ap[bp:bp + 64, q0:(g + 1) * 512]), reads=[kkd.b, dqT[j].b], writes=[sp_.b])
                et = ET[(ubase + u) % 3]
                pt = PT[(ubase + u) % 4]
                op("act", ACTF(et.ap[:, c0:512], sp_.ap[:, c0:512], AF.Exp, scale=0.125), reads=[sp_.b], writes=[et.b])
                op("pool", TT(pt.ap[:, c0:512], et.ap[:, c0:512], mT.ap[:, kb, c0:512], ALU.mult),
                   reads=[et.b, mT.b], writes=[pt.b])

            def ds_b(u, units=units, g=g, ubase=ubase):
                h, kb, nkb = units[u]
                j, bp = h // 2, 64 * (h % 2)
                c0 = 0 if kb < 4 * g else 128 * (kb - 4 * g)
                lv = (0, 65) if bp == 0 else (64, 192)
                acc = bank[7]
                pt = PT[(ubase + u) % 4]
                op("pe", MM(acc.ap[:, c0:512] if bp else acc.ap[0:65, c0:512], V.ap[:, kb, lv[0]:lv[1]], pt.ap[:, c0:512],
                            start=(kb == 0), stop=(kb == nkb - 1)), reads=[V.b, pt.b], writes=[acc.b])
                if kb == nkb - 1:
                    softmax_norm(acc, 64 if bp == 0 else 0, bp, oT[2][j], g, rd, tmp, bcb)

            pipeline(len(units), [ds_a, None, ds_b])
        sc.barrier()
        ar.release(m)

    def layer_norm_tiles(l, g_d, b_d):
        m = ar.mark()
        gB = ar.tile([128, D], F32)
        bB = ar.tile([128, D], F32)
        st = ar.tile([128, 2, 6], F32)
        mv = ar.tile([128, 4], F32)
        op("sp", DMA(gB.ap, g_d[l:l + 1, :].to_broadcast([128, D])), writes=[gB.b], dma="lng")
        op("sp", DMA(bB.ap, b_d[l:l + 1, :].to_broadcast([128, D])), writes=[bB.b], dma="lnb")
        for i in range(NT):
            xi = x_tok[:, i, :]
            for c in range(2):
                op("dve", (lambda o, i_: (lambda e: e.bn_stats(o, i_)))(st.ap[:, c, :], x_tok[:, i, c * 512:(c + 1) * 512]),
                   reads=[xtb[i]], writes=[st.b])
            op("dve", (lambda o, i_: (lambda e: e.bn_aggr(o, i_)))(mv.ap[:, 0:2], st.ap.rearrange("p a b -> p (a b)")),
               reads=[st.b], writes=[mv.b])
            op("act", ACTF(mv.ap[:, 2:3], mv.ap[:, 1:2], AF.Sqrt, bias=epsT.ap, scale=1.0), reads=[mv.b, epsT.b], writes=[mv.b])
            op("dve", RECIP(mv.ap[:, 2:3], mv.ap[:, 2:3]), reads=[mv.b], writes=[mv.b])
            op("dve", TS(mv.ap[:, 3:4], mv.ap[:, 0:1], mv.ap[:, 2:3], -1.0, ALU.mult, ALU.mult), reads=[mv.b], writes=[mv.b])
            op("act", ACTF(xi, xi, AF.Identity, bias=mv.ap[:, 3:4], scale=mv.ap[:, 2:3]), reads=[xtb[i], mv.b], writes=[xtb[i]])
            op("pool", TT(xi, xi, gB.ap, ALU.mult), reads=[xtb[i], gB.b], writes=[xtb[i]])
            op("pool", TT(xi, xi, bB.ap, ALU.add), reads=[xtb[i], bB.b], writes=[xtb[i]])
        sc.barrier()
        ar.release(m)

    def merge_phase(l):
        m = ar.mark()
        mg = ar.tile([128, 8, S], BF16)
        mgb = [Buf() for _ in range(NG)]
        xT, xTb = new_xT()
        make_xT(xT, xTb)
        wu = [ar.tile([128, 9, 128], BF16) for _ in range(2)]
        gs = [ar.tile([128, 512], F32) for _ in range(2)]
        tm = [ar.tile([128, 512], F32) for _ in range(2)]
        accs = [ar.tile([128, 512], F32) for _ in range(2)]
        urot = Rot([0, 1, 2])
        grot = Rot([3, 4, 5])
        nrows = (WF, WS, WD)
        k = 0
        for fc in range(8):
            wut = wu[fc % 2]
            for mi in range(3):
                for pr in range(3):
                    r0 = pr * 128
                    nr = min(128, nrows[mi] - r0)
                    wload(wut.ap[0:nr, mi * 3 + pr, :], wut.b, wup_d[mi][l, r0:r0 + nr, fc * 128:(fc + 1) * 128],
                          "wu%d" % (fc % 2))
            wg, kg = next_slot()
            for mi in range(3):
                c_ = O_G + mi * D + fc * 128
                wload(wg.ap[:, :, mi * 128:(mi + 1) * 128], wg.b,
                      w_in_d[l][:, c_:c_ + 128].rearrange("(kc p) n -> p kc n", p=128), kg)
            for g in range(NG):
                for mi in range(3):
                    ac = accs[(k // 3) % 2]
                    ub = urot.next()
                    for pr in range(3):
                        nr = min(128, nrows[mi] - pr * 128)
                        op("pe", MM(ub.ap, wut.ap[0:nr, mi * 3 + pr, :], oT[mi][pr].ap[0:nr, g * 512:(g + 1) * 512],
                                    start=(pr == 0), stop=(pr == 2)), reads=[wut.b, oT[mi][pr].b], writes=[ub.b])
                    gb = grot.next()
                    for kc in range(8):
                        op("pe", MM(gb.ap, wg.ap[:, kc, mi * 128:(mi + 1) * 128], xT[:, kc, g * 512:(g + 1) * 512],
                                    start=(kc == 0), stop=(kc == 7)), reads=[wg.b, xTb[g]], writes=[gb.b])
                    gt = gs[k % 2]
                    op("act", ACTF(gt.ap, gb.ap, AF.Sigmoid), reads=[gb.b], writes=[gt.b])
                    if mi == 0:
                        op("dve", TT(ac.ap, ub.ap, gt.ap, ALU.mult), reads=[ub.b, gt.b], writes=[ac.b])
                    else:
                        t_ = tm[k % 2]
                        op("dve", TT(t_.ap, ub.ap, gt.ap, ALU.mult), reads=[ub.b, gt.b], writes=[t_.b])
                        if mi == 1:
                            op("pool", TT(ac.ap, ac.ap, t_.ap, ALU.add), reads=[ac.b, t_.b], writes=[ac.b])
                        else:
                            op("pool", TT(mg.ap[:, fc, g * 512:(g + 1) * 512], ac.ap, t_.ap, ALU.add),
                               reads=[ac.b, t_.b], writes=[mgb[g]])
                    k += 1
        wo = [wtile(wout_d[l][:, c * 512:(c + 1) * 512], 512) for c in range(2)]
        orot = Rot([0, 1, 2, 3, 4, 5])
        for i in range(NT):
            for c in range(2):
                pb = orot.next()
                for kc in range(8):
                    op("pe", MM(pb.ap, mg.ap[:, kc, i * 128:(i + 1) * 128], wo[c].ap[:, kc, :],
                                start=(kc == 0), stop=(kc == 7)), reads=[mgb[i // 4], wo[c].b], writes=[pb.b])
                xs = x_tok[:, i, c * 512:(c + 1) * 512]
                op("dve", STT(xs, xs, ALPHA, pb.ap, ALU.mult, ALU.add), reads=[xtb[i], pb.b], writes=[xtb[i]])
        sc.barrier()
        ar.release(m)

    def ffn_phase(l, sq):
        m = ar.mark()
        sa = scratch(0)
        xT, xTb = new_xT()
        make_xT(xT, xTb)
        m1 = ar.mark()
        pT = ar.tile([128, 2, S], BF16)
        pst = [ar.tile([128, PLE], F32) for _ in range(2)]
        psb = [ar.tile([128, PLE], BF16) for _ in range(2)]
        wp = ar.tile([128, 2, D], BF16)
        sg_ = [ar.tile([128, 512], F32) for _ in range(2)]
        pl = [ar.tile([128, 512], F32) for _ in range(2)]
        rot = Rot([0, 1, 2, 3])
        trot = Rot([4, 5])
        for i in range(NT):
            a, b_ = pst[i % 2], psb[i % 2]
            op("sp", DMA(a.ap, p_d[l, sq, i * 128:(i + 1) * 128, :]), writes=[a.b], dma="p%d" % (i % 2))
            op("pool", CPY(b_.ap, a.ap), reads=[a.b], writes=[b_.b])
            tb = trot.next()
            tbv = tb.ap.bitcast(BF16)
            for c in range(2):
                op("pe", TRN(tbv[:, c * 128:(c + 1) * 128], b_.ap[:, c * 128:(c + 1) * 128], ident_b),
                   reads=[b_.b, cst_b.b], writes=[tb.b])
            op("act", ACPY(pT.ap[:, :, i * 128:(i + 1) * 128], tbv[:, 0:256].rearrange("p (c t) -> p c t", t=128)),
               reads=[tb.b], writes=[pT.b])
        wload(wp.ap, wp.b, wple_d[l].rearrange("(kc p) n -> p kc n", p=128), "wp")
        k = 0
        for cb in range(2):
            wg = wtile(wpg_d[l][:, cb * 512:(cb + 1) * 512], 512)
            for f4 in range(4):
                fc = cb * 4 + f4
                for g in range(NG):
                    gb = rot.next()
                    for kc in range(8):
                        op("pe", MM(gb.ap, wg.ap[:, kc, f4 * 128:(f4 + 1) * 128], xT[:, kc, g * 512:(g + 1) * 512],
                                    start=(kc == 0), stop=(kc == 7)), reads=[wg.b, xTb[g]], writes=[gb.b])
                    pb = rot.next()
                    for kc in range(2):
                        op("pe", MM(pb.ap, wp.ap[:, kc, fc * 128:(fc + 1) * 128], pT.ap[:, kc, g * 512:(g + 1) * 512],
                                    start=(kc == 0), stop=(kc == 1)), reads=[wp.b, pT.b], writes=[pb.b])
                    s_, p_ = sg_[k % 2], pl[k % 2]
                    k += 1
                    op("act", ACTF(s_.ap, gb.ap, AF.Sigmoid), reads=[gb.b], writes=[s_.b])
                    op("dve", TT(p_.ap, pb.ap, s_.ap, ALU.mult), reads=[pb.b, s_.b], writes=[p_.b])
                    tb = trot.next()
                    for s in range(4):
                        op("pe", TRN(tb.ap[:, s * 128:(s + 1) * 128], p_.ap[:, s * 128:(s + 1) * 128], ident_f),
                           reads=[p_.b, cst_f.b], writes=[tb.b])
                    for s in range(4):
                        i = g * 4 + s
                        xs = x_tok[:, i, fc * 128:(fc + 1) * 128]
                        op("dve", STT(xs, xs, ALPHA, tb.ap[:, s * 128:(s + 1) * 128], ALU.mult, ALU.add),
                           reads=[xtb[i], tb.b], writes=[xtb[i]])
        sc.barrier()
        ar.release(m1)
        TG = 512
        hT = ar.tile([128, 32, TG], BF16)
        wfo = [Tile(sa.alloc([128, 32, 128], BF16)) for _ in range(2)]
        rt = [Tile(sa.alloc([128, 512], F32)) for _ in range(2)]
        ot = [Tile(sa.alloc([128, 512], F32)) for _ in range(2)]
        hrot = Rot([0, 1, 2, 3])
        orot = Rot([4, 5])
        trot = Rot([6, 7])
        k = 0
        for tg in range(S // TG):
            t0 = tg * TG
            for w8 in range(8):
                wt = wtile(wfi_d[l][:, w8 * 512:(w8 + 1) * 512], 512)
                for f4 in range(4):
                    ffc = w8 * 4 + f4
                    hb = hrot.next()
                    for kc in range(8):
                        op("pe", MM(hb.ap[:, 0:TG], wt.ap[:, kc, f4 * 128:(f4 + 1) * 128], xT[:, kc, t0:t0 + TG],
                                    start=(kc == 0), stop=(kc == 7)), reads=[wt.b, xTb[t0 // 512]], writes=[hb.b])
                    r_ = rt[k % 2]
                    k += 1
                    op("act", ACTF(r_.ap[:, 0:TG], hb.ap[:, 0:TG], AF.Relu), reads=[hb.b], writes=[r_.b])
                    op("pool", TT(hT.ap[:, ffc, :], r_.ap[:, 0:TG], r_.ap[:, 0:TG], ALU.mult), reads=[r_.b], writes=[hT.b])
            for fc in range(8):
                wo = wfo[fc % 2]
                wload(wo.ap, wo.b, wfo_d[l][:, fc * 128:(fc + 1) * 128].rearrange("(kc p) n -> p kc n", p=128),
                      "wfo%d" % (fc % 2))
                ob = orot.next()
                for ffc in range(32):
                    op("pe", MM(ob.ap[:, 0:TG], wo.ap[:, ffc, :], hT.ap[:, ffc, :], start=(ffc == 0), stop=(ffc == 31)),
                       reads=[wo.b, hT.b], writes=[ob.b])
                o_ = ot[fc % 2]
                op("act", ACPY(o_.ap[:, 0:TG], ob.ap[:, 0:TG]), reads=[ob.b], writes=[o_.b])
                tb = trot.next()
                for s in range(TG // 128):
                    op("pe", TRN(tb.ap[:, s * 128:(s + 1) * 128], o_.ap[:, s * 128:(s + 1) * 128], ident_f),
                       reads=[o_.b, cst_f.b], writes=[tb.b])
                for s in range(TG // 128):
                    i = t0 // 128 + s
                    xs = x_tok[:, i, fc * 128:(fc + 1) * 128]
                    op("dve", TT(xs, xs, tb.ap[:, s * 128:(s + 1) * 128], ALU.add), reads=[xtb[i], tb.b], writes=[xtb[i]])
        sc.barrier()
        ar.release(m)

    stages = STAGES
    for sq in range(NSEQ):
        for i in range(NT):
            op("sp", DMA(x_tok[:, i, :], x_d[sq, i * 128:(i + 1) * 128, :]), writes=[xtb[i]], dma="x%d" % (i % 4))
        for l in range(DEPTH):
            if "fox" in stages:
                fox_phase(l)
            if "sb" in stages:
                sb_phase(l)
            if "dsa" in stages:
                dsa_phase(l)
            if DEBUG and sq == 0 and l == 0:
                for mi in range(3):
                    if ("fox", "sb", "dsa")[mi] not in stages:
                        continue
                    for pr in range(3):
                        nr_ = 64 if (mi > 0 and pr == 2) else 128
                        op("pool", DMA(dbg_d[mi * 3 + pr, 0:nr_, :], oT[mi][pr].ap[0:nr_, :]), reads=[oT[mi][pr].b],
                           writes=[outb[0]], dma="dbg")
            if "merge" in stages:
                merge_phase(l)
                layer_norm_tiles(l, ln1g_d, ln1b_d)
            if "ffn" in stages:
                ffn_phase(l, sq)
                layer_norm_tiles(l, ln2g_d, ln2b_d)
        for i in range(NT):
            op("sp", DMA(out_d[sq, i * 128:(i + 1) * 128, :], x_tok[:, i, :]), reads=[xtb[i]], writes=[outb[i % 4]],
               dma="o%d" % (i % 4))
    sc.barrier()
    sc.emit()
    stack.close()
    print("sbuf peak", ar.peak, "limit", SB_LIMIT, "ops", {e: len(v) for e, v in sc.ops.items()}, "chans", len(sc.chans))
    return nc


STAGES = ("fox", "sb", "dsa", "merge", "ffn")
DEBUG = False
FOX_STOP = 0


def host_consts(S):
    c = np.zeros((128, 6, 128), np.float32)
    c[:, 0, :] = np.eye(128, dtype=np.float32)
    c[:, 1, :] = 1.0
    kk = np.arange(128)[:, None]
    qq = np.arange(128)[None, :]
    c[:, 2, :] = np.where(kk > qq, -30000.0 * 8.0, 0.0)
    c[:, 3, :] = (qq < kk).astype(np.float32)
    c[:, 4, :] = np.concatenate([c[64:, 0, :], c[:64, 0, :]], axis=0)
    c[:, 5, :] = np.concatenate([c[64:, 2, :], c[:64, 2, :]], axis=0)
    half = HD // 2
    inv = (10000.0 ** (-(np.arange(half, dtype=np.float32) / half))).astype(np.float32)
    ang = np.arange(S, dtype=np.float32)[None, :] * inv[:, None]
    cos = np.cos(ang).astype(np.float32)
    sin = np.sin(ang).astype(np.float32)
    cos2 = np.concatenate([cos, cos, cos, cos], axis=0)
    sin2 = np.concatenate([-sin, sin, -sin, sin], axis=0)
    return c, np.ascontiguousarray(cos2), np.ascontiguousarray(sin2)


def rot_cols(w_in):
    def swap(o, nh):
        idx = []
        for h in range(nh):
            idx += list(range(o + h * 64 + 32, o + h * 64 + 64)) + list(range(o + h * 64, o + h * 64 + 32))
        return idx
    idx = swap(O_DQ, NDSA) + swap(O_DK, 1) + swap(O_IQ, NIDX) + swap(O_IK, 1)
    return np.ascontiguousarray(w_in[:, :, idx])


_CACHE = {}


def kernel(x, p, w_in, b_forget, w_up_fox, w_up_sb, w_up_dsa, w_out, ln1_g, ln1_b,
           w_ff_in, w_ff_out, w_ple, w_ple_gate, ln2_g, ln2_b):
    NC = 8
    B, S, _ = x.shape
    DEPTH = w_in.shape[0]
    NSEQ = B // NC
    NSEL = min(256, S // 4)
    key = (NSEQ, S, DEPTH, NSEL)
    nc = build_program(*key)
    cst, cos2, sin2 = host_consts(S)
    f = lambda a: np.ascontiguousarray(np.asarray(a, dtype=np.float32))
    shared = {
        "w_in": f(w_in), "w_rot": rot_cols(np.asarray(w_in, dtype=np.float32)), "b_forget": f(b_forget),
        "w_up_fox": f(w_up_fox), "w_up_sb": f(w_up_sb), "w_up_dsa": f(w_up_dsa), "w_out": f(w_out),
        "ln1_g": f(ln1_g), "ln1_b": f(ln1_b), "w_ff_in": f(w_ff_in), "w_ff_out": f(w_ff_out),
        "w_ple": f(w_ple), "w_ple_gate": f(w_ple_gate), "ln2_g": f(ln2_g), "ln2_b": f(ln2_b),
        "consts": cst, "ropecos": cos2, "ropesin": sin2,
    }
    x = np.asarray(x, dtype=np.float32)
    p = np.asarray(p, dtype=np.float32)
    in_maps = []
    for c in range(NC):
        d = dict(shared)
        d["x"] = np.ascontiguousarray(x[c * NSEQ:(c + 1) * NSEQ])
        d["p"] = np.ascontiguousarray(p[:, c * NSEQ:(c + 1) * NSEQ])
        in_maps.append(d)
    res = run_bass_kernel_spmd(nc, in_maps, core_ids=list(range(NC)))
    if DEBUG:
        _CACHE["dbg"] = [r["dbg"] for r in res.results]
    return np.concatenate([r["out"] for r in res.results], axis=0)
```

```python
from contextlib import ExitStack
import numpy as np
import concourse.bass as bass
import concourse.mybir as mybir
from concourse.bass_utils import run_bass_kernel_spmd

F32 = mybir.dt.float32
BF16 = mybir.dt.bfloat16
U8 = mybir.dt.uint8
AF = mybir.ActivationFunctionType
ALU = mybir.AluOpType
AXX = mybir.AxisListType.X
NBIS = 22

D = 1024
HD = 64
NFOX, NSB, NDSA, NIDX = 6, 5, 5, 8
WF, WS, WD = 384, 320, 320
PLE = 256
DFF = 4096
C_IN = 6222
NEG = -1.0e30
ALPHA = 4 ** 0.25
LN_EPS = 1e-5
O_FQ, O_FK, O_FV, O_FF = 0, 384, 768, 1152
O_SQ, O_SK, O_SV = 1158, 1478, 1798
O_DQ, O_DK, O_DV = 2118, 2438, 2502
O_IQ, O_IK, O_IW = 2566, 3078, 3142
O_G = 3150
R_DQ, R_DK, R_IQ, R_IK = 0, 320, 384, 896
NROT = 960

SB_BASE = 16640
SB_LIMIT = 229344


class Buf:
    __slots__ = ("w", "rs", "chan")

    def __init__(self):
        self.w = None
        self.rs = {}
        self.chan = None


class Chan:
    def __init__(self, sem):
        self.sem = sem
        self.count = 0


class Sched:
    ENG = ("pe", "act", "dve", "pool", "sp")

    def __init__(self, nc, stack):
        self.nc = nc
        self.stack = stack
        self.ops = {e: [] for e in self.ENG}
        self.cnt = {e: 0 for e in self.ENG}
        self.sem = {e: stack.enter_context(nc.semaphore("s_" + e)) for e in ("pe", "act", "dve", "pool")}
        self.seen = {e: {} for e in self.ENG}
        self.chans = []
        self.chmap = {}

    def chan(self, key):
        if key not in self.chmap:
            c = Chan(self.stack.enter_context(self.nc.semaphore("c%d" % len(self.chans))))
            c.last = None
            self.chans.append(c)
            self.chmap[key] = c
        return self.chmap[key]

    def _wait(self, eng, tok):
        sem, val = tok
        k = id(sem)
        if self.seen[eng].get(k, 0) >= val:
            return
        self.seen[eng][k] = val
        self.ops[eng].append(("w", sem, val))

    def op(self, eng, fn, reads=(), writes=(), dma=None):
        own = self.sem.get(eng)
        deps = []
        if dma is not None:
            ch = self.chan(dma)
            if ch.last is not None and ch.last is not writes[0] and ch.count > 0:
                deps.append((ch.sem, ch.count))
            ch.last = writes[0]
        for b in reads:
            if b.w is not None:
                deps.append(b.w)
        for b in writes:
            if b.w is not None:
                deps.append(b.w)
            for t in b.rs.values():
                if dma is not None or t[0] is not own:
                    deps.append(t)
        for t in deps:
            if dma is None and eng == "pe" and t[0] is own:
                continue
            self._wait(eng, t)
        if dma is None:
            self.cnt[eng] += 1
            tok = (own, self.cnt[eng])
            self.ops[eng].append(("i", fn, own, 1))
        else:
            ch.count += 16
            tok = (ch.sem, ch.count)
            self.ops[eng].append(("i", fn, ch.sem, 16))
        for b in writes:
            b.w = tok
            b.rs = {}
        for b in reads:
            k = id(tok[0])
            if k not in b.rs or b.rs[k][1] < tok[1]:
                b.rs[k] = tok
        return tok

    def barrier(self):
        toks = [(self.sem[e], self.cnt[e]) for e in self.sem if self.cnt[e] > 0]
        toks += [(c.sem, c.count) for c in self.chans if c.count > 0]
        for e in self.ENG:
            for t in toks:
                self._wait(e, t)

    def emit(self):
        with self.nc.Block() as block:
            def mk(eng):
                ops = self.ops[eng]

                def f(e):
                    for o in ops:
                        if o[0] == "w":
                            e.wait_ge(o[1], o[2])
                        else:
                            o[1](e).then_inc(o[2], o[3])
                return f
            block.tensor(mk("pe"))
            block.scalar(mk("act"))
            block.vector(mk("dve"))
            block.gpsimd(mk("pool"))
            block.sync(mk("sp"))


class Tile:
    __slots__ = ("ap", "b")

    def __init__(self, ap, b=None):
        self.ap = ap
        self.b = b if b is not None else Buf()


class Arena:
    cnt = [0]

    def __init__(self, nc, base=SB_BASE, limit=SB_LIMIT):
        self.nc = nc
        self.top = base
        self.base = base
        self.limit = limit
        self.peak = 0

    def alloc(self, shape, dtype, at=None):
        esz = {F32: 4, BF16: 2, U8: 1}[dtype]
        size = esz
        for s in shape[1:]:
            size *= s
        size = (size + 63) // 64 * 64
        if at is None:
            off = self.top
            self.top += size
            self.peak = max(self.peak, self.top)
            assert self.top <= self.limit, ("SBUF overflow", self.top, self.limit)
        else:
            off = at
        Arena.cnt[0] += 1
        h = self.nc.alloc_sbuf_tensor_at("t%d" % Arena.cnt[0], list(shape), dtype, offset=off)
        return h.ap()

    def tile(self, shape, dtype):
        return Tile(self.alloc(shape, dtype))

    def mark(self):
        return self.top

    def release(self, m):
        self.top = m


def MM(out, lhsT, rhs, start=True, stop=True):
    return lambda e: e.matmul(out, lhsT, rhs, start=start, stop=stop)


def TRN(out, in_, ident):
    return lambda e: e.transpose(out, in_, ident)


def ACTF(out, in_, func, bias=0.0, scale=1.0):
    return lambda e: e.activation(out, in_, func, bias=bias, scale=scale)


def TS(out, in0, s1, s2, op0, op1=None):
    if op1 is None:
        return lambda e: e.tensor_scalar(out, in0, s1, None, op0)
    return lambda e: e.tensor_scalar(out, in0, s1, s2, op0, op1)


def TT(out, in0, in1, op):
    return lambda e: e.tensor_tensor(out, in0, in1, op)


def STT(out, in0, scalar, in1, op0, op1):
    return lambda e: e.scalar_tensor_tensor(out, in0, scalar, in1, op0, op1)


def CPY(out, in_):
    return lambda e: e.tensor_copy(out, in_)


def ACPY(out, in_):
    return lambda e: e.activation(out, in_, AF.Copy)


def MSET(ap, v):
    return lambda e: e.memset(ap, v)


def DMA(out, in_):
    return lambda e: e.dma_start(out=out, in_=in_)


def SCAN(out, d0, d1, init):
    return lambda e: e.tensor_tensor_scan(out, d0, d1, init, ALU.mult, ALU.add)


def RECIP(out, in_):
    return lambda e: e.reciprocal(out, in_)


def build_program(NSEQ, S, DEPTH, NSEL):
    NT = S // 128
    NG = S // 512
    nc = bass.Bass("TRN2", target_bir_lowering=False)
    stack = ExitStack()

    def din(name, shape):
        return nc.dram_tensor(name, list(shape), F32, kind="ExternalInput").ap()

    x_d = din("x", [NSEQ, S, D])
    p_d = din("p", [DEPTH, NSEQ, S, PLE])
    w_in_d = din("w_in", [DEPTH, D, C_IN])
    w_rot_d = din("w_rot", [DEPTH, D, NROT])
    bf_d = din("b_forget", [DEPTH, NFOX])
    wup_d = [din("w_up_fox", [DEPTH, WF, D]), din("w_up_sb", [DEPTH, WS, D]), din("w_up_dsa", [DEPTH, WD, D])]
    wout_d = din("w_out", [DEPTH, D, D])
    ln1g_d = din("ln1_g", [DEPTH, D])
    ln1b_d = din("ln1_b", [DEPTH, D])
    wfi_d = din("w_ff_in", [DEPTH, D, DFF])
    wfo_d = din("w_ff_out", [DEPTH, DFF, D])
    wple_d = din("w_ple", [DEPTH, PLE, D])
    wpg_d = din("w_ple_gate", [DEPTH, D, D])
    ln2g_d = din("ln2_g", [DEPTH, D])
    ln2b_d = din("ln2_b", [DEPTH, D])
    cst_d = din("consts", [128, 6, 128])
    cos_d = din("ropecos", [128, S])
    sin_d = din("ropesin", [128, S])
    out_d = nc.dram_tensor("out", [NSEQ, S, D], F32, kind="ExternalOutput").ap()
    dbg_d = nc.dram_tensor("dbg", [9, 128, S], F32, kind="ExternalOutput").ap() if DEBUG else None

    sc = Sched(nc, stack)
    ar = Arena(nc)
    psa = nc.alloc_psum_tensor("ps", [128, 8, 512], F32).ap()
    bank = [Tile(psa[:, k, :]) for k in range(8)]

    class Rot:
        def __init__(self, ids):
            self.ids = ids
            self.i = 0

        def next(self):
            t = bank[self.ids[self.i % len(self.ids)]]
            self.i += 1
            return t

    op = sc.op

    x_tok = ar.alloc([128, NT, D], F32)
    xtb = [Buf() for _ in range(NT)]
    oT_base = ar.mark()
    oT = [[ar.tile([128, S], BF16) for _ in range(3)] for _ in range(3)]
    oT_sz = (ar.mark() - oT_base) // 3

    def scratch(first_free_mixer):
        if S < 2048:
            return ar
        return Arena(nc, oT_base + first_free_mixer * oT_sz, oT_base + 3 * oT_sz)

    cst_f = ar.tile([128, 6, 128], F32)
    cst_b = ar.tile([128, 6, 128], BF16)
    identS_b, fmaskS_b = cst_b.ap[:, 4, :], cst_b.ap[:, 5, :]
    ident_b, ones_b, fmask_b, tri_b = (cst_b.ap[:, i, :] for i in range(4))
    ident_f, ones_f, tri_f = cst_f.ap[:, 0, :], cst_f.ap[:, 1, :], cst_f.ap[:, 3, :]
    wslot = [ar.tile([128, 8, 512], BF16) for _ in range(2)]
    wsl_i = [0]
    small = ar.tile([128, 16], F32)
    epsT = ar.tile([128, 1], F32)
    outb = [Buf() for _ in range(4)]

    op("sp", DMA(cst_f.ap, cst_d), writes=[cst_f.b], dma="cst")
    op("dve", CPY(cst_b.ap, cst_f.ap), reads=[cst_f.b], writes=[cst_b.b])
    op("pool", MSET(epsT.ap, LN_EPS), writes=[epsT.b])
    pw2 = ar.tile([128, 32], F32)
    for t in range(32):
        op("pool", MSET(pw2.ap[:, t:t + 1], 2.0 ** (-(t + 1))), writes=[pw2.b])

    def ones_bc(npart, n):
        return cst_f.ap[0:npart, 1, 0:1].to_broadcast([npart, n])

    def wload(dst_ap, dst_b, src_ap, key):
        op("pool", DMA(dst_ap, src_ap), writes=[dst_b], dma=key)

    def next_slot():
        k = wsl_i[0] % 2
        wsl_i[0] += 1
        return wslot[k], "w%d" % k

    def wtile(src2d, ncols, rows=D):
        t, key = next_slot()
        kc = rows // 128
        wload(t.ap[:, 0:kc, 0:ncols], t.b, src2d.rearrange("(kc p) n -> p kc n", p=128), key)
        return t

    def make_xT(xT, xTb):
        m = ar.mark()
        xb = [ar.tile([128, D], BF16) for _ in range(2)]
        rot = Rot([0, 1, 2, 3])
        for i in range(NT):
            t = xb[i % 2]
            op("pool", CPY(t.ap, x_tok[:, i, :]), reads=[xtb[i]], writes=[t.b])
            pb = rot.next()
            pbv = pb.ap.bitcast(BF16)
            for c in range(8):
                op("pe", TRN(pbv[:, c * 128:(c + 1) * 128], t.ap[:, c * 128:(c + 1) * 128], ident_b),
                   reads=[t.b, cst_b.b], writes=[pb.b])
            op("act", ACPY(xT[:, :, i * 128:(i + 1) * 128], pbv.rearrange("p (c t) -> p c t", c=8)),
               reads=[pb.b], writes=[xTb[i // 4]])
        sc.barrier()
        ar.release(m)

    def new_xT():
        xT = ar.alloc([128, 8, S], BF16)
        xTb = [Buf() for _ in range(NG)]
        return xT, xTb

    def proj_fm(xT, xTb, wt, col0, ncols, dst_ap, dst_b, rot, evac="act"):
        for g in range(NG):
            pb = rot.next()
            for kc in range(8):
                op("pe", MM(pb.ap[0:ncols, :], wt.ap[:, kc, col0:col0 + ncols], xT[:, kc, g * 512:(g + 1) * 512],
                            start=(kc == 0), stop=(kc == 7)), reads=[wt.b, xTb[g]], writes=[pb.b])
            if evac == "act":
                op("act", ACPY(dst_ap[0:ncols, g * 512:(g + 1) * 512], pb.ap[0:ncols, :]), reads=[pb.b], writes=[dst_b])
            else:
                op("dve", CPY(dst_ap[0:ncols, g * 512:(g + 1) * 512], pb.ap[0:ncols, :]), reads=[pb.b], writes=[dst_b])

    def proj_rope(xT, xTb, wa, wb, ca, cb, ncols, dst_ap, dst_b, rot, ct):
        for g in range(NG):
            op("sp", DMA(ct.ap[:, 0, :], cos_d[:, g * 512:(g + 1) * 512]), writes=[ct.b], dma="cs")
            op("sp", DMA(ct.ap[:, 1, :], sin_d[:, g * 512:(g + 1) * 512]), writes=[ct.b], dma="cs")
            pa = rot.next()
            pb = rot.next()
            for kc in range(8):
                op("pe", MM(pa.ap[0:ncols, :], wa.ap[:, kc, ca:ca + ncols], xT[:, kc, g * 512:(g + 1) * 512],
                            start=(kc == 0), stop=(kc == 7)), reads=[wa.b, xTb[g]], writes=[pa.b])
            for kc in range(8):
                op("pe", MM(pb.ap[0:ncols, :], wb.ap[:, kc, cb:cb + ncols], xT[:, kc, g * 512:(g + 1) * 512],
                            start=(kc == 0), stop=(kc == 7)), reads=[wb.b, xTb[g]], writes=[pb.b])
            t1 = ct.ap[0:ncols, 2, :]
            t2 = ct.ap[0:ncols, 3, :]
            op("dve", TT(t1, pa.ap[0:ncols, :], ct.ap[0:ncols, 0, :], ALU.mult), reads=[pa.b, ct.b], writes=[ct.b])
            op("dve", TT(t2, pb.ap[0:ncols, :], ct.ap[0:ncols, 1, :], ALU.mult), reads=[pb.b, ct.b], writes=[ct.b])
            op("pool", TT(dst_ap[0:ncols, g * 512:(g + 1) * 512], t1, t2, ALU.add), reads=[ct.b], writes=[dst_b])

    def softmax_norm(acc, den_row, bp, dst, g, rd, tmp, bcb):
        r = den_row
        op("dve", RECIP(rd.ap[r:r + 1, :], acc.ap[r:r + 1, :]), reads=[acc.b], writes=[rd.b])
        op("pe", MM(bcb.ap, ones_f[r:r + 1, :], rd.ap[r:r + 1, :]), reads=[rd.b, cst_f.b], writes=[bcb.b])
        op("act", ACPY(tmp.ap[bp:bp + 64, :], bcb.ap[bp:bp + 64, :]), reads=[bcb.b], writes=[tmp.b])
        op("dve", TT(dst.ap[bp:bp + 64, g * 512:(g + 1) * 512], acc.ap[bp:bp + 64, :], tmp.ap[bp:bp + 64, :], ALU.mult),
           reads=[acc.b, tmp.b], writes=[dst.b])

    def pipeline(n_units, stages):
        ns = len(stages)
        for t in range(n_units + ns - 1):
            for st in range(ns):
                u = t - st
                if 0 <= u < n_units and stages[st] is not None:
                    stages[st](u)

    def fox_phase(l):
        m = ar.mark()
        sa = scratch(1)
        qT = [ar.tile([128, S], BF16) for _ in range(3)]
        kT = [ar.tile([128, S], BF16) for _ in range(3)]
        V = ar.tile([128, NT, 3 * 192], BF16)
        CQ = [Tile(sa.alloc([128, S], BF16)) for _ in range(3)]
        negc = ar.tile([128, NT * NFOX], F32)
        bneg = ar.tile([128, 1], F32)
        ft = [Tile(sa.alloc([8, 512], F32)) for _ in range(3)]
        c3 = Tile(sa.alloc([8, 3, 512], BF16))
        m2 = ar.mark()
        xT, xTb = new_xT()
        make_xT(xT, xTb)
        rot = Rot([0, 1, 2, 3])
        w_l = w_in_d[l]
        if FOX_STOP == 1:
            sc.barrier(); ar.release(m); return
        for cq_ in CQ:
            op("pool", MSET(cq_.ap, 0.0), writes=[cq_.b])
        op("pool", MSET(V.ap, 0.0), writes=[V.b])
        for j in range(3):
            op("pool", MSET(V.ap[:, :, j * 192 + 64:j * 192 + 65], 1.0), writes=[V.b])
        op("sp", DMA(bneg.ap[0:NFOX, :], bf_d[l].rearrange("(h o) -> h o", o=1)), writes=[bneg.b], dma="bneg")
        op("dve", TS(bneg.ap[0:NFOX, :], bneg.ap[0:NFOX, :], -1.0, None, ALU.mult), reads=[bneg.b], writes=[bneg.b])
        wq = wtile(w_l[:, O_FQ:O_FQ + 384], 384)
        for j in range(3):
            proj_fm(xT, xTb, wq, j * 128, 128, qT[j].ap, qT[j].b, rot)
        wk = wtile(w_l[:, O_FK:O_FK + 384], 384)
        for j in range(3):
            proj_fm(xT, xTb, wk, j * 128, 128, kT[j].ap, kT[j].b, rot, evac="dve")
        wv = wtile(w_l[:, O_FV:O_FV + 390], 390)
        for i in range(NT):
            pb = rot.next()
            for kc in range(8):
                op("pe", MM(pb.ap[:, 0:384], xT[:, kc, i * 128:(i + 1) * 128], wv.ap[:, kc, 0:384],
                            start=(kc == 0), stop=(kc == 7)), reads=[wv.b, xTb[i // 4]], writes=[pb.b])
            src = pb.ap[:, 0:384].rearrange("p (j e d) -> p j e d", j=3, e=2)
            dstv = V.ap[:, i, :].rearrange("p (j c) -> p j c", c=192)
            op("act", ACPY(dstv[:, :, 0:64], src[:, :, 0, :]), reads=[pb.b], writes=[V.b])
            op("dve", CPY(dstv[:, :, 128:192], src[:, :, 1, :]), reads=[pb.b], writes=[V.b])
        if FOX_STOP == 2:
            sc.barrier(); ar.release(m); return
        e_t, n_t, r_t = ft
        for g in range(NG):
            pb = rot.next()
            for kc in range(8):
                op("pe", MM(pb.ap[0:NFOX, :], wv.ap[:, kc, 384:390], xT[:, kc, g * 512:(g + 1) * 512],
                            start=(kc == 0), stop=(kc == 7)), reads=[wv.b, xTb[g]], writes=[pb.b])
            op("act", ACTF(e_t.ap[0:NFOX, :], pb.ap[0:NFOX, :], AF.Exp, bias=bneg.ap[0:NFOX, :], scale=-1.0),
               reads=[pb.b, bneg.b], writes=[e_t.b])
            op("act", ACTF(e_t.ap[0:NFOX, :], e_t.ap[0:NFOX, :], AF.Ln, bias=1.0, scale=1.0), reads=[e_t.b], writes=[e_t.b])
            init = 0.0 if g == 0 else small.ap[0:NFOX, 0:1]
            op("dve", SCAN(n_t.ap[0:NFOX, :], ones_bc(NFOX, 512), e_t.ap[0:NFOX, :], init),
               reads=[e_t.b, small.b, cst_f.b], writes=[n_t.b])
            op("dve", CPY(small.ap[0:NFOX, 0:1], n_t.ap[0:NFOX, 511:512]), reads=[n_t.b], writes=[small.b])
            for s in range(4):
                i = g * 4 + s
                tb = rot.next()
                op("pe", MM(tb.ap[:, 0:NFOX], n_t.ap[0:NFOX, s * 128:(s + 1) * 128], ident_f[0:NFOX, 0:NFOX]),
                   reads=[n_t.b, cst_f.b], writes=[tb.b])
                op("act", ACPY(negc.ap[:, i * NFOX:(i + 1) * NFOX], tb.ap[:, 0:NFOX]), reads=[tb.b], writes=[negc.b])
            op("dve", TS(r_t.ap[0:NFOX, :], n_t.ap[0:NFOX, :], -8.0, None, ALU.mult), reads=[n_t.b], writes=[r_t.b])
            for k3 in range(3):
                op("dve", CPY(c3.ap[0:NFOX, k3, :], r_t.ap[0:NFOX, :]), reads=[r_t.b], writes=[c3.b])
                if k3 < 2:
                    op("dve", TT(r_t.ap[0:NFOX, :], r_t.ap[0:NFOX, :], c3.ap[0:NFOX, k3, :], ALU.subtract),
                       reads=[r_t.b, c3.b], writes=[r_t.b])
            for h in range(NFOX):
                cq = CQ[h // 2]
                rb = 64 * (h % 2)
                for k3 in range(3):
                    op("sp", DMA(cq.ap[rb + k3:rb + k3 + 1, g * 512:(g + 1) * 512], c3.ap[h:h + 1, k3, :]),
                       reads=[c3.b], writes=[cq.b], dma="cq%d" % (h // 2))
        sc.barrier()
        ar.release(m2)
        if FOX_STOP == 3:
            ar.release(m); return
        PT = [ar.tile([128, 512], BF16) for _ in range(3)]
        rd = ar.tile([128, 512], F32)
        tmp = ar.tile([128, 512], F32)
        bcb = bank[7]
        units = []
        gi = 0
        for h in range(NFOX):
            for g in range(NG):
                nkb = 4 * g + 4
                for kb in range(nkb):
                    units.append((h, g, kb, nkb, gi))
                gi += 1

        def fx_a(u):
            h, g, kb, nkb, gi_ = units[u]
            j, bp = h // 2, 64 * (h % 2)
            cq = CQ[j]
            c0 = 0 if kb < 4 * g else 128 * (kb - 4 * g)
            diag = kb >= 4 * g
            sp_ = bank[u % 4]
            q0 = g * 512 + c0
            op("pe", MM(sp_.ap[:, c0:512], kT[j].ap[bp:bp + 64, kb * 128:(kb + 1) * 128],
                        qT[j].ap[bp:bp + 64, q0:(g + 1) * 512], start=True, stop=False),
               reads=[kT[j].b, qT[j].b], writes=[sp_.b])
            op("pe", MM(sp_.ap[:, c0:512], ones_b[bp:bp + 64, :], cq.ap[bp:bp + 64, q0:(g + 1) * 512],
                        start=False, stop=not diag), reads=[cq.b, cst_b.b], writes=[sp_.b])
            if diag:
                op("pe", MM(sp_.ap[:, c0:c0 + 128], ident_b[bp:bp + 64, :], fmask_b[bp:bp + 64, :],
                            start=False, stop=False), reads=[cst_b.b], writes=[sp_.b])
                op("pe", MM(sp_.ap[:, c0:c0 + 128], identS_b[bp:bp + 64, :], fmaskS_b[bp:bp + 64, :],
                            start=False, stop=True), reads=[cst_b.b], writes=[sp_.b])
            pt = PT[u % 3]
            op("act", ACTF(pt.ap[:, c0:512], sp_.ap[:, c0:512], AF.Exp,
                           bias=negc.ap[:, kb * NFOX + h:kb * NFOX + h + 1], scale=0.125),
               reads=[sp_.b, negc.b], writes=[pt.b])

        def fx_b(u):
            h, g, kb, nkb, gi_ = units[u]
            j, bp = h // 2, 64 * (h % 2)
            c0 = 0 if kb < 4 * g else 128 * (kb - 4 * g)
            lv = (j * 192, j * 192 + 65) if bp == 0 else (j * 192 + 64, j * 192 + 192)
            acc = bank[4 + gi_ % 3]
            pt = PT[u % 3]
            op("pe", MM(acc.ap[:, c0:512] if bp else acc.ap[0:65, c0:512], V.ap[:, kb, lv[0]:lv[1]], pt.ap[:, c0:512],
                        start=(kb == 0), stop=(kb == nkb - 1)), reads=[V.b, pt.b], writes=[acc.b])
            if kb == nkb - 1:
                r = 64 if bp == 0 else 0
                op("dve", RECIP(rd.ap[r:r + 1, :], acc.ap[r:r + 1, :]), reads=[acc.b], writes=[rd.b])

        def fx_c(u):
            h, g, kb, nkb, gi_ = units[u]
            if kb != nkb - 1:
                return
            j, bp = h // 2, 64 * (h % 2)
            r = 64 if bp == 0 else 0
            acc = bank[4 + gi_ % 3]
            dst = oT[0][j]
            op("pe", MM(bcb.ap, ones_f[r:r + 1, :], rd.ap[r:r + 1, :]), reads=[rd.b, cst_f.b], writes=[bcb.b])
            op("act", ACPY(tmp.ap[bp:bp + 64, :], bcb.ap[bp:bp + 64, :]), reads=[bcb.b], writes=[tmp.b])
            op("dve", TT(dst.ap[bp:bp + 64, g * 512:(g + 1) * 512], acc.ap[bp:bp + 64, :], tmp.ap[bp:bp + 64, :], ALU.mult),
               reads=[acc.b, tmp.b], writes=[dst.b])

        pipeline(len(units), [fx_a, fx_b, None, fx_c])
        sc.barrier()
        ar.release(m)

    def sb_phase(l):
        m = ar.mark()
        qT = [ar.tile([128, S], BF16) for _ in range(3)]
        kT = [ar.tile([128, S], BF16) for _ in range(3)]
        V = ar.tile([128, NT, 384], BF16)
        m2 = ar.mark()
        xT, xTb = new_xT()
        make_xT(xT, xTb)
        rot = Rot([0, 1, 2, 3])
        w_l = w_in_d[l]
        op("pool", MSET(V.ap, 0.0), writes=[V.b])
        wq = wtile(w_l[:, O_SQ:O_SQ + 320], 320)
        for j in range(3):
            proj_fm(xT, xTb, wq, j * 128, 128 if j < 2 else 64, qT[j].ap, qT[j].b, rot)
        wk = wtile(w_l[:, O_SK:O_SK + 320], 320)
        for j in range(3):
            proj_fm(xT, xTb, wk, j * 128, 128 if j < 2 else 64, kT[j].ap, kT[j].b, rot, evac="dve")
        wv = wtile(w_l[:, O_SV:O_SV + 320], 320)
        for i in range(NT):
            pb = rot.next()
            for kc in range(8):
                op("pe", MM(pb.ap[:, 0:320], xT[:, kc, i * 128:(i + 1) * 128], wv.ap[:, kc, 0:320],
                            start=(kc == 0), stop=(kc == 7)), reads=[wv.b, xTb[i // 4]], writes=[pb.b])
            op("act", ACPY(V.ap[:, i, 0:320], pb.ap[:, 0:320]), reads=[pb.b], writes=[V.b])
        sc.barrier()
        ar.release(m2)
        CH = 1024
        NB = 2
        SPt = [ar.tile([128, CH], F32) for _ in range(NB)]
        PXt = [ar.tile([128, CH + 1], F32) for _ in range(NB)]
        At = [ar.tile([128, CH], BF16) for _ in range(NB)]
        ATt = [ar.tile([128, CH // 128, 128], BF16) for _ in range(NB)]
        NTs = [ar.tile([128, 1], F32) for _ in range(4)]
        units = []
        gi = 0
        for h in range(NSB):
            for i in range(NT):
                nk = 128 * (i + 1)
                starts = list(range(0, nk, CH))
                for ci, k0 in enumerate(reversed(starts)):
                    units.append((h, i, k0, min(CH, nk - k0), ci == 0, ci == len(starts) - 1, gi))
                gi += 1

        def zview(u, n):
            b0 = 2 * (u % 2)
            nb = (n + 511) // 512
            return b0, nb, psa[:, b0:b0 + nb, :].rearrange("p c n -> p (c n)")[:, 0:n]

        def sb_a(u):
            h, i, k0, n, first, last, gi_ = units[u]
            j, bp = h // 2, 64 * (h % 2)
            sp_, px, nt_ = SPt[u % NB], PXt[u % NB], NTs[u % 4]
            b0, nb, zall = zview(u, n)
            for c in range(nb):
                nn = min(512, n - c * 512)
                zb = bank[b0 + c]
                op("pe", MM(zb.ap[:, 0:nn], qT[j].ap[bp:bp + 64, i * 128:(i + 1) * 128],
                            kT[j].ap[bp:bp + 64, k0 + c * 512:k0 + c * 512 + nn]), reads=[qT[j].b, kT[j].b], writes=[zb.b])
            zbs = [bank[b0 + c].b for c in range(nb)]
            op("act", ACTF(sp_.ap[:, 0:n], zall, AF.Exp, scale=0.125), reads=zbs, writes=[sp_.b])
            op("act", ACTF(sp_.ap[:, 0:n], sp_.ap[:, 0:n], AF.Ln, bias=1.0), reads=[sp_.b], writes=[sp_.b])
            if first:
                op("pool", TT(sp_.ap[:, n - 128:n], sp_.ap[:, n - 128:n], tri_f, ALU.mult),
                   reads=[sp_.b, cst_f.b], writes=[sp_.b])
            op("dve", MSET(px.ap[:, 0:1], 0.0), writes=[px.b])
            op("dve", SCAN(px.ap[:, 1:n + 1], ones_bc(128, n), sp_.ap[:, 0:n], 0.0),
               reads=[sp_.b, cst_f.b, px.b], writes=[px.b])
            if first:
                op("dve", TS(nt_.ap, px.ap[:, n:n + 1], -1.0, None, ALU.mult), reads=[px.b], writes=[nt_.b])
            else:
                ntp = NTs[(u - 1) % 4]
                op("dve", STT(nt_.ap, px.ap[:, n:n + 1], -1.0, ntp.ap, ALU.mult, ALU.add),
                   reads=[px.b, ntp.b], writes=[nt_.b])
            op("dve", STT(sp_.ap[:, 0:n], zall, 0.125, px.ap[:, 0:n], ALU.mult, ALU.add),
               reads=zbs + [px.b], writes=[sp_.b])

        def sb_b1(u):
            h, i, k0, n, first, last, gi_ = units[u]
            sp_, a_, nt_ = SPt[u % NB], At[u % NB], NTs[u % 4]
            op("act", ACTF(a_.ap[:, 0:n], sp_.ap[:, 0:n], AF.Exp, bias=nt_.ap, scale=1.0),
               reads=[sp_.b, nt_.b], writes=[a_.b])
            if first:
                op("pool", TT(a_.ap[:, n - 128:n], a_.ap[:, n - 128:n], tri_b, ALU.mult),
                   reads=[a_.b, cst_b.b], writes=[a_.b])
            tb = bank[4 + u % 2]
            tbv = tb.ap.bitcast(BF16)
            for kb in range(n // 128):
                op("pe", TRN(tbv[:, kb * 128:(kb + 1) * 128], a_.ap[:, kb * 128:(kb + 1) * 128], ident_b),
                   reads=[a_.b, cst_b.b], writes=[tb.b])

        def sb_b2(u):
            h, i, k0, n, first, last, gi_ = units[u]
            tb = bank[4 + u % 2]
            tbv = tb.ap.bitcast(BF16)
            at_ = ATt[u % NB]
            n8 = n // 128
            op("act", ACPY(at_.ap[:, 0:n8, :], tbv[:, 0:n8 * 128].rearrange("p (c t) -> p c t", t=128)),
               reads=[tb.b], writes=[at_.b])

        def sb_b3(u):
            h, i, k0, n, first, last, gi_ = units[u]
            j, bp = h // 2, 64 * (h % 2)
            at_ = ATt[u % NB]
            acc = bank[6 + gi_ % 2]
            lo, hi = (j * 128, j * 128 + 64) if bp == 0 else (j * 128, j * 128 + 128)
            n8 = n // 128
            for kb in range(n8):
                op("pe", MM(acc.ap[0:hi - lo, 0:128], V.ap[:, k0 // 128 + kb, lo:hi], at_.ap[:, kb, :],
                            start=(first and kb == 0), stop=(last and kb == n8 - 1)), reads=[V.b, at_.b], writes=[acc.b])
            if last:
                op("dve", CPY(oT[1][j].ap[bp:bp + 64, i * 128:(i + 1) * 128], acc.ap[bp:bp + 64, 0:128]),
                   reads=[acc.b], writes=[oT[1][j].b])

        pipeline(len(units), [sb_a, sb_b1, sb_b2, sb_b3])
        sc.barrier()
        ar.release(m)

    def dsa_phase(l):
        m = ar.mark()
        sa = scratch(2)
        dqT = [ar.tile([128, S], BF16) for _ in range(3)]
        kkd = ar.tile([128, S], BF16)
        kki = ar.tile([128, S], BF16)
        iqT = [ar.tile([128, S], BF16) for _ in range(4)]
        V = ar.tile([128, NT, 192], BF16)
        iw = ar.tile([128, NT, 8], F32)
        aw = ar.tile([128, NT, 8], F32)
        sg = ar.tile([128, NT, 8], F32)
        m2 = ar.mark()
        xT, xTb = new_xT()
        make_xT(xT, xTb)
        ct = Tile(sa.alloc([128, 4, 512], F32))
        rot = Rot([0, 1, 2, 3, 4, 5])
        w_l = w_in_d[l]
        r_l = w_rot_d[l]
        op("pool", MSET(V.ap, 0.0), writes=[V.b])
        op("pool", MSET(V.ap[:, :, 64:65], 1.0), writes=[V.b])
        wa = wtile(w_l[:, O_DQ:O_DQ + 320], 320)
        wb = wtile(r_l[:, R_DQ:R_DQ + 320], 320)
        for j in range(3):
            proj_rope(xT, xTb, wa, wb, j * 128, j * 128, 128 if j < 2 else 64, dqT[j].ap, dqT[j].b, rot, ct)
        wa = wtile(w_l[:, O_IQ:O_IQ + 512], 512)
        wb = wtile(r_l[:, R_IQ:R_IQ + 512], 512)
        for j in range(4):
            proj_rope(xT, xTb, wa, wb, j * 128, j * 128, 128, iqT[j].ap, iqT[j].b, rot, ct)
        wa, ka = next_slot()
        for q_, o_ in enumerate((O_DK, O_DK, O_IK, O_IK)):
            wload(wa.ap[:, :, q_ * 64:(q_ + 1) * 64], wa.b, w_l[:, o_:o_ + 64].rearrange("(kc p) n -> p kc n", p=128), ka)
        wb, kb_ = next_slot()
        for q_, o_ in enumerate((R_DK, R_DK, R_IK, R_IK)):
            wload(wb.ap[:, :, q_ * 64:(q_ + 1) * 64], wb.b, r_l[:, o_:o_ + 64].rearrange("(kc p) n -> p kc n", p=128), kb_)
        proj_rope(xT, xTb, wa, wb, 0, 0, 128, kkd.ap, kkd.b, rot, ct)
        proj_rope(xT, xTb, wa, wb, 128, 128, 128, kki.ap, kki.b, rot, ct)
        wv, kv = next_slot()
        wload(wv.ap[:, :, 0:64], wv.b, w_l[:, O_DV:O_DV + 64].rearrange("(kc p) n -> p kc n", p=128), kv)
        wload(wv.ap[:, :, 64:72], wv.b, w_l[:, O_IW:O_IW + 8].rearrange("(kc p) n -> p kc n", p=128), kv)
        for i in range(NT):
            pb = rot.next()
            for kc in range(8):
                op("pe", MM(pb.ap[:, 0:72], xT[:, kc, i * 128:(i + 1) * 128], wv.ap[:, kc, 0:72],
                            start=(kc == 0), stop=(kc == 7)), reads=[wv.b, xTb[i // 4]], writes=[pb.b])
            op("act", ACPY(V.ap[:, i, 0:64], pb.ap[:, 0:64]), reads=[pb.b], writes=[V.b])
            op("act", ACPY(V.ap[:, i, 128:192], pb.ap[:, 0:64]), reads=[pb.b], writes=[V.b])
            op("dve", CPY(iw.ap[:, i, :], pb.ap[:, 64:72]), reads=[pb.b], writes=[iw.b])
        op("act", ACTF(aw.ap, iw.ap, AF.Abs), reads=[iw.b], writes=[aw.b])
        op("act", ACTF(sg.ap, iw.ap, AF.Sign), reads=[iw.b], writes=[sg.b])
        sc.barrier()
        ar.release(m2)
        score = Tile(wslot[0].ap.rearrange("p a b -> p (a b)").bitcast(F32), wslot[0].b)
        work = Tile(wslot[1].ap.rearrange("p a b -> p (a b)").bitcast(F32), wslot[1].b)
        rl = [ar.tile([128, 512], F32) for _ in range(2)]
        msk = ar.tile([128, S], BF16)
        mT = ar.tile([128, NT, 512], BF16)
        bs = ar.tile([128, 8], F32)
        wt = ar.tile([128, 32], F32)
        thr = ar.tile([128, 1], F32)
        ET = [ar.tile([128, 512], BF16) for _ in range(3)]
        PT = [ar.tile([128, 512], BF16) for _ in range(4)]
        rd, tmp = rl[0], rl[1]
        ucount = [0]
        trot = Rot([4])
        srot = Rot([5, 6])
        bcb = bank[4]
        ri = 0
        ei = 0
        for g in range(NG):
            for s in range(4):
                i = g * 4 + s
                nk = 128 * (i + 1)
                nb = (nk + 511) // 512
                for h in range(NIDX):
                    j, bp = h // 2, 64 * (h % 2)
                    lb0 = 2 * (h % 2) if nb <= 2 else 0
                    for c in range(nb):
                        n = min(512, nk - c * 512)
                        lb = bank[lb0 + c]
                        op("pe", MM(lb.ap[:, 0:n], iqT[j].ap[bp:bp + 64, i * 128:(i + 1) * 128],
                                    kki.ap[bp:bp + 64, c * 512:c * 512 + n]), reads=[iqT[j].b, kki.b], writes=[lb.b])
                        r_ = rl[ri % 2]
                        ri += 1
                        op("act", ACTF(r_.ap[:, 0:n], lb.ap[:, 0:n], AF.Relu, scale=aw.ap[:, i, h:h + 1]),
                           reads=[lb.b, aw.b], writes=[r_.b])
                        dst = score.ap[:, c * 512:c * 512 + n]
                        if h == 0:
                            op("dve", TS(dst, r_.ap[:, 0:n], sg.ap[:, i, h:h + 1], None, ALU.mult),
                               reads=[r_.b, sg.b], writes=[score.b])
                        else:
                            op("dve", STT(dst, r_.ap[:, 0:n], sg.ap[:, i, h:h + 1], dst, ALU.mult, ALU.add),
                               reads=[r_.b, sg.b, score.b], writes=[score.b])
                if nk > NSEL:
                    sv = score.ap[:, 0:nk]
                    op("dve", (lambda o, i_: (lambda e: e.reduce_max(out=o, in_=i_, axis=AXX)))(bs.ap[:, 0:1], sv),
                       reads=[score.b], writes=[bs.b])
                    op("dve", (lambda o, i_: (lambda e: e.tensor_reduce(out=o, in_=i_, axis=AXX, op=ALU.min)))(bs.ap[:, 1:2], sv),
                       reads=[score.b], writes=[bs.b])
                    op("dve", TS(bs.ap[:, 2:3], bs.ap[:, 0:1], bs.ap[:, 1:2], 1.001, ALU.subtract, ALU.mult),
                       reads=[bs.b], writes=[bs.b])
                    op("dve", TS(bs.ap[:, 3:4], bs.ap[:, 0:1], bs.ap[:, 1:2], 0.5, ALU.add, ALU.mult),
                       reads=[bs.b], writes=[bs.b])
                    op("dve", TS(wt.ap, pw2.ap, bs.ap[:, 2:3], None, ALU.mult), reads=[bs.b, pw2.b], writes=[wt.b])
                    op("dve", MSET(score.ap[0:64, nk - 64:nk], NEG), writes=[score.b])
                    for t in range(NBIS):
                        op("dve", (lambda o, i_, m_, c_: (lambda e: e.tensor_scalar(o, i_, m_, None, ALU.is_ge, ALU.add, accum_out=c_)))(
                            msk.ap[:, 0:nk], sv, bs.ap[:, 3:4], bs.ap[:, 4:5]), reads=[score.b, bs.b], writes=[msk.b, bs.b])
                        op("dve", TS(bs.ap[:, 5:6], bs.ap[:, 4:5], NSEL - 0.5, 0.5, ALU.is_ge, ALU.subtract),
                           reads=[bs.b], writes=[bs.b])
                        op("dve", STT(bs.ap[:, 3:4], bs.ap[:, 5:6], wt.ap[:, t:t + 1], bs.ap[:, 3:4], ALU.mult, ALU.add),
                           reads=[bs.b, wt.b], writes=[bs.b])
                    op("dve", TT(thr.ap, bs.ap[:, 3:4], wt.ap[:, NBIS:NBIS + 1], ALU.subtract), reads=[bs.b, wt.b], writes=[thr.b])
                else:
                    op("dve", MSET(score.ap[0:64, nk - 64:nk], NEG), writes=[score.b])
                    op("dve", MSET(thr.ap, -1.0e29), writes=[thr.b])
                op("dve", TS(msk.ap[:, 0:nk], score.ap[:, 0:nk], thr.ap, None, ALU.is_ge),
                   reads=[score.b, thr.b], writes=[msk.b])
                for c8 in range(0, i + 1, 8):
                    n8 = min(8, i + 1 - c8)
                    tb = trot.next()
                    tbv = tb.ap.bitcast(BF16)
                    for kb in range(c8, c8 + n8):
                        op("pe", TRN(tbv[:, (kb - c8) * 128:(kb - c8 + 1) * 128], msk.ap[:, kb * 128:(kb + 1) * 128], ident_b),
                           reads=[msk.b, cst_b.b], writes=[tb.b])
                    op("act", ACPY(mT.ap[:, c8:c8 + n8, s * 128:(s + 1) * 128],
                                   tbv[:, 0:n8 * 128].rearrange("p (c t) -> p c t", t=128)), reads=[tb.b], writes=[mT.b])
            units = []
            for h in range(NDSA):
                nkb = 4 * g + 4
                for kb in range(nkb):
                    units.append((h, kb, nkb))
            ubase = ucount[0]
            ucount[0] += len(units)

            def ds_a(u, units=units, g=g, ubase=ubase):
                h, kb, nkb = units[u]
                j, bp = h // 2, 64 * (h % 2)
                c0 = 0 if kb < 4 * g else 128 * (kb - 4 * g)
                sp_ = bank[5 + (ubase + u) % 2]
                q0 = g * 512 + c0
                op("pe", MM(sp_.ap[:, c0:512], kkd.ap[bp:bp + 64, kb * 128:(kb + 1) * 128],
                            dqT[j].ap[bp:bp + 64, q0:(g + 1) * 512]), reads=[kkd.b, dqT[j].b], writes=[sp_.b])
                et = ET[(ubase + u) % 3]
                pt = PT[(ubase + u) % 4]
                op("act", ACTF(et.ap[:, c0:512], sp_.ap[:, c0:512], AF.Exp, scale=0.125), reads=[sp_.b], writes=[et.b])
                op("pool", TT(pt.ap[:, c0:512], et.ap[:, c0:512], mT.ap[:, kb, c0:512], ALU.mult),
                   reads=[et.b, mT.b], writes=[pt.b])

            def ds_b(u, units=units, g=g, ubase=ubase):
                h, kb, nkb = units[u]
                j, bp = h // 2, 64 * (h % 2)
                c0 = 0 if kb < 4 * g else 128 * (kb - 4 * g)
                lv = (0, 65) if bp == 0 else (64, 192)
                acc = bank[(7, 3)[h % 2]]
                pt = PT[(ubase + u) % 4]
                op("pe", MM(acc.ap[:, c0:512] if bp else acc.ap[0:65, c0:512], V.ap[:, kb, lv[0]:lv[1]], pt.ap[:, c0:512],
                            start=(kb == 0), stop=(kb == nkb - 1)), reads=[V.b, pt.b], writes=[acc.b])
                if kb == nkb - 1:
                    r = 64 if bp == 0 else 0
                    op("dve", RECIP(rd.ap[r:r + 1, :], acc.ap[r:r + 1, :]), reads=[acc.b], writes=[rd.b])

            def ds_c(u, units=units, g=g):
                h, kb, nkb = units[u]
                if kb != nkb - 1:
                    return
                j, bp = h // 2, 64 * (h % 2)
                r = 64 if bp == 0 else 0
                acc = bank[(7, 3)[h % 2]]
                dst = oT[2][j]
                op("pe", MM(bcb.ap, ones_f[r:r + 1, :], rd.ap[r:r + 1, :]), reads=[rd.b, cst_f.b], writes=[bcb.b])
                op("act", ACPY(tmp.ap[bp:bp + 64, :], bcb.ap[bp:bp + 64, :]), reads=[bcb.b], writes=[tmp.b])
                op("dve", TT(dst.ap[bp:bp + 64, g * 512:(g + 1) * 512], acc.ap[bp:bp + 64, :], tmp.ap[bp:bp + 64, :], ALU.mult),
                   reads=[acc.b, tmp.b], writes=[dst.b])

            pipeline(len(units), [ds_a, None, ds_b, None, ds_c])
        sc.barrier()
        ar.release(m)

    def layer_norm_tiles(l, g_d, b_d):
        m = ar.mark()
        gB = ar.tile([128, D], F32)
        bB = ar.tile([128, D], F32)
        op("sp", DMA(gB.ap, g_d[l:l + 1, :].to_broadcast([128, D])), writes=[gB.b], dma="lng")
        op("sp", DMA(bB.ap, b_d[l:l + 1, :].to_broadcast([128, D])), writes=[bB.b], dma="lnb")
        sts = [ar.tile([128, 2, 6], F32) for _ in range(2)]
        mvs = [ar.tile([128, 4], F32) for _ in range(6)]

        def ln_a(i):
            st_, mv_ = sts[i % 2], mvs[i % 6]
            for c in range(2):
                op("dve", (lambda o, i_: (lambda e: e.bn_stats(o, i_)))(st_.ap[:, c, :], x_tok[:, i, c * 512:(c + 1) * 512]),
                   reads=[xtb[i]], writes=[st_.b])
            op("dve", (lambda o, i_: (lambda e: e.bn_aggr(o, i_)))(mv_.ap[:, 0:2], st_.ap.rearrange("p a b -> p (a b)")),
               reads=[st_.b], writes=[mv_.b])

        def ln_b(i):
            mv_ = mvs[i % 6]
            op("act", ACTF(mv_.ap[:, 2:3], mv_.ap[:, 1:2], AF.Sqrt, bias=epsT.ap, scale=1.0), reads=[mv_.b, epsT.b], writes=[mv_.b])

        def ln_c(i):
            mv_ = mvs[i % 6]
            op("dve", RECIP(mv_.ap[:, 2:3], mv_.ap[:, 2:3]), reads=[mv_.b], writes=[mv_.b])
            op("dve", TS(mv_.ap[:, 3:4], mv_.ap[:, 0:1], mv_.ap[:, 2:3], -1.0, ALU.mult, ALU.mult), reads=[mv_.b], writes=[mv_.b])

        def ln_d(i):
            mv_ = mvs[i % 6]
            xi = x_tok[:, i, :]
            op("act", ACTF(xi, xi, AF.Identity, bias=mv_.ap[:, 3:4], scale=mv_.ap[:, 2:3]), reads=[xtb[i], mv_.b], writes=[xtb[i]])

        def ln_e(i):
            xi = x_tok[:, i, :]
            op("pool", TT(xi, xi, gB.ap, ALU.mult), reads=[xtb[i], gB.b], writes=[xtb[i]])
            op("pool", TT(xi, xi, bB.ap, ALU.add), reads=[xtb[i], bB.b], writes=[xtb[i]])

        pipeline(NT, [ln_a, ln_b, ln_c, ln_d, ln_e])
        sc.barrier()
        ar.release(m)

    def merge_phase(l):
        m = ar.mark()
        mg = ar.tile([128, 8, S], BF16)
        mgb = [Buf() for _ in range(NG)]
        xT, xTb = new_xT()
        make_xT(xT, xTb)
        wu = [ar.tile([128, 9, 128], BF16) for _ in range(2)]
        gs = [ar.tile([128, 512], F32) for _ in range(2)]
        tm = [ar.tile([128, 512], F32) for _ in range(2)]
        accs = [ar.tile([128, 512], F32) for _ in range(2)]
        urot = Rot([0, 1, 2])
        grot = Rot([3, 4, 5])
        nrows = (WF, WS, WD)
        k = 0
        for fc in range(8):
            wut = wu[fc % 2]
            for mi in range(3):
                for pr in range(3):
                    r0 = pr * 128
                    nr = min(128, nrows[mi] - r0)
                    wload(wut.ap[0:nr, mi * 3 + pr, :], wut.b, wup_d[mi][l, r0:r0 + nr, fc * 128:(fc + 1) * 128],
                          "wu%d" % (fc % 2))
            wg, kg = next_slot()
            for mi in range(3):
                c_ = O_G + mi * D + fc * 128
                wload(wg.ap[:, :, mi * 128:(mi + 1) * 128], wg.b,
                      w_in_d[l][:, c_:c_ + 128].rearrange("(kc p) n -> p kc n", p=128), kg)
            for g in range(NG):
                for mi in range(3):
                    ac = accs[(k // 3) % 2]
                    ub = urot.next()
                    for pr in range(3):
                        nr = min(128, nrows[mi] - pr * 128)
                        op("pe", MM(ub.ap, wut.ap[0:nr, mi * 3 + pr, :], oT[mi][pr].ap[0:nr, g * 512:(g + 1) * 512],
                                    start=(pr == 0), stop=(pr == 2)), reads=[wut.b, oT[mi][pr].b], writes=[ub.b])
                    gb = grot.next()
                    for kc in range(8):
                        op("pe", MM(gb.ap, wg.ap[:, kc, mi * 128:(mi + 1) * 128], xT[:, kc, g * 512:(g + 1) * 512],
                                    start=(kc == 0), stop=(kc == 7)), reads=[wg.b, xTb[g]], writes=[gb.b])
                    gt = gs[k % 2]
                    op("act", ACTF(gt.ap, gb.ap, AF.Sigmoid), reads=[gb.b], writes=[gt.b])
                    if mi == 0:
                        op("dve", TT(ac.ap, ub.ap, gt.ap, ALU.mult), reads=[ub.b, gt.b], writes=[ac.b])
                    else:
                        t_ = tm[k % 2]
                        op("dve", TT(t_.ap, ub.ap, gt.ap, ALU.mult), reads=[ub.b, gt.b], writes=[t_.b])
                        if mi == 1:
                            op("pool", TT(ac.ap, ac.ap, t_.ap, ALU.add), reads=[ac.b, t_.b], writes=[ac.b])
                        else:
                            op("pool", TT(mg.ap[:, fc, g * 512:(g + 1) * 512], ac.ap, t_.ap, ALU.add),
                               reads=[ac.b, t_.b], writes=[mgb[g]])
                    k += 1
        wo = [wtile(wout_d[l][:, c * 512:(c + 1) * 512], 512) for c in range(2)]
        orot = Rot([0, 1, 2, 3, 4, 5])
        for i in range(NT):
            for c in range(2):
                pb = orot.next()
                for kc in range(8):
                    op("pe", MM(pb.ap, mg.ap[:, kc, i * 128:(i + 1) * 128], wo[c].ap[:, kc, :],
                                start=(kc == 0), stop=(kc == 7)), reads=[mgb[i // 4], wo[c].b], writes=[pb.b])
                xs = x_tok[:, i, c * 512:(c + 1) * 512]
                op("dve", STT(xs, xs, ALPHA, pb.ap, ALU.mult, ALU.add), reads=[xtb[i], pb.b], writes=[xtb[i]])
        sc.barrier()
        ar.release(m)

    def ffn_phase(l, sq):
        m = ar.mark()
        sa = scratch(0)
        xT, xTb = new_xT()
        make_xT(xT, xTb)
        m1 = ar.mark()
        pT = ar.tile([128, 2, S], BF16)
        pst = [ar.tile([128, PLE], F32) for _ in range(2)]
        psb = [ar.tile([128, PLE], BF16) for _ in range(2)]
        wp = ar.tile([128, 2, D], BF16)
        sg_ = [ar.tile([128, 512], F32) for _ in range(2)]
        pl = [ar.tile([128, 512], F32) for _ in range(2)]
        rot = Rot([0, 1, 2, 3])
        trot = Rot([4, 5])
        for i in range(NT):
            a, b_ = pst[i % 2], psb[i % 2]
            op("sp", DMA(a.ap, p_d[l, sq, i * 128:(i + 1) * 128, :]), writes=[a.b], dma="p%d" % (i % 2))
            op("pool", CPY(b_.ap, a.ap), reads=[a.b], writes=[b_.b])
            tb = trot.next()
            tbv = tb.ap.bitcast(BF16)
            for c in range(2):
                op("pe", TRN(tbv[:, c * 128:(c + 1) * 128], b_.ap[:, c * 128:(c + 1) * 128], ident_b),
                   reads=[b_.b, cst_b.b], writes=[tb.b])
            op("act", ACPY(pT.ap[:, :, i * 128:(i + 1) * 128], tbv[:, 0:256].rearrange("p (c t) -> p c t", t=128)),
               reads=[tb.b], writes=[pT.b])
        wload(wp.ap, wp.b, wple_d[l].rearrange("(kc p) n -> p kc n", p=128), "wp")
        k = 0
        for cb in range(2):
            wg = wtile(wpg_d[l][:, cb * 512:(cb + 1) * 512], 512)
            for f4 in range(4):
                fc = cb * 4 + f4
                for g in range(NG):
                    gb = rot.next()
                    for kc in range(8):
                        op("pe", MM(gb.ap, wg.ap[:, kc, f4 * 128:(f4 + 1) * 128], xT[:, kc, g * 512:(g + 1) * 512],
                                    start=(kc == 0), stop=(kc == 7)), reads=[wg.b, xTb[g]], writes=[gb.b])
                    pb = rot.next()
                    for kc in range(2):
                        op("pe", MM(pb.ap, wp.ap[:, kc, fc * 128:(fc + 1) * 128], pT.ap[:, kc, g * 512:(g + 1) * 512],
                                    start=(kc == 0), stop=(kc == 1)), reads=[wp.b, pT.b], writes=[pb.b])
                    s_, p_ = sg_[k % 2], pl[k % 2]
                    k += 1
                    op("act", ACTF(s_.ap, gb.ap, AF.Sigmoid), reads=[gb.b], writes=[s_.b])
                    op("dve", TT(p_.ap, pb.ap, s_.ap, ALU.mult), reads=[pb.b, s_.b], writes=[p_.b])
                    tb = trot.next()
                    for s in range(4):
                        op("pe", TRN(tb.ap[:, s * 128:(s + 1) * 128], p_.ap[:, s * 128:(s + 1) * 128], ident_f),
                           reads=[p_.b, cst_f.b], writes=[tb.b])
                    for s in range(4):
                        i = g * 4 + s
                        xs = x_tok[:, i, fc * 128:(fc + 1) * 128]
                        op("dve", STT(xs, xs, ALPHA, tb.ap[:, s * 128:(s + 1) * 128], ALU.mult, ALU.add),
                           reads=[xtb[i], tb.b], writes=[xtb[i]])
        sc.barrier()
        ar.release(m1)
        TG = 512
        hT = ar.tile([128, 32, TG], BF16)
        wfo = [Tile(sa.alloc([128, 32, 128], BF16)) for _ in range(2)]
        rt = [Tile(sa.alloc([128, 512], F32)) for _ in range(2)]
        ot = [Tile(sa.alloc([128, 512], F32)) for _ in range(2)]
        hrot = Rot([0, 1, 2, 3])
        orot = Rot([4, 5])
        trot = Rot([6, 7])
        k = 0
        for tg in range(S // TG):
            t0 = tg * TG
            for w8 in range(8):
                wt = wtile(wfi_d[l][:, w8 * 512:(w8 + 1) * 512], 512)
                for f4 in range(4):
                    ffc = w8 * 4 + f4
                    hb = hrot.next()
                    for kc in range(8):
                        op("pe", MM(hb.ap[:, 0:TG], wt.ap[:, kc, f4 * 128:(f4 + 1) * 128], xT[:, kc, t0:t0 + TG],
                                    start=(kc == 0), stop=(kc == 7)), reads=[wt.b, xTb[t0 // 512]], writes=[hb.b])
                    r_ = rt[k % 2]
                    k += 1
                    op("act", ACTF(r_.ap[:, 0:TG], hb.ap[:, 0:TG], AF.Relu), reads=[hb.b], writes=[r_.b])
                    op("pool", TT(hT.ap[:, ffc, :], r_.ap[:, 0:TG], r_.ap[:, 0:TG], ALU.mult), reads=[r_.b], writes=[hT.b])
            for fc in range(8):
                wo = wfo[fc % 2]
                wload(wo.ap, wo.b, wfo_d[l][:, fc * 128:(fc + 1) * 128].rearrange("(kc p) n -> p kc n", p=128),
                      "wfo%d" % (fc % 2))
                ob = orot.next()
                for ffc in range(32):
                    op("pe", MM(ob.ap[:, 0:TG], wo.ap[:, ffc, :], hT.ap[:, ffc, :], start=(ffc == 0), stop=(ffc == 31)),
                       reads=[wo.b, hT.b], writes=[ob.b])
                o_ = ot[fc % 2]
                op("act", ACPY(o_.ap[:, 0:TG], ob.ap[:, 0:TG]), reads=[ob.b], writes=[o_.b])
                tb = trot.next()
                for s in range(TG // 128):
                    op("pe", TRN(tb.ap[:, s * 128:(s + 1) * 128], o_.ap[:, s * 128:(s + 1) * 128], ident_f),
                       reads=[o_.b, cst_f.b], writes=[tb.b])
                for s in range(TG // 128):
                    i = t0 // 128 + s
                    xs = x_tok[:, i, fc * 128:(fc + 1) * 128]
                    op("dve", TT(xs, xs, tb.ap[:, s * 128:(s + 1) * 128], ALU.add), reads=[xtb[i], tb.b], writes=[xtb[i]])
        sc.barrier()
        ar.release(m)

    stages = STAGES
    for sq in range(NSEQ):
        for i in range(NT):
            op("sp", DMA(x_tok[:, i, :], x_d[sq, i * 128:(i + 1) * 128, :]), writes=[xtb[i]], dma="x%d" % (i % 4))
        for l in range(DEPTH):
            if "fox" in stages:
                fox_phase(l)
            if "sb" in stages:
                sb_phase(l)
            if "dsa" in stages:
                dsa_phase(l)
            if DEBUG and sq == 0 and l == 0:
                for mi in range(3):
                    if ("fox", "sb", "dsa")[mi] not in stages:
                        continue
                    for pr in range(3):
                        nr_ = 64 if (mi > 0 and pr == 2) else 128
                        op("pool", DMA(dbg_d[mi * 3 + pr, 0:nr_, :], oT[mi][pr].ap[0:nr_, :]), reads=[oT[mi][pr].b],
                           writes=[outb[0]], dma="dbg")
            if "merge" in stages:
                merge_phase(l)
                layer_norm_tiles(l, ln1g_d, ln1b_d)
            if "ffn" in stages:
                ffn_phase(l, sq)
                layer_norm_tiles(l, ln2g_d, ln2b_d)
        for i in range(NT):
            op("sp", DMA(out_d[sq, i * 128:(i + 1) * 128, :], x_tok[:, i, :]), reads=[xtb[i]], writes=[outb[i % 4]],
               dma="o%d" % (i % 4))
    sc.barrier()
    sc.emit()
    stack.close()
    print("sbuf peak", ar.peak, "limit", SB_LIMIT, "ops", {e: len(v) for e, v in sc.ops.items()}, "chans", len(sc.chans))
    return nc


STAGES = ("fox", "sb", "dsa", "merge", "ffn")
DEBUG = False
FOX_STOP = 0


def host_consts(S):
    c = np.zeros((128, 6, 128), np.float32)
    c[:, 0, :] = np.eye(128, dtype=np.float32)
    c[:, 1, :] = 1.0
    kk = np.arange(128)[:, None]
    qq = np.arange(128)[None, :]
    c[:, 2, :] = np.where(kk > qq, -30000.0 * 8.0, 0.0)
    c[:, 3, :] = (qq < kk).astype(np.float32)
    c[:, 4, :] = np.concatenate([c[64:, 0, :], c[:64, 0, :]], axis=0)
    c[:, 5, :] = np.concatenate([c[64:, 2, :], c[:64, 2, :]], axis=0)
    half = HD // 2
    inv = (10000.0 ** (-(np.arange(half, dtype=np.float32) / half))).astype(np.float32)
    ang = np.arange(S, dtype=np.float32)[None, :] * inv[:, None]
    cos = np.cos(ang).astype(np.float32)
    sin = np.sin(ang).astype(np.float32)
    cos2 = np.concatenate([cos, cos, cos, cos], axis=0)
    sin2 = np.concatenate([-sin, sin, -sin, sin], axis=0)
    return c, np.ascontiguousarray(cos2), np.ascontiguousarray(sin2)


def rot_cols(w_in):
    def swap(o, nh):
        idx = []
        for h in range(nh):
            idx += list(range(o + h * 64 + 32, o + h * 64 + 64)) + list(range(o + h * 64, o + h * 64 + 32))
        return idx
    idx = swap(O_DQ, NDSA) + swap(O_DK, 1) + swap(O_IQ, NIDX) + swap(O_IK, 1)
    return np.ascontiguousarray(w_in[:, :, idx])


_CACHE = {}


def kernel(x, p, w_in, b_forget, w_up_fox, w_up_sb, w_up_dsa, w_out, ln1_g, ln1_b,
           w_ff_in, w_ff_out, w_ple, w_ple_gate, ln2_g, ln2_b):
    NC = 8
    B, S, _ = x.shape
    DEPTH = w_in.shape[0]
    NSEQ = B // NC
    NSEL = min(256, S // 4)
    key = (NSEQ, S, DEPTH, NSEL)
    nc = build_program(*key)
    cst, cos2, sin2 = host_consts(S)
    f = lambda a: np.ascontiguousarray(np.asarray(a, dtype=np.float32))
    shared = {
        "w_in": f(w_in), "w_rot": rot_cols(np.asarray(w_in, dtype=np.float32)), "b_forget": f(b_forget),
        "w_up_fox": f(w_up_fox), "w_up_sb": f(w_up_sb), "w_up_dsa": f(w_up_dsa), "w_out": f(w_out),
        "ln1_g": f(ln1_g), "ln1_b": f(ln1_b), "w_ff_in": f(w_ff_in), "w_ff_out": f(w_ff_out),
        "w_ple": f(w_ple), "w_ple_gate": f(w_ple_gate), "ln2_g": f(ln2_g), "ln2_b": f(ln2_b),
        "consts": cst, "ropecos": cos2, "ropesin": sin2,
    }
    x = np.asarray(x, dtype=np.float32)
    p = np.asarray(p, dtype=np.float32)
    in_maps = []
    for c in range(NC):
        d = dict(shared)
        d["x"] = np.ascontiguousarray(x[c * NSEQ:(c + 1) * NSEQ])
        d["p"] = np.ascontiguousarray(p[:, c * NSEQ:(c + 1) * NSEQ])
        in_maps.append(d)
    res = run_bass_kernel_spmd(nc, in_maps, core_ids=list(range(NC)))
    if DEBUG:
        _CACHE["dbg"] = [r["dbg"] for r in res.results]
    return np.concatenate([r["out"] for r in res.results], axis=0)
```
